# Optimizing a Trainium2 kernel written in Bass

```python
import math
import jax, jax.numpy as jnp
from jax import lax
import numpy as np

D_MODEL = 1024
BATCH = 4
SEQ = 4096
DEPTH = 2
DEC_BATCH = 8
DEC_SEQ = 64
PAST_LEN = 4096

CHUNK = 64
N_MIXERS = 2
N_GLA = (DEPTH + 1) // 2
N_CONV = DEPTH // 2
GLA_HEADS = 4
GLA_DK = D_MODEL // 2 // GLA_HEADS
GLA_DV = D_MODEL // GLA_HEADS
GLA_KEY_DIM = GLA_HEADS * GLA_DK
GLA_VAL_DIM = GLA_HEADS * GLA_DV
GATE_RANK = 16
GATE_TAU = 16.0
CONV_WIDTH = 31
CONV_DIM = D_MODEL
FFN_DIM = 4 * D_MODEL
N_MOD = 6
EPS = 1e-6

kernel_name = "hybrid_gla_conformer_stream_step"


def rmsnorm(x, g):
    xf = x.astype(jnp.float32)
    y = xf * lax.rsqrt(jnp.mean(xf * xf, axis=-1, keepdims=True) + EPS)
    return (y * g.astype(jnp.float32)).astype(x.dtype)


def layernorm(x, g, b):
    xf = x.astype(jnp.float32)
    mu = jnp.mean(xf, axis=-1, keepdims=True)
    var = jnp.mean(jnp.square(xf - mu), axis=-1, keepdims=True)
    y = (xf - mu) * lax.rsqrt(var + EPS) * g.astype(jnp.float32) + b.astype(jnp.float32)
    return y.astype(x.dtype)


def gla_chunk_scan(q, k, v, g, s0):
    B, T = q.shape[0], q.shape[1]
    C = min(CHUNK, T)
    n = T // C

    def blocks(a):
        return a.reshape(B, n, C, GLA_HEADS, a.shape[-1]).transpose(1, 0, 3, 2, 4)

    qb, kb, vb, gb = blocks(q), blocks(k), blocks(v), blocks(g)
    bcum = jnp.cumsum(gb, axis=3)
    blast = bcum[:, :, :, -1:, :]
    q_dec = qb * jnp.exp(bcum)
    k_inv = kb * jnp.exp(-bcum)
    k_to_end = kb * jnp.exp(blast - bcum)
    mask = jnp.tril(jnp.ones((C, C), dtype=bool))
    scores = jnp.einsum('nbhcd,nbhsd->nbhcs', q_dec, k_inv)
    intra = jnp.einsum('nbhcs,nbhse->nbhce', jnp.where(mask, scores, 0.0), vb)

    def step(s, xs):
        qd, ke, vv, bl = xs
        inter = jnp.einsum('bhcd,bhde->bhce', qd, s)
        s_new = jnp.exp(bl[:, :, 0, :])[..., None] * s + jnp.einsum('bhcd,bhce->bhde', ke, vv)
        return s_new, inter

    s_fin, inter = lax.scan(step, s0, (q_dec, k_to_end, vb, blast))
    o = (intra + inter).transpose(1, 0, 3, 2, 4).reshape(B, T, GLA_HEADS, GLA_DV)
    return o, s_fin


def gla_mixer(h, s0, w_in, w_ga, w_gb, b_g, norm_g, w_out):
    B, T, _ = h.shape
    proj = h @ w_in
    q, k, v, r = jnp.split(proj, [GLA_KEY_DIM, 2 * GLA_KEY_DIM, 2 * GLA_KEY_DIM + GLA_VAL_DIM], axis=-1)
    g = jax.nn.log_sigmoid(((h @ w_ga) @ w_gb + b_g).astype(jnp.float32)) / GATE_TAU
    qh = q.astype(jnp.float32).reshape(B, T, GLA_HEADS, GLA_DK) * (GLA_DK ** -0.5)
    kh = k.astype(jnp.float32).reshape(B, T, GLA_HEADS, GLA_DK)
    vh = v.astype(jnp.float32).reshape(B, T, GLA_HEADS, GLA_DV)
    gh = g.reshape(B, T, GLA_HEADS, GLA_DK)
    o, s_fin = gla_chunk_scan(qh, kh, vh, gh, s0.astype(jnp.float32))
    o = rmsnorm(o, norm_g).reshape(B, T, GLA_VAL_DIM).astype(h.dtype)
    o = o * jax.nn.silu(r)
    return o @ w_out, s_fin


def conv_mixer(h, buf, w_in, b_in, w_dw, b_dw, ln_g, ln_b, w_out, b_out):
    u = h @ w_in + b_in
    a, gt = jnp.split(u, 2, axis=-1)
    u = a * jax.nn.sigmoid(gt)
    ext = jnp.concatenate([buf.astype(u.dtype), u], axis=1)
    y = lax.conv_general_dilated(ext, w_dw[:, None, :].astype(ext.dtype), window_strides=(1,),
                                 padding='VALID', dimension_numbers=('NWC', 'WIO', 'NWC'),
                                 feature_group_count=CONV_DIM) + b_dw
    y = jax.nn.silu(layernorm(y, ln_g, ln_b))
    return y @ w_out + b_out, ext[:, -(CONV_WIDTH - 1):]


def run_trunk(x, c, gla_states, conv_bufs, p):
    new_gla = []
    new_conv = []
    for i in range(DEPTH):
        mod = jax.nn.silu(c) @ p['w_mod'][i] + p['b_mod'][i]
        sh_m, sc_m, gt_m, sh_f, sc_f, gt_f = [m[:, None, :] for m in jnp.split(mod, N_MOD, axis=-1)]
        h = rmsnorm(x, p['norm_mix_pre'][i]) * (1.0 + sc_m) + sh_m
        j = i // N_MIXERS
        if i % N_MIXERS == 0:
            y, s = gla_mixer(h, gla_states[j], p['gla_w_in'][j], p['gla_w_gate_a'][j], p['gla_w_gate_b'][j],
                             p['gla_b_gate'][j], p['gla_norm'][j], p['gla_w_out'][j])
            new_gla.append(s)
        else:
            y, s = conv_mixer(h, conv_bufs[j], p['conv_w_in'][j], p['conv_b_in'][j], p['conv_w_dw'][j],
                              p['conv_b_dw'][j], p['conv_ln_g'][j], p['conv_ln_b'][j],
                              p['conv_w_out'][j], p['conv_b_out'][j])
            new_conv.append(s)
        x = x + gt_m * rmsnorm(y, p['norm_mix_post'][i])
        h = rmsnorm(x, p['norm_ffn_pre'][i]) * (1.0 + sc_f) + sh_f
        y = jnp.square(jax.nn.relu(h @ p['w_ffn_up'][i])) @ p['w_ffn_down'][i]
        x = x + gt_f * rmsnorm(y, p['norm_ffn_post'][i])
    return x, jnp.stack(new_gla), jnp.stack(new_conv)


def setup_inputs(seed: int = 0) -> dict:
    key = jax.random.key(seed)
    ks = jax.random.split(key, 32)

    def nrm(k, shape, scale):
        return jax.random.normal(k, shape, jnp.float32) * scale

    def gain(k, shape):
        return 1.0 + 0.05 * jax.random.normal(k, shape, jnp.float32)

    D = D_MODEL
    return {
        'x_prompt': nrm(ks[0], (BATCH, SEQ, D), 1.0),
        'x_sample': nrm(ks[1], (DEC_BATCH, DEC_SEQ, D), 1.0),
        'c_prompt': nrm(ks[2], (BATCH, D), 1.0),
        'c_sample': nrm(ks[3], (DEC_BATCH, D), 1.0),
        'state_gla': nrm(ks[4], (N_GLA, DEC_BATCH, GLA_HEADS, GLA_DK, GLA_DV), 0.5),
        'cache_conv': nrm(ks[5], (N_CONV, DEC_BATCH, CONV_WIDTH - 1, CONV_DIM), 1.0),
        'w_mod': nrm(ks[6], (DEPTH, D, N_MOD * D), 0.5 * D ** -0.5),
        'b_mod': nrm(ks[7], (DEPTH, N_MOD * D), 0.02),
        'norm_mix_pre': gain(ks[8], (DEPTH, D)),
        'norm_mix_post': gain(ks[9], (DEPTH, D)),
        'norm_ffn_pre': gain(ks[10], (DEPTH, D)),
        'norm_ffn_post': gain(ks[11], (DEPTH, D)),
        'w_ffn_up': nrm(ks[12], (DEPTH, D, FFN_DIM), D ** -0.5),
        'w_ffn_down': nrm(ks[13], (DEPTH, FFN_DIM, D), FFN_DIM ** -0.5),
        'gla_w_in': nrm(ks[14], (N_GLA, D, 2 * GLA_KEY_DIM + 2 * GLA_VAL_DIM), D ** -0.5),
        'gla_w_gate_a': nrm(ks[15], (N_GLA, D, GATE_RANK), D ** -0.5),
        'gla_w_gate_b': nrm(ks[16], (N_GLA, GATE_RANK, GLA_KEY_DIM), GATE_RANK ** -0.5),
        'gla_b_gate': nrm(ks[17], (N_GLA, GLA_KEY_DIM), 0.1),
        'gla_norm': gain(ks[18], (N_GLA, GLA_DV)),
        'gla_w_out': nrm(ks[19], (N_GLA, GLA_VAL_DIM, D), GLA_VAL_DIM ** -0.5),
        'conv_w_in': nrm(ks[20], (N_CONV, D, 2 * CONV_DIM), D ** -0.5),
        'conv_b_in': nrm(ks[21], (N_CONV, 2 * CONV_DIM), 0.02),
        'conv_w_dw': nrm(ks[22], (N_CONV, CONV_WIDTH, CONV_DIM), CONV_WIDTH ** -0.5),
        'conv_b_dw': nrm(ks[23], (N_CONV, CONV_DIM), 0.02),
        'conv_ln_g': gain(ks[24], (N_CONV, CONV_DIM)),
        'conv_ln_b': nrm(ks[25], (N_CONV, CONV_DIM), 0.02),
        'conv_w_out': nrm(ks[26], (N_CONV, CONV_DIM, D), CONV_DIM ** -0.5),
        'conv_b_out': nrm(ks[27], (N_CONV, D), 0.02),
    }


def reference(x_prompt, x_sample, c_prompt, c_sample, state_gla, cache_conv, w_mod, b_mod,
              norm_mix_pre, norm_mix_post, norm_ffn_pre, norm_ffn_post, w_ffn_up, w_ffn_down,
              gla_w_in, gla_w_gate_a, gla_w_gate_b, gla_b_gate, gla_norm, gla_w_out,
              conv_w_in, conv_b_in, conv_w_dw, conv_b_dw, conv_ln_g, conv_ln_b, conv_w_out, conv_b_out):
    p = {
        'w_mod': w_mod, 'b_mod': b_mod,
        'norm_mix_pre': norm_mix_pre, 'norm_mix_post': norm_mix_post,
        'norm_ffn_pre': norm_ffn_pre, 'norm_ffn_post': norm_ffn_post,
        'w_ffn_up': w_ffn_up, 'w_ffn_down': w_ffn_down,
        'gla_w_in': gla_w_in, 'gla_w_gate_a': gla_w_gate_a, 'gla_w_gate_b': gla_w_gate_b,
        'gla_b_gate': gla_b_gate, 'gla_norm': gla_norm, 'gla_w_out': gla_w_out,
        'conv_w_in': conv_w_in, 'conv_b_in': conv_b_in, 'conv_w_dw': conv_w_dw, 'conv_b_dw': conv_b_dw,
        'conv_ln_g': conv_ln_g, 'conv_ln_b': conv_ln_b, 'conv_w_out': conv_w_out, 'conv_b_out': conv_b_out,
    }
    b_p = x_prompt.shape[0]
    gla0 = jnp.zeros((N_GLA, b_p, GLA_HEADS, GLA_DK, GLA_DV), jnp.float32)
    conv0 = jnp.zeros((N_CONV, b_p, CONV_WIDTH - 1, CONV_DIM), x_prompt.dtype)
    y_prompt, new_gla_p, new_conv_p = run_trunk(x_prompt, c_prompt, gla0, conv0, p)
    y_sample, new_gla_s, new_conv_s = run_trunk(x_sample, c_sample, state_gla, cache_conv, p)
    return (y_prompt, y_sample, new_gla_p, new_conv_p, new_gla_s, new_conv_s)
```

```python
import numpy as np
import concourse.bass as bass
import concourse.mybir as mybir
from concourse.bass_utils import run_bass_kernel_spmd

F32 = mybir.dt.float32
BF16 = mybir.dt.bfloat16
AF = mybir.ActivationFunctionType
ALU = mybir.AluOpType

D = 1024
KC = 8
H = 4
DK = 128
DV = 256
FF = 4096
NMAIN = 2112
NPRE = 1984
NTOK = NMAIN + 64
EPS = 1e-6
SAME_ENGINE_SYNC = True
SYNC_ENGS = ("act", "dve", "pool")
FULL_SYNC_ENGS = ("act", "dve", "pool")
SUB = 384
NTL = 3
NPOOL = 0
_DEBUG_STOP = None


class _Stop(Exception):
    pass

V_NORM = 0
V_BMOD = 64
V_GN = 160
V_CBIN = 162
V_CBDW = 178
V_LNG = 186
V_LNB = 194
V_CBOUT = 202
V_WDW = 210
NV = 210 + 248

C_TL = 0
C_TU = 128
C_MASK = 256
C_ID = 384


def vnorm(which, l):
    return V_NORM + (which * 2 + l) * 8


class Op:
    __slots__ = ("fn", "waits", "inc", "dma", "tag")

    def __init__(self, fn):
        self.fn = fn
        self.tag = None
        self.waits = []
        self.inc = False
        self.dma = None


class Prog:
    ENGS = ("pe", "act", "dve", "pool", "sp")

    def __init__(self):
        self.ops = {e: [] for e in self.ENGS}
        self.last_write = {}
        self.readers = {}
        self.waited = {e: {} for e in self.ENGS}
        self.dma_val = {}
        self.out_tokens = []
        self.tag = "init"

    def _need(self, eng, tok, deps, raw):
        if tok is None:
            return
        if tok[0] == "e":
            _, src, idx = tok
            if src == eng:
                if not (SAME_ENGINE_SYNC and eng in SYNC_ENGS and (raw or eng in FULL_SYNC_ENGS)):
                    return
            key = src
        else:
            _, src, idx = tok
            key = ("d", src)
        if self.waited[eng].get(key, -1) >= idx:
            return
        if deps.get(key, -1) < idx:
            deps[key] = idx

    def op(self, eng, fn, reads=(), writes=(), dma_sem=None):
        deps = {}
        for r in reads:
            self._need(eng, self.last_write.get(r), deps, True)
        for w in writes:
            self._need(eng, self.last_write.get(w), deps, False)
            for t in self.readers.get(w, ()):
                self._need(eng, t, deps, False)
        o = Op(fn)
        for key, idx in deps.items():
            self.waited[eng][key] = idx
            if isinstance(key, tuple):
                o.waits.append(("d", key[1], idx))
            else:
                self.ops[key][idx].inc = True
                o.waits.append(("e", key, idx))
        my_idx = len(self.ops[eng])
        o.tag = self.tag
        self.ops[eng].append(o)
        if dma_sem is not None:
            v = self.dma_val.get(dma_sem, 0) + 16
            self.dma_val[dma_sem] = v
            o.dma = dma_sem
            tok = ("d", dma_sem, v)
        else:
            tok = ("e", eng, my_idx)
        for r in reads:
            self.readers.setdefault(r, []).append(tok)
        for w in writes:
            self.last_write[w] = tok
            self.readers[w] = []
        return tok

    def barrier(self, engs=("pe", "act", "dve")):
        last = {}
        for s in engs:
            idx = len(self.ops[s]) - 1
            while idx >= 0 and self.ops[s][idx].fn is None:
                idx -= 1
            if idx >= 0:
                last[s] = idx
        for e in engs:
            deps = {}
            for s, idx in last.items():
                if s != e:
                    self._need(e, ("e", s, idx), deps, False)
            if deps:
                o = Op(None)
                for key, idx in deps.items():
                    self.waited[e][key] = idx
                    self.ops[key][idx].inc = True
                    o.waits.append(("e", key, idx))
                self.ops[e].append(o)

    def emit(self, nc, block, sems, dsems):
        cum = {}
        for e in self.ENGS:
            c = 0
            arr = []
            for o in self.ops[e]:
                if o.inc:
                    c += 1
                arr.append(c)
            cum[e] = arr
        prog = self

        def run(eng_name, e):
            for o in prog.ops[eng_name]:
                for w in o.waits:
                    if w[0] == "e":
                        e.wait_ge(sems[w[1]], cum[w[1]][w[2]])
                    else:
                        e.wait_ge(dsems[w[1]], w[2])
                if o.fn is None:
                    continue
                ins = o.fn(e)
                if o.dma is not None:
                    ins.then_inc(dsems[o.dma], 16)
                elif o.inc:
                    ins.then_inc(sems[eng_name], 1)

        @block.tensor
        def _(e):
            run("pe", e)

        @block.scalar
        def _(e):
            run("act", e)

        @block.vector
        def _(e):
            run("dve", e)

        @block.gpsimd
        def _(e):
            run("pool", e)

        @block.sync
        def _(e):
            run("sp", e)


def build_nc():
    nc = bass.Bass("TRN2", target_bir_lowering=False)

    def din(name, shape):
        return nc.dram_tensor(name, list(shape), F32, kind="ExternalInput").ap()

    def dout(name, shape):
        return nc.dram_tensor(name, list(shape), F32, kind="ExternalOutput").ap()

    d_xT = din("xT", [128, KC, NTOK])
    d_xp = din("xpT", [128, KC, NPRE])
    d_cT = din("cT", [128, KC, 2])
    d_sg = din("sgla", [H, DK, DV])
    d_cc = din("cconvT", [128, KC, 30])
    d_flag = din("flag", [128, 1])
    d_vecs = din("vecs", [128, NV])
    d_consts = din("consts", [128, 512])
    d_wmod = din("w_mod", [2, D, 6 * D])
    d_wup = din("w_ffn_up", [2, D, FF])
    d_wdn = din("w_ffn_down", [2, FF, D])
    d_win = din("gla_w_in", [D, 3072])
    d_wga = din("gla_w_ga", [D, 16])
    d_wgb = din("gla_w_gb", [17, 512])
    d_wo = din("gla_w_out", [D, D])
    d_cwin = din("conv_w_in", [D, 2 * D])
    d_cwo = din("conv_w_out", [D, D])

    o_yT = dout("yT", [128, KC, NTOK])
    o_sgm = dout("sg_main", [H, DK, DV])
    o_sgs = dout("sg_smp", [H, DK, DV])
    o_cvm = dout("cv_main", [128, KC, 30])
    o_cvs = dout("cv_smp", [128, KC, 30])

    P = Prog()
    TMAX = 1152

    def kcp(ap2d):
        return ap2d.rearrange("(kc p) n -> p kc n", p=128)

    from contextlib import ExitStack
    es = ExitStack()
    with es:
        arena = {"off": 16512}
        DTSZ = {F32: 4, BF16: 2}

        def sb(name, shape, dt):
            size = DTSZ[dt]
            for d_ in shape[1:]:
                size *= d_
            size = (size + 31) // 32 * 32
            off = arena["off"]
            assert off + size <= 229376, (name, off, size)
            arena["off"] = off + size
            arena["peak"] = max(arena.get("peak", 0), arena["off"])
            return nc.alloc_sbuf_tensor_at(name, list(shape), dt, offset=off)

        xT = sb("xT_sb", [128, KC, TMAX], F32)
        ring = [sb(f"ring{i}", [128, 8192], BF16) for i in range(4)]
        vecs = sb("vecs_sb", [128, NV], F32)
        cstb = sb("consts_bf", [128, 512], BF16)
        ones = sb("ones_bf", [128, 128], BF16)
        kst = sb("kst", [128, 4], F32)
        flag = sb("flag_sb", [128, 1], F32)
        cT = sb("cT_sb", [128, KC, 2], F32)
        ctmp = sb("ctmp", [128, KC, 2], F32)
        csil = sb("csil", [128, KC, 2], BF16)
        modv = sb("modv", [128, 2, 6, KC, 2], F32)
        Aall = sb("Aall", [128, 2, 2, KC, 2], F32)
        Gall = sb("Gall", [128, 2, 2, KC, 2], F32)
        nbg = sb("nbg", [128, 8], F32)
        wga = sb("wga", [128, KC, 16], BF16)
        wgb = sb("wgb", [17, 512], BF16)
        S_m = sb("S_m", [128, H, DV], F32)
        S_s = sb("S_s", [128, H, DV], F32)
        S_bf = [sb(f"S_bf{i}", [128, H, DV], BF16) for i in range(2)]
        S_bfs = sb("S_bfs", [128, H, DV], BF16)
        uctx = sb("uctx", [128, KC, 30], BF16)
        cctx = sb("cctx", [128, KC, 30], F32)
        ulast_m = sb("ulast_m", [128, KC, 30], F32)
        ulast_s = sb("ulast_s", [128, KC, 30], F32)
        rs = sb("rs", [128, SUB], F32)
        sq = sb("sq", [128, KC, SUB], BF16)
        tA = sb("tA", [128, 512], F32)
        tB = sb("tB", [128, 512], F32)

        mark_init = arena["off"]
        cst = sb("consts_sb", [128, 512], F32)
        psb = [es.enter_context(nc.psum_tensor(f"ps{i}", [128, 512], F32)) for i in range(8)]
        sem = {e: es.enter_context(nc.semaphore(f"s_{e}")) for e in Prog.ENGS}
        dsems = {}

        def dsem(name):
            if name not in dsems:
                dsems[name] = es.enter_context(nc.semaphore(f"d_{name}"))
            return name

        bank_ctr = [0]
        bank_pool = [list(range(8))]

        def bank():
            pool = bank_pool[0]
            for _ in range(len(pool)):
                b = pool[bank_ctr[0] % len(pool)]
                bank_ctr[0] += 1
                k = ("ps", b)
                if k in P.last_write and not P.readers.get(k):
                    continue
                return b
            raise RuntimeError("no free PSUM bank")

        TL = cstb[:, C_TL:C_TL + 128]
        TU = cstb[:, C_TU:C_TU + 128]
        IDb = cstb[:, C_ID:C_ID + 128]
        MASK = cstb[:, C_MASK:C_MASK + 128]
        ONE = kst[:, 0:1]
        EPSC = kst[:, 1:2]

        def mm(out, lhsT, rhs, start, stop, reads, writes):
            P.op("pe", lambda e, o=out, l=lhsT, r=rhs, s=start, t=stop: e.matmul(o, lhsT=l, rhs=r, start=s, stop=t),
                 reads=reads, writes=writes)

        def act(out, in_, func, reads, writes, scale=1.0, bias=None):
            if bias is None:
                P.op("act", lambda e: e.activation(out=out, in_=in_, func=func, scale=scale), reads=reads, writes=writes)
            else:
                P.op("act", lambda e: e.activation(out=out, in_=in_, func=func, scale=scale, bias=bias),
                     reads=reads, writes=writes)

        def dve(fn, reads, writes):
            P.op("dve", fn, reads=reads, writes=writes)

        def tt(out, in0, in1, op, reads, writes):
            dve(lambda e: e.tensor_tensor(out=out, in0=in0, in1=in1, op=op), reads, writes)

        def stt(out, in0, scalar, in1, op0, op1, reads, writes):
            dve(lambda e: e.scalar_tensor_tensor(out=out, in0=in0, scalar=scalar, in1=in1, op0=op0, op1=op1),
                reads, writes)

        def dma(eng, out, in_, semname, reads, writes):
            return P.op(eng, lambda e: e.dma_start(out=out, in_=in_), reads=reads, writes=writes, dma_sem=dsem(semname))

        slot_ctr = [0]

        def load_slab(parts):
            s = slot_ctr[0] % 4
            slot_ctr[0] += 1
            for (dstf, src) in parts:
                dma("pool", dstf(ring[s]), src, f"ring{s}", reads=(), writes=[("ring", s)])
            return s

        def slot_kn(s, kc, n, off=0):
            return ring[s][:, off:off + kc * n].rearrange("p (k n) -> p k n", k=kc)

        dma("sp", vecs[:], d_vecs, "vecs", (), ["vecs"])
        dma("sp", cst[:], d_consts, "consts", (), ["cst"])
        dma("sp", flag[:], d_flag, "flag", (), ["flag"])
        dma("sp", cT[:], d_cT, "cT", (), ["cT"])
        dma("sp", S_s[:], d_sg.rearrange("h d e -> d h e"), "S_s", (), [("S", "s", h_) for h_ in range(H)])
        dma("sp", cctx[:], d_cc, "cctx", (), ["cctx"])
        dma("pool", wga[:], kcp(d_wga), "wga", (), ["wga"])
        dma("pool", wgb[:], d_wgb, "wgb", (), ["wgb"])
        P.op("pool", lambda e: e.memset(ones[:], 1.0), (), ["ones"])
        P.op("pool", lambda e: e.memset(kst[:, 0:1], 1.0), (), ["kst0"])
        P.op("pool", lambda e: e.memset(kst[:, 1:2], EPS), (), ["kst"])
        P.op("pool", lambda e: e.memset(S_m[:], 0.0), (), [("S", "m", h_) for h_ in range(H)])
        P.op("pool", lambda e: e.memset(S_bf[0][:], 0.0), (), [("Sbf", 0)])
        dve(lambda e: e.tensor_copy(out=cstb[:], in_=cst[:]), ["cst"], ["cstb"])
        dve(lambda e: e.tensor_copy(out=S_bfs[:], in_=S_s[:]), [("S", "s", h_) for h_ in range(H)], [("Sbf", "s")])
        dve(lambda e: e.tensor_scalar_mul(out=nbg[:], in0=vecs[:, V_CBIN + 8:V_CBIN + 16], scalar1=-1.0), ["vecs"], ["nbg"])
        act(csil[:], cT[:], AF.Silu, ["cT"], ["csil"])
        P.barrier()
        arena["off"] = mark_init

        def mod_slab(l, m):
            s = load_slab([(lambda r: r[:, 0:8192].rearrange("p (k n) -> p k n", k=8),
                            kcp(d_wmod[l])[:, :, m * 1024:(m + 1) * 1024])])
            w = slot_kn(s, 8, 1024)
            b = bank()
            for j in range(8):
                for kc in range(KC):
                    mm(psb[b][:, j * 2:j * 2 + 2], w[:, kc, j * 128:(j + 1) * 128], csil[:, kc, :],
                       kc == 0, kc == KC - 1, [("ring", s), "csil"], [("ps", b)])
            bm = vecs[:, V_BMOD + (l * 6 + m) * 8: V_BMOD + (l * 6 + m) * 8 + 8]
            tt(modv[:, l, m, :, :], psb[b][:, 0:16].rearrange("p (k s) -> p k s", s=2),
               bm.unsqueeze(2).to_broadcast([128, 8, 2]), ALU.add, [("ps", b), "vecs"], [("mod", l, m)])

        def mod_derive(l, w):
            msc = 1 + 3 * w
            mgt = 2 + 3 * w
            gpre = vecs[:, vnorm(0 if w == 0 else 2, l): vnorm(0 if w == 0 else 2, l) + 8]
            gpost = vecs[:, vnorm(1 if w == 0 else 3, l): vnorm(1 if w == 0 else 3, l) + 8]
            for s in range(2):
                stt(Aall[:, l, w, :, s], modv[:, l, msc, :, s], 1.0, gpre, ALU.add, ALU.mult,
                    [("mod", l, msc), "vecs"], [("A", l, w)])
                tt(Gall[:, l, w, :, s], modv[:, l, mgt, :, s], gpost, ALU.mult,
                   [("mod", l, mgt), "vecs"], [("G", l, w)])

        def segs_of(pas, c0, c1):
            out = []
            for (a, b, s) in pas["segs"]:
                lo, hi = max(a, c0), min(b, c1)
                if lo < hi:
                    out.append((lo, hi, s))
            return out

        def stats_rstd(src_key, n, nfeat_scale):
            b = bank()
            for kc in range(KC):
                mm(psb[b][:, 0:n], ones[:], sq[:, kc, 0:n], kc == 0, kc == KC - 1, ["ones", "sq", ("sq", kc // 4)], [("ps", b)])
            act(rs[:, 0:n], psb[b][:, 0:n], AF.Ln, [("ps", b), "kst"], ["rs"], scale=nfeat_scale, bias=EPSC)
            act(rs[:, 0:n], rs[:, 0:n], AF.Exp, ["rs"], ["rs"], scale=-0.5)

        def prenorm_g(pas, c0, c1, l, w, scratch, skey, hdst, hkey, hoff, delay=0):
            n = c1 - c0
            xk = ("xT", pas["stidx"][c0])
            act(sq[:, 0:4, 0:n], xT[:, 0:4, c0:c1], AF.Square, [xk], [("sq", 0)])
            tt(sq[:, 4:8, 0:n], xT[:, 4:8, c0:c1], xT[:, 4:8, c0:c1], ALU.mult, [xk], [("sq", 1)])
            yield
            for _ in range(delay):
                yield
            stats_rstd(xk, n, 1.0 / D)
            yield
            tt(scratch[:, :, 0:n], xT[:, :, c0:c1], rs[:, 0:n].unsqueeze(1).to_broadcast([128, KC, n]), ALU.mult,
               [xk, "rs"], [skey])
            yield
            msh = 0 if w == 0 else 3
            for (a, b_, s) in segs_of(pas, c0, c1):
                for kc in range(KC):
                    o_ = hdst[:, kc, hoff + a - c0: hoff + b_ - c0]
                    i_ = scratch[:, kc, a - c0:b_ - c0]
                    sc_ = Aall[:, l, w, kc, s:s + 1]
                    bi_ = modv[:, l, msh, kc, s:s + 1]
                    if kc % 2 == 0:
                        act(o_, i_, AF.Identity, [skey, ("A", l, w), ("mod", l, msh)], [hkey], scale=sc_, bias=bi_)
                    else:
                        dve(lambda e, o_=o_, i_=i_, sc_=sc_, bi_=bi_: e.tensor_scalar(
                            out=o_, in0=i_, scalar1=sc_, scalar2=bi_, op0=ALU.mult, op1=ALU.add),
                            [skey, ("A", l, w), ("mod", l, msh)], [hkey])
                    if kc % 2 == 1:
                        yield

        def prenorm(*a, **k):
            for _ in prenorm_g(*a, **k):
                pass

        def postnorm_g(pas, c0, c1, l, w, ybuf, ykey, yoff):
            n = c1 - c0
            xk = ("xT", pas["stidx"][c0])
            yv = ybuf[:, :, yoff:yoff + n]
            act(sq[:, :, 0:n], yv, AF.Square, [ykey], ["sq", ("sq", 0), ("sq", 1)])
            yield
            stats_rstd(ykey, n, 1.0 / D)
            yield
            tt(yv, yv, rs[:, 0:n].unsqueeze(1).to_broadcast([128, KC, n]), ALU.mult, [ykey, "rs"], [ykey])
            yield
            for (a, b_, s) in segs_of(pas, c0, c1):
                for kc in range(KC):
                    stt(xT[:, kc, a:b_], ybuf[:, kc, yoff + a - c0: yoff + b_ - c0], Gall[:, l, w, kc, s:s + 1],
                        xT[:, kc, a:b_], ALU.mult, ALU.add, [ykey, ("G", l, w), xk], [xk])
                    if kc % 2 == 1:
                        yield

        def postnorm(*a, **k):
            for _ in postnorm_g(*a, **k):
                pass

        def mk_pass(name, kind, src, col0, T, subs, segs):
            stidx = {}
            sts = []
            c = 0
            for i, n in enumerate(subs):
                sts.append((c, c + n))
                for cc in range(c, c + n):
                    stidx[cc] = i
                c += n
            assert c == T
            return dict(name=name, kind=kind, src=src, col0=col0, T=T, sts=sts, segs=segs, stidx=stidx)

        passes = [
            mk_pass("P1", "prefix", d_xp, 0, 1024, [384, 384, 256], [(0, 1024, 0)]),
            mk_pass("P2", "prefix", d_xp, 1024, 960, [384, 384, 192], [(0, 960, 0)]),
            mk_pass("A", "main", d_xT, 0, 1152, [384, 384, 384], [(0, 1152, 0)]),
            mk_pass("B", "main", d_xT, 1152, 1024, [384, 384, 256], [(0, 960, 0), (960, 1024, 1)]),
        ]

        def load_x(pas):
            for i, (c0, c1) in enumerate(pas["sts"]):
                dma("sp", xT[:, :, c0:c1], pas["src"][:, :, pas["col0"] + c0: pas["col0"] + c1], f"x{i}",
                    (), [("xT", i)])

        chunk_ctr = [0]

        def run(g):
            for _ in g:
                pass

        def chain(*gens):
            for g in gens:
                if g is not None:
                    yield from g

        def merge(*gens):
            gens = [g for g in gens if g is not None]
            while gens:
                for g in list(gens):
                    try:
                        next(g)
                    except StopIteration:
                        gens.remove(g)

        def idle(k):
            for _ in range(k):
                yield

        def merge_g(*gens):
            gens = [g for g in gens if g is not None]
            while gens:
                for g in list(gens):
                    try:
                        next(g)
                    except StopIteration:
                        gens.remove(g)
                yield

        def gla_phase(pas, extras=()):
            extras = list(extras)
            main = pas["kind"] == "main"
            l = 0
            mark0 = arena["off"]
            NP = 1 if main else 2

            def pb(name, shape, dt):
                return sb(f"{name}_{pas['name']}", shape, dt)
            NH = 1 if main else 2
            hTs = [pb(f"g_hT{p_}", [128, KC, SUB], BF16) for p_ in range(NH)]
            yst = pb("g_y", [128, KC, SUB], F32)
            lrT = pb("g_lrT", [17, SUB], BF16)
            l_sbs = [pb(f"g_l{p_}", [128, NTL, 512], BF16) for p_ in range(NP)]
            kends = [pb(f"g_kend{p_}", [128, NTL, 512], BF16) for p_ in range(NP)]
            vtoks = [pb(f"g_vtok{p_}", [128, NTL, 1024], BF16) for p_ in range(NP)]
            ebs = [pb(f"g_eb{p_}", [128, H, SUB], F32) for p_ in range(NP)]
            if main:
                einv = pb("g_einv", [128, H, SUB], F32)
                qdec = pb("g_qdec", [128, H, SUB], BF16)
                kinv = pb("g_kinv", [128, H, SUB], BF16)
                rsil = pb("g_rsil", [128, KC, SUB], BF16)
                sT = pb("g_sT", [128, H, 128], BF16)
                osq = pb("g_osq", [128, KC, 128], BF16)
                rsh = pb("g_rsh", [128, H, 128], F32)
                otmp = pb("g_otmp", [128, KC, 128], F32)

            P.op("dve", lambda e: e.memset(lrT[:], 1.0), (), ["lrT"])
            otile_ctr = [0]
            if main:
                bank_pool[0] = [4, 5, 6, 7]

            full = lambda r: r[:, 0:8192].rearrange("p (k n) -> p k n", k=8)
            if main:
                sQK = load_slab([(full, kcp(d_win)[:, :, 0:1024])])
            else:
                sQK = load_slab([(lambda r: full(r)[:, :, 512:1024], kcp(d_win)[:, :, 512:1024])])
            sV = load_slab([(full, kcp(d_win)[:, :, 1024:2048])])
            if main:
                sR = load_slab([(full, kcp(d_win)[:, :, 2048:3072])])
                sO = load_slab([(full, kcp(d_wo))])
                wR = slot_kn(sR, 8, 1024)
                wO = slot_kn(sO, 8, 1024)
            wQK = slot_kn(sQK, 8, 1024)
            wV = slot_kn(sV, 8, 1024)
            sts = pas["sts"]

            def tiles_of(n):
                out = []
                t0 = 0
                while t0 < n:
                    out.append((t0, min(128, n - t0)))
                    t0 += 128
                return out

            def S1(si):
                c0, c1 = sts[si]
                n = c1 - c0
                yield from prenorm_g(pas, c0, c1, l, 0, yst, "g_y", hTs[si % NH], ("g_hT", si % NH), 0,
                                     delay=4 if main else 0)

            def S2(si):
                c0, c1 = sts[si]
                n = c1 - c0
                hT = hTs[si % NH]
                hk = ("g_hT", si % NH)
                p_ = si % NP
                l_sb, kend, vtok, eb = l_sbs[p_], kends[p_], vtoks[p_], ebs[p_]
                b = bank()
                for kc in range(KC):
                    mm(psb[b][0:16, 0:n], wga[:, kc, :], hT[:, kc, 0:n], kc == 0, kc == KC - 1,
                       ["wga", hk], [("ps", b)])
                act(lrT[0:16, 0:n], psb[b][0:16, 0:n], AF.Copy, [("ps", b)], ["lrT"])
                yield
                for i, (t0, Pn) in enumerate(tiles_of(n)):
                    b = bank()
                    mm(psb[b][0:Pn, :], lrT[0:17, t0:t0 + Pn], wgb[0:17, :], True, True, ["lrT", "wgb"], [("ps", b)])
                    act(tA[0:Pn, :], psb[b][0:Pn, :], AF.Exp, [("ps", b)], ["tA"], scale=-1.0)
                    act(l_sb[0:Pn, i, :], tA[0:Pn, :], AF.Ln, ["tA", "kst0"], [("l", p_, i)], bias=ONE[0:Pn, :])
                    yield
                    for hf in range(2):
                        b = bank()
                        for kc in range(KC):
                            mm(psb[b][0:Pn, :], hT[:, kc, t0:t0 + Pn], wV[:, kc, hf * 512:(hf + 1) * 512],
                               kc == 0, kc == KC - 1, [hk, ("ring", sV)], [("ps", b)])
                        act(vtok[0:Pn, i, hf * 512:(hf + 1) * 512], psb[b][0:Pn, :], AF.Copy, [("ps", b)], [("vtok", p_, i)])
                    yield
                    b = bank()
                    mm(psb[b][0:Pn, :], TU[0:Pn, 0:Pn], l_sb[0:Pn, i, :], True, True, ["cstb", ("l", p_, i)], [("ps", b)])
                    act(tB[0:Pn, :], psb[b][0:Pn, :], AF.Exp, [("ps", b)], ["tB"])
                    b = bank()
                    for kc in range(KC):
                        mm(psb[b][0:Pn, :], hT[:, kc, t0:t0 + Pn], wQK[:, kc, 512:1024], kc == 0, kc == KC - 1,
                           [hk, ("ring", sQK)], [("ps", b)])
                    tt(kend[0:Pn, i, :], psb[b][0:Pn, :], tB[0:Pn, :], ALU.mult, [("ps", b), "tB"], [("kend", p_, i)])
                    yield
                    b = bank()
                    for hd in range(H):
                        mm(psb[b][:, hd * 128: hd * 128 + Pn], l_sb[0:Pn, i, hd * 128:(hd + 1) * 128], TL[0:Pn, 0:Pn],
                           True, True, [("l", p_, i), "cstb"], [("ps", b)])
                    pv = psb[b][:, :].rearrange("p (h t) -> p h t", h=H)[:, :, 0:Pn]
                    act(eb[:, :, t0:t0 + Pn], pv, AF.Exp, [("ps", b)], [("eb", p_)])
                    if main:
                        act(einv[:, :, t0:t0 + Pn], pv, AF.Exp, [("ps", b)], ["einv"], scale=-1.0)
                    yield

            def S3(si):
                c0, c1 = sts[si]
                n = c1 - c0
                hT = hTs[si % NH]
                hk = ("g_hT", si % NH)
                eb = ebs[0]
                for hd in range(H):
                    b = bank()
                    for kc in range(KC):
                        mm(psb[b][:, 0:n], wQK[:, kc, hd * 128:(hd + 1) * 128], hT[:, kc, 0:n], kc == 0, kc == KC - 1,
                           [("ring", sQK), hk], [("ps", b)])
                    stt(qdec[:, hd, 0:n], psb[b][:, 0:n], float(DK) ** -0.5, eb[:, hd, 0:n], ALU.mult, ALU.mult,
                        [("ps", b), ("eb", 0)], ["qdec"])
                    yield
                    b = bank()
                    for kc in range(KC):
                        mm(psb[b][:, 0:n], wQK[:, kc, 512 + hd * 128: 512 + (hd + 1) * 128], hT[:, kc, 0:n],
                           kc == 0, kc == KC - 1, [("ring", sQK), hk], [("ps", b)])
                    tt(kinv[:, hd, 0:n], psb[b][:, 0:n], einv[:, hd, 0:n], ALU.mult, [("ps", b), "einv"], ["kinv"])
                    yield
                for c in range(KC):
                    b = bank()
                    for kc in range(KC):
                        mm(psb[b][:, 0:n], wR[:, kc, c * 128:(c + 1) * 128], hT[:, kc, 0:n], kc == 0, kc == KC - 1,
                           [("ring", sR), hk], [("ps", b)])
                    act(rsil[:, c, 0:n], psb[b][:, 0:n], AF.Silu, [("ps", b)], ["rsil"])
                    yield

            def S4(si):
                c0, c1 = sts[si]
                tl = tiles_of(c1 - c0)
                obs = {}
                gens = []
                for i in range(len(tl)):
                    if i == 0:
                        gens.append(T_a(si, i, obs))
                    else:
                        gens.append(merge_g(T_b(si, i - 1, obs), T_a(si, i, obs)))
                gens.append(T_b(si, len(tl) - 1, obs))
                yield from chain(*gens)

            def T_a(si, i, obs):
                c0, c1 = sts[si]
                n = c1 - c0
                p_ = si % NP
                l_sb, kend, vtok, eb = l_sbs[p_], kends[p_], vtoks[p_], ebs[p_]
                t0, Pn = tiles_of(n)[i]
                if True:
                    nch = Pn // 64
                    chunks = []
                    for ch in range(nch):
                        col = c0 + t0 + ch * 64
                        chunks.append([s for (a, b_, s) in pas["segs"] if a <= col < b_][0])
                    if main:
                        b = bank()
                        for hd in range(H):
                            mm(psb[b][0:Pn, hd * 128: hd * 128 + Pn], kinv[:, hd, t0:t0 + Pn], qdec[:, hd, t0:t0 + Pn],
                               True, True, ["kinv", "qdec"], [("ps", b)])
                        tt(sT[0:Pn, :, 0:Pn], psb[b][0:Pn, :].rearrange("p (h t) -> p h t", h=H)[:, :, 0:Pn],
                           MASK[0:Pn, 0:Pn].unsqueeze(1).to_broadcast([Pn, H, Pn]), ALU.mult,
                           [("ps", b), "cstb"], ["sT"])
                        yield
                    if main:
                        tp_ = otile_ctr[0] % 2
                        otile_ctr[0] += 1
                        ob = (2 * tp_, 2 * tp_ + 1)
                    kvb = []

                    def kv_mm(ch):
                        r0 = ch * 64
                        bb = (bank(), bank())
                        for hd in range(H):
                            bk = bb[hd // 2]
                            mm(psb[bk][:, (hd % 2) * 256:(hd % 2) * 256 + 256],
                               kend[r0:r0 + 64, i, hd * 128:(hd + 1) * 128], vtok[r0:r0 + 64, i, hd * 256:(hd + 1) * 256],
                               True, True, [("kend", p_, i), ("vtok", p_, i)], [("ps", bk)])
                        kvb.append(bb)

                    def state_of(ch):
                        if chunks[ch] == 0:
                            return S_m, "m"
                        return S_s, "s"

                    def update(ch):
                        Sst, Skey = state_of(ch)
                        lastc = t0 + ch * 64 + 63
                        for hd in range(H):
                            bk = kvb[ch][hd // 2]
                            stt(Sst[:, hd, :], Sst[:, hd, :], eb[:, hd, lastc:lastc + 1],
                                psb[bk][:, (hd % 2) * 256:(hd % 2) * 256 + 256], ALU.mult, ALU.add,
                                [("S", Skey, hd), ("eb", p_), ("ps", bk)], [("S", Skey, hd)])
                        if chunks[ch] == 0:
                            chunk_ctr[0] += 1
                            npar = chunk_ctr[0] % 2
                            if main:
                                act(S_bf[npar][:], S_m[:], AF.Copy, [("S", "m", h_) for h_ in range(H)], [("Sbf", npar)])

                    def outputs(ch, sbf, sbk):
                        r0 = ch * 64
                        for idx in range(8):
                            hd, ec = idx // 2, idx % 2
                            bk = ob[idx // 4]
                            oc = (idx % 4) * 128 + ch * 64
                            mm(psb[bk][:, oc:oc + 64],
                               vtok[r0:r0 + 64, i, hd * 256 + ec * 128: hd * 256 + ec * 128 + 128],
                               sT[r0:r0 + 64, hd, r0:r0 + 64], True, False, [("vtok", p_, i), "sT"], [("ps", bk)])
                            mm(psb[bk][:, oc:oc + 64], sbf[:, hd, ec * 128:(ec + 1) * 128],
                               qdec[:, hd, t0 + ch * 64: t0 + ch * 64 + 64], False, True,
                               [sbk, "qdec"], [("ps", bk)])

                    srcs = []
                    cc_ = chunk_ctr[0]
                    for ch in range(nch):
                        if chunks[ch] == 0:
                            srcs.append((S_bf[cc_ % 2], ("Sbf", cc_ % 2)))
                            cc_ += 1
                        else:
                            srcs.append((S_bfs, ("Sbf", "s")))
                    kv_mm(0)
                    yield
                    if nch == 1:
                        if main:
                            outputs(0, *srcs[0])
                            yield
                        update(0)
                        yield
                    else:
                        update(0)
                        kv_mm(1)
                        yield
                        if main:
                            outputs(0, *srcs[0])
                            yield
                        update(1)
                        yield
                        if main:
                            outputs(1, *srcs[1])
                            yield
                    if main:
                        obs[i] = ob

            def T_b(si, i, obs):
                c0, c1 = sts[si]
                n = c1 - c0
                t0, Pn = tiles_of(n)[i]
                if True:
                    if main:
                        ob = obs[i]
                        o3 = [psb[ob[k]][:, :].rearrange("p (a t) -> p a t", a=4)[:, :, 0:Pn] for k in range(2)]
                        for k in range(2):
                            act(osq[:, k * 4:(k + 1) * 4, 0:Pn], o3[k], AF.Square, [("ps", ob[k])], ["osq"])
                        yield
                        yield
                        yield
                        b = bank()
                        for hd in range(H):
                            for ec in range(2):
                                mm(psb[b][:, hd * 128: hd * 128 + Pn], ones[:], osq[:, hd * 2 + ec, 0:Pn], ec == 0, ec == 1,
                                   ["ones", "osq"], [("ps", b)])
                        yield
                        pv = psb[b][:, :].rearrange("p (h t) -> p h t", h=H)[:, :, 0:Pn]
                        act(rsh[:, :, 0:Pn], pv, AF.Ln, [("ps", b), "kst"], ["rsh"], scale=1.0 / DV, bias=EPSC)
                        act(rsh[:, :, 0:Pn], rsh[:, :, 0:Pn], AF.Exp, ["rsh"], ["rsh"], scale=-0.5)
                        for k in range(2):
                            tt(otmp[:, k * 4:(k + 1) * 4, 0:Pn].rearrange("p (h e) t -> p h e t", e=2),
                               o3[k].rearrange("p (h e) t -> p h e t", e=2),
                               rsh[:, k * 2:(k + 1) * 2, 0:Pn].unsqueeze(2).to_broadcast([128, 2, 2, Pn]), ALU.mult,
                               [("ps", ob[k]), "rsh"], ["otmp"])
                        yield
                        for ec in range(2):
                            ov = otmp[:, :, 0:Pn].rearrange("p (h e) t -> p h e t", e=2)[:, :, ec, :]
                            rv = rsil[:, :, t0:t0 + Pn].rearrange("p (h e) t -> p h e t", e=2)[:, :, ec, :]
                            stt(rv, ov, vecs[:, V_GN + ec:V_GN + ec + 1], rv, ALU.mult, ALU.mult,
                                ["otmp", "vecs", "rsil"], ["rsil"])
                        yield

            def S5a(si):
                c0, c1 = sts[si]
                n = c1 - c0
                for c in range(KC):
                    b = bank()
                    for kc in range(KC):
                        mm(psb[b][:, 0:n], wO[:, kc, c * 128:(c + 1) * 128], rsil[:, kc, 0:n], kc == 0, kc == KC - 1,
                           [("ring", sO), "rsil"], [("ps", b)])
                    act(yst[:, c, 0:n], psb[b][:, 0:n], AF.Copy, [("ps", b)], ["g_y"])
                    yield

            def S5b(si):
                c0, c1 = sts[si]
                yield from postnorm_g(pas, c0, c1, l, 0, yst, "g_y", 0)

            nst = len(sts)
            if main:
                run(S1(0))
                run(S2(0))
                run(S3(0))
                for si in range(nst):
                    nxt = si + 1 < nst
                    merge(S4(si), S1(si + 1) if nxt else None)
                    run(S5a(si))
                    merge(S5b(si), chain(S2(si + 1), S3(si + 1)) if nxt else None)
            else:
                run(S1(0))
                if nst > 1:
                    run(S1(1))
                run(S2(0))
                for si in range(nst):
                    merge(S4(si), chain(S1(si + 2) if si + 2 < nst else None, S2(si + 1) if si + 1 < nst else None))
                    for _ in range(2):
                        if extras:
                            l_, m_ = extras.pop(0)
                            mod_slab(l_, m_)
                while extras:
                    l_, m_ = extras.pop(0)
                    mod_slab(l_, m_)
            P.barrier()
            bank_pool[0] = list(range(8))
            arena["off"] = mark0

        def ffn_phase(pas, l):
            T = pas["T"]
            mark0 = arena["off"]
            if True:
                def pb(name, shape, dt):
                    return sb(f"{name}_{pas['name']}{l}", shape, dt)
                hT = pb("f_hT", [128, KC, TMAX], BF16)
                yf = pb("f_y", [128, KC, TMAX], F32)
                hid = [pb(f"f_hid{i}", [128, 4, TMAX], BF16) for i in range(2)]
                for si, (c0, c1) in enumerate(pas["sts"]):
                    prenorm(pas, c0, c1, l, 1, yf[:, :, c0:c1], ("f_y", si), hT, ("f_hT", si), c0)

                def load_g(g):
                    return load_slab([
                        (lambda r: r[:, 0:4096].rearrange("p (k n) -> p k n", k=8), kcp(d_wup[l])[:, :, g * 512:(g + 1) * 512]),
                        (lambda r: r[:, 4096:8192].rearrange("p (k n) -> p k n", k=4), kcp(d_wdn[l])[:, g * 4:(g + 1) * 4, :]),
                    ])

                def up(g, s):
                    U = ring[s][:, 0:4096].rearrange("p (k n) -> p k n", k=8)
                    hb = hid[g % 2]
                    for si, (c0, c1) in enumerate(pas["sts"]):
                        n = c1 - c0
                        for j in range(4):
                            b = bank()
                            for kc in range(KC):
                                mm(psb[b][:, 0:n], U[:, kc, j * 128:(j + 1) * 128], hT[:, kc, c0:c1], kc == 0, kc == KC - 1,
                                   [("ring", s), ("f_hT", si)], [("ps", b)])
                            tmp = tA if j % 2 == 0 else tB
                            tk = "tA" if j % 2 == 0 else "tB"
                            act(tmp[:, 0:n], psb[b][:, 0:n], AF.Relu, [("ps", b)], [tk])
                            tt(hb[:, j, c0:c1], tmp[:, 0:n], tmp[:, 0:n], ALU.mult, [tk], [("hid", g % 2, si)])

                def down(g, s):
                    Dn = ring[s][:, 4096:8192].rearrange("p (k n) -> p k n", k=4)
                    hb = hid[g % 2]
                    for si, (c0, c1) in enumerate(pas["sts"]):
                        n = c1 - c0
                        for c in range(KC):
                            b = bank()
                            for j in range(4):
                                mm(psb[b][:, 0:n], Dn[:, j, c * 128:(c + 1) * 128], hb[:, j, c0:c1], j == 0, j == 3,
                                   [("ring", s), ("hid", g % 2, si)], [("ps", b)])
                            if g == 0:
                                act(yf[:, c, c0:c1], psb[b][:, 0:n], AF.Copy, [("ps", b)], [("f_y", si)])
                            else:
                                tt(yf[:, c, c0:c1], yf[:, c, c0:c1], psb[b][:, 0:n], ALU.add, [("f_y", si), ("ps", b)],
                                   [("f_y", si)])

                slots = {}
                slots[0] = load_g(0)
                up(0, slots[0])
                for g in range(8):
                    if g + 1 < 8:
                        slots[g + 1] = load_g(g + 1)
                        up(g + 1, slots[g + 1])
                    down(g, slots[g])
                for si, (c0, c1) in enumerate(pas["sts"]):
                    postnorm(pas, c0, c1, l, 1, yf, ("f_y", si), c0)
                P.barrier()
            arena["off"] = mark0

        def conv_phase(pas):
            l = 1
            T = pas["T"]
            isB = pas["name"] == "B"
            mark0 = arena["off"]
            if True:
                def pb(name, shape, dt):
                    return sb(f"{name}_{pas['name']}", shape, dt)
                UW = 30 + TMAX + 30
                uT = pb("c_uT", [128, KC, UW], BF16)
                yst = pb("c_y", [128, KC, SUB], F32)
                diags = [pb(f"c_diag{i_}", [128, 31, 128], BF16) for i_ in range(2)]
                mu = pb("c_mu", [128, SUB], F32)
                accs = [pb(f"c_acc{i_}", [128, SUB], F32) for i_ in range(2)]
                actr = [0]

                def ucol(c):
                    return 30 + c if (not isB or c < 960) else 60 + c

                if not isB:
                    P.op("dve", lambda e: e.memset(uT[:, :, 0:30], 0.0), (), ["uT"])
                else:
                    dve(lambda e: e.tensor_copy(out=uT[:, :, 0:30], in_=uctx[:]), ["uctx"], ["uT"])
                    dve(lambda e: e.tensor_copy(out=uT[:, :, 990:1020], in_=cctx[:]), ["cctx"], ["uT"])

                mark1 = arena["off"]
                sts = pas["sts"]
                nst = len(sts)
                hT = sb(f"c_hT_{pas['name']}", [128, KC, TMAX], BF16)
                slabs = []
                for hf in range(2):
                    slabs.append(load_slab([
                        (lambda r: r[:, 0:8192].rearrange("p (k n) -> p k n", k=8)[:, :, 0:512],
                         kcp(d_cwin)[:, :, hf * 512:(hf + 1) * 512]),
                        (lambda r: r[:, 0:8192].rearrange("p (k n) -> p k n", k=8)[:, :, 512:1024],
                         kcp(d_cwin)[:, :, 1024 + hf * 512: 1024 + (hf + 1) * 512]),
                    ]))

                def C1(si):
                    c0, c1 = sts[si]
                    yield from prenorm_g(pas, c0, c1, l, 0, yst, "c_y", hT, ("c_hT", si), c0)

                def C2(si):
                    c0, c1 = sts[si]
                    n = c1 - c0
                    for hf in range(2):
                        s = slabs[hf]
                        W = slot_kn(s, 8, 1024)
                        for cc in range(4):
                            fc = hf * 4 + cc
                            ba = bank()
                            for kc in range(KC):
                                mm(psb[ba][:, 0:n], W[:, kc, cc * 128:(cc + 1) * 128], hT[:, kc, c0:c1], kc == 0, kc == KC - 1,
                                   [("ring", s), ("c_hT", si)], [("ps", ba)])
                            bg = bank()
                            for kc in range(KC):
                                mm(psb[bg][:, 0:n], W[:, kc, 512 + cc * 128: 512 + (cc + 1) * 128], hT[:, kc, c0:c1],
                                   kc == 0, kc == KC - 1, [("ring", s), ("c_hT", si)], [("ps", bg)])
                            tg = tA if fc % 2 == 0 else tB
                            tgk = "tA" if fc % 2 == 0 else "tB"
                            act(tg[:, 0:n], psb[bg][:, 0:n], AF.Sigmoid, [("ps", bg), "vecs"], [tgk],
                                bias=vecs[:, V_CBIN + 8 + fc: V_CBIN + 9 + fc])
                            bav = vecs[:, V_CBIN + fc: V_CBIN + fc + 1]
                            for (a, b_, sq_) in segs_of(pas, c0, c1):
                                stt(uT[:, fc, ucol(a): ucol(a) + (b_ - a)], psb[ba][:, a - c0:b_ - c0], bav,
                                    tg[:, a - c0:b_ - c0], ALU.add, ALU.mult, [("ps", ba), "vecs", tgk], ["uT"])
                            if isB:
                                if c0 <= 930 and c1 >= 960:
                                    stt(ulast_m[:, fc, :], psb[ba][:, 930 - c0:960 - c0], bav, tg[:, 930 - c0:960 - c0],
                                        ALU.add, ALU.mult, [("ps", ba), "vecs", tgk], ["ulast_m"])
                                if c0 <= 994 and c1 >= 1024:
                                    stt(ulast_s[:, fc, :], psb[ba][:, 994 - c0:1024 - c0], bav, tg[:, 994 - c0:1024 - c0],
                                        ALU.add, ALU.mult, [("ps", ba), "vecs", tgk], ["ulast_s"])
                            yield

                run(C1(0))
                for si in range(nst):
                    merge(C2(si), C1(si + 1) if si + 1 < nst else None)
                P.barrier()
                arena["off"] = mark1
                if not isB:
                    dve(lambda e: e.tensor_copy(out=uctx[:], in_=uT[:, :, 30 + T - 30: 30 + T]), ["uT"], ["uctx"])
                zT = sb(f"c_zT_{pas['name']}", [128, KC, TMAX], BF16)
                csegs_st = [[] for _ in sts]
                for (a, b_, sq_) in pas["segs"]:
                    c = a
                    while c < b_:
                        si = pas["stidx"][c]
                        nn = min(SUB, min(b_, sts[si][1]) - c)
                        csegs_st[si].append((c, nn, ucol(c) - 30))
                        c += nn
                sO = load_slab([(lambda r: r[:, 0:8192].rearrange("p (k n) -> p k n", k=8), kcp(d_cwo))])
                wO = slot_kn(sO, 8, 1024)
                dctr = [0]

                def C3(si):
                    for fc in range(KC):
                        wd = vecs[:, V_WDW + fc * 31: V_WDW + fc * 31 + 31]
                        dp = dctr[0] % 2
                        dctr[0] += 1
                        diag = diags[dp]
                        dgk = ("diag", dp)
                        tt(diag[:], IDb.unsqueeze(1).to_broadcast([128, 31, 128]), wd.unsqueeze(2).to_broadcast([128, 31, 128]),
                           ALU.mult, ["cstb", "vecs"], [dgk])
                        yield
                        for (c, nn, us) in csegs_st[si]:
                            if NPOOL > 0:
                                ap_ = actr[0] % 2
                                actr[0] += 1
                                acc = accs[ap_]
                                ak = ("acc", ap_)
                                P.op("pool", lambda e, acc=acc, fc=fc, us=us, nn=nn, wd=wd: e.tensor_scalar_mul(
                                    out=acc[:, 0:nn], in0=uT[:, fc, us: us + nn], scalar1=wd[:, 0:1]), ["uT", "vecs"], [ak])
                                for j in range(1, NPOOL):
                                    P.op("pool", lambda e, acc=acc, fc=fc, us=us, nn=nn, wd=wd, j=j: e.scalar_tensor_tensor(
                                        out=acc[:, 0:nn], in0=uT[:, fc, us + j: us + j + nn], scalar=wd[:, j:j + 1],
                                        in1=acc[:, 0:nn], op0=ALU.mult, op1=ALU.add), ["uT", "vecs", ak], [ak])
                            b = bank()
                            for j in range(NPOOL, 31):
                                mm(psb[b][:, 0:nn], diag[:, j, :], uT[:, fc, us + j: us + j + nn], j == NPOOL, j == 30,
                                   [dgk, "uT"], [("ps", b)])
                            if NPOOL > 0:
                                stt(zT[:, fc, c:c + nn], psb[b][:, 0:nn], vecs[:, V_CBDW + fc: V_CBDW + fc + 1], acc[:, 0:nn],
                                    ALU.add, ALU.add, [("ps", b), "vecs", ak], [("zT", si)])
                            else:
                                act(zT[:, fc, c:c + nn], psb[b][:, 0:nn], AF.Identity, [("ps", b), "vecs"], [("zT", si)],
                                    bias=vecs[:, V_CBDW + fc: V_CBDW + fc + 1])
                            yield

                def C4(si):
                    c0, c1 = sts[si]
                    n = c1 - c0
                    zk = ("zT", si)
                    act(sq[:, :, 0:n], zT[:, :, c0:c1], AF.Square, [zk], ["sq", ("sq", 0), ("sq", 1)])
                    yield
                    yield
                    yield
                    bm = bank()
                    for kc in range(KC):
                        mm(psb[bm][:, 0:n], ones[:], zT[:, kc, c0:c1], kc == 0, kc == KC - 1, ["ones", zk], [("ps", bm)])
                    yield
                    b2 = bank()
                    for kc in range(KC):
                        mm(psb[b2][:, 0:n], ones[:], sq[:, kc, 0:n], kc == 0, kc == KC - 1, ["ones", "sq", ("sq", kc // 4)], [("ps", b2)])
                    dve(lambda e, n=n, bm=bm: e.tensor_scalar_mul(out=mu[:, 0:n], in0=psb[bm][:, 0:n], scalar1=1.0 / D),
                        [("ps", bm)], ["mu"])
                    tt(rs[:, 0:n], mu[:, 0:n], mu[:, 0:n], ALU.mult, ["mu"], ["rs"])
                    yield
                    stt(rs[:, 0:n], psb[b2][:, 0:n], 1.0 / D, rs[:, 0:n], ALU.mult, ALU.subtract, [("ps", b2), "rs"], ["rs"])
                    act(rs[:, 0:n], rs[:, 0:n], AF.Ln, ["rs", "kst"], ["rs"], bias=EPSC)
                    act(rs[:, 0:n], rs[:, 0:n], AF.Exp, ["rs"], ["rs"], scale=-0.5)
                    yield
                    tt(yst[:, :, 0:n], zT[:, :, c0:c1], mu[:, 0:n].unsqueeze(1).to_broadcast([128, KC, n]), ALU.subtract,
                       [zk, "mu"], ["c_y"])
                    yield
                    tt(yst[:, :, 0:n], yst[:, :, 0:n], rs[:, 0:n].unsqueeze(1).to_broadcast([128, KC, n]), ALU.mult,
                       ["c_y", "rs"], ["c_y"])
                    yield
                    for fc in range(KC):
                        act(zT[:, fc, c0:c1], yst[:, fc, 0:n], AF.Silu, ["c_y", "vecs"], [zk],
                            scale=vecs[:, V_LNG + fc: V_LNG + fc + 1], bias=vecs[:, V_LNB + fc: V_LNB + fc + 1])
                        if fc % 2 == 1:
                            yield

                def C5a(si):
                    c0, c1 = sts[si]
                    n = c1 - c0
                    for c in range(KC):
                        b = bank()
                        for kc in range(KC):
                            mm(psb[b][:, 0:n], wO[:, kc, c * 128:(c + 1) * 128], zT[:, kc, c0:c1], kc == 0, kc == KC - 1,
                               [("ring", sO), ("zT", si)], [("ps", b)])
                        act(yst[:, c, 0:n], psb[b][:, 0:n], AF.Identity, [("ps", b), "vecs"], ["c_y"],
                            bias=vecs[:, V_CBOUT + c: V_CBOUT + c + 1])
                        yield

                def C5b(si):
                    c0, c1 = sts[si]
                    yield from postnorm_g(pas, c0, c1, l, 0, yst, "c_y", 0)

                run(C3(0))
                for si in range(nst):
                    merge(C3(si + 1) if si + 1 < nst else None, chain(C4(si), idle(5), C5a(si), C5b(si)))
                P.barrier()
            arena["off"] = mark0

        def chk(tag):
            if _DEBUG_STOP == tag:
                raise _Stop()

        def whole():
            chk("init")
            mod_slab(0, 0)
            mod_slab(0, 1)
            gpre = vecs[:, vnorm(0, 0): vnorm(0, 0) + 8]
            for s in range(2):
                stt(Aall[:, 0, 0, :, s], modv[:, 0, 1, :, s], 1.0, gpre, ALU.add, ALU.mult, [("mod", 0, 1), "vecs"], [("A", 0, 0)])
            chk("mod0")
            later = [(0, 2), (0, 3), (0, 4), (0, 5), (1, 0), (1, 1), (1, 2), (1, 3), (1, 4), (1, 5)]
            for pi, pas in enumerate(passes):
                cur["pas"] = pas
                P.tag = pas["name"] + ".gla"
                load_x(pas)
                chk(pas["name"] + "load")
                if pas["kind"] == "prefix":
                    gla_phase(pas, later[0:6] if pi == 0 else later[6:10])
                else:
                    gla_phase(pas)
                chk(pas["name"] + "gla")
                if pas["kind"] == "prefix":
                    P.tag = pas["name"] + ".mod"
                    if pas["name"] == "P2":
                        dve(lambda e: e.tensor_scalar_mul(out=S_m[:], in0=S_m[:], scalar1=flag[:, 0:1]), [("S", "m", h_) for h_ in range(H)] + ["flag"], [("S", "m", h_) for h_ in range(H)])
                        par = chunk_ctr[0] % 2
                        act(S_bf[par][:], S_m[:], AF.Copy, [("S", "m", h_) for h_ in range(H)], [("Sbf", par)])
                        for s in range(2):
                            tt(Gall[:, 0, 0, :, s], modv[:, 0, 2, :, s], vecs[:, vnorm(1, 0): vnorm(1, 0) + 8], ALU.mult,
                               [("mod", 0, 2), "vecs"], [("G", 0, 0)])
                        mod_derive(0, 1)
                        mod_derive(1, 0)
                        mod_derive(1, 1)
                    chk(pas["name"])
                    continue
                P.tag = pas["name"] + ".ffn0"
                ffn_phase(pas, 0)
                chk(pas["name"] + "ffn0")
                P.tag = pas["name"] + ".conv"
                conv_phase(pas)
                chk(pas["name"] + "conv")
                P.tag = pas["name"] + ".ffn1"
                ffn_phase(pas, 1)
                chk(pas["name"] + "ffn1")
                for i, (c0, c1) in enumerate(pas["sts"]):
                    t = dma("sp", o_yT[:, :, pas["col0"] + c0: pas["col0"] + c1], xT[:, :, c0:c1], f"oy{i}", [("xT", i)], ())
                    P.out_tokens.append(t)
                chk(pas["name"])

        cur = {}
        try:
            whole()
        except _Stop:
            pas = cur.get("pas")
            if pas is not None and pas["kind"] == "main":
                for i, (c0, c1) in enumerate(pas["sts"]):
                    t = dma("sp", o_yT[:, :, pas["col0"] + c0: pas["col0"] + c1], xT[:, :, c0:c1], f"oy{i}", [("xT", i)], ())
                    P.out_tokens.append(t)
        P.out_tokens.append(dma("sp", o_sgm.rearrange("h d e -> d h e"), S_m[:], "o_sgm", [("S", "m", h_) for h_ in range(H)], ()))
        P.out_tokens.append(dma("sp", o_sgs.rearrange("h d e -> d h e"), S_s[:], "o_sgs", [("S", "s", h_) for h_ in range(H)], ()))
        P.out_tokens.append(dma("sp", o_cvm, ulast_m[:], "o_cvm", ["ulast_m"], ()))
        P.out_tokens.append(dma("sp", o_cvs, ulast_s[:], "o_cvs", ["ulast_s"], ()))
        fin = Op(None)
        seen = {}
        for t in P.out_tokens:
            seen[t[1]] = max(seen.get(t[1], 0), t[2])
        for k, v in seen.items():
            fin.waits.append(("d", k, v))
        P.ops["sp"].append(fin)

        global _LAST_P
        _LAST_P = P
        with nc.Block() as block:
            P.emit(nc, block, sem, dsems)
    return nc


_NC_CACHE = {}


def _fm(a2d):
    t = a2d.shape[0]
    return np.ascontiguousarray(a2d.reshape(t, KC, 128).transpose(2, 1, 0))


def _vec_cols(v):
    return v.reshape(-1, 128).T


def _consts():
    s = np.arange(128)[:, None]
    t = np.arange(128)[None, :]
    same = (s // 64) == (t // 64)
    TLm = np.where(same & (s <= t), -1.0 / 16.0, 0.0)
    TUm = np.where(same & (s > t), -1.0 / 16.0, 0.0)
    MK = np.where(same & (s <= t), 1.0, 0.0)
    ID = np.eye(128)
    return np.ascontiguousarray(np.concatenate([TLm, TUm, MK, ID], axis=1).astype(np.float32))


def kernel(x_prompt, x_sample, c_prompt, c_sample, state_gla, cache_conv, w_mod, b_mod,
           norm_mix_pre, norm_mix_post, norm_ffn_pre, norm_ffn_post, w_ffn_up, w_ffn_down,
           gla_w_in, gla_w_gate_a, gla_w_gate_b, gla_b_gate, gla_norm, gla_w_out,
           conv_w_in, conv_b_in, conv_w_dw, conv_b_dw, conv_ln_g, conv_ln_b, conv_w_out, conv_b_out):
    f = lambda a: np.ascontiguousarray(np.asarray(a, dtype=np.float32))
    x_prompt, x_sample, c_prompt, c_sample = f(x_prompt), f(x_sample), f(c_prompt), f(c_sample)
    state_gla, cache_conv = f(state_gla), f(cache_conv)
    if "nc" not in _NC_CACHE:
        _NC_CACHE["nc"] = build_nc()
    nc = _NC_CACHE["nc"]

    cols = []
    for arr in (norm_mix_pre, norm_mix_post, norm_ffn_pre, norm_ffn_post):
        for l in range(2):
            cols.append(_vec_cols(f(arr)[l]))
    for l in range(2):
        cols.append(_vec_cols(f(b_mod)[l]))
    cols.append(_vec_cols(f(gla_norm)[0]))
    cols.append(_vec_cols(f(conv_b_in)[0]))
    cols.append(_vec_cols(f(conv_b_dw)[0]))
    cols.append(_vec_cols(f(conv_ln_g)[0]))
    cols.append(_vec_cols(f(conv_ln_b)[0]))
    cols.append(_vec_cols(f(conv_b_out)[0]))
    wdw = f(conv_w_dw)[0]
    cols.append(np.ascontiguousarray(wdw.reshape(31, KC, 128).transpose(2, 1, 0)).reshape(128, KC * 31))
    vecs = np.ascontiguousarray(np.concatenate(cols, axis=1).astype(np.float32))
    assert vecs.shape == (128, NV), vecs.shape
    consts = _consts()
    wgb_aug = np.ascontiguousarray(np.concatenate([f(gla_w_gate_b)[0], f(gla_b_gate)[0][None, :]], axis=0))

    shared = {
        "vecs": vecs, "consts": consts,
        "w_mod": f(w_mod), "w_ffn_up": f(w_ffn_up), "w_ffn_down": f(w_ffn_down),
        "gla_w_in": f(gla_w_in)[0], "gla_w_ga": f(gla_w_gate_a)[0], "gla_w_gb": wgb_aug,
        "gla_w_out": f(gla_w_out)[0], "conv_w_in": f(conv_w_in)[0], "conv_w_out": f(conv_w_out)[0],
    }
    in_maps = []
    for c in range(8):
        b = c // 2
        odd = c % 2
        t0 = 0 if not odd else 4096 - NMAIN
        xm = x_prompt[b, t0:t0 + NMAIN]
        xall = np.concatenate([xm, x_sample[c]], axis=0)
        xp = x_prompt[b, 0:NPRE]
        c2 = np.stack([c_prompt[b], c_sample[c]], axis=0)
        m = dict(shared)
        m["xT"] = _fm(xall)
        m["xpT"] = _fm(xp)
        m["cT"] = np.ascontiguousarray(c2.reshape(2, KC, 128).transpose(2, 1, 0))
        m["sgla"] = np.ascontiguousarray(state_gla[0, c])
        m["cconvT"] = _fm(cache_conv[0, c])
        m["flag"] = np.full((128, 1), float(odd), dtype=np.float32)
        in_maps.append(m)

    res = run_bass_kernel_spmd(nc, in_maps, core_ids=list(range(8)))
    R = res.results

    def tm(a):
        return np.ascontiguousarray(a.transpose(2, 1, 0).reshape(a.shape[2], D))

    y_prompt = np.empty((4, 4096, D), np.float32)
    y_sample = np.empty((8, 64, D), np.float32)
    gla_p = np.empty((1, 4, H, DK, DV), np.float32)
    conv_p = np.empty((1, 4, 30, D), np.float32)
    gla_s = np.empty((1, 8, H, DK, DV), np.float32)
    conv_s = np.empty((1, 8, 30, D), np.float32)
    for c in range(8):
        b = c // 2
        odd = c % 2
        y = tm(np.asarray(R[c]["yT"]))
        if not odd:
            y_prompt[b, 0:2048] = y[0:2048]
        else:
            y_prompt[b, 2048:4096] = y[NMAIN - 2048:NMAIN]
            gla_p[0, b] = np.asarray(R[c]["sg_main"])
            conv_p[0, b] = tm(np.asarray(R[c]["cv_main"]))
        y_sample[c] = y[NMAIN:NMAIN + 64]
        gla_s[0, c] = np.asarray(R[c]["sg_smp"])
        conv_s[0, c] = tm(np.asarray(R[c]["cv_smp"]))
    return (y_prompt, y_sample, gla_p, conv_p, gla_s, conv_s)
```

```python
import numpy as np
import concourse.bass as bass
import concourse.mybir as mybir
from concourse.bass_utils import run_bass_kernel_spmd

F32 = mybir.dt.float32
BF16 = mybir.dt.bfloat16
AF = mybir.ActivationFunctionType
ALU = mybir.AluOpType

D = 1024
KC = 8
H = 4
DK = 128
DV = 256
FF = 4096
NMAIN = 2112
NPRE = 1984
NTOK = NMAIN + 64
EPS = 1e-6
SAME_ENGINE_SYNC = True
SYNC_ENGS = ("act", "dve", "pool")
FULL_SYNC_ENGS = ("act", "dve", "pool")
SUB = 384
NTL = 3
NPOOL = 0
_DEBUG_STOP = None


class _Stop(Exception):
    pass

V_NORM = 0
V_BMOD = 64
V_GN = 160
V_CBIN = 162
V_CBDW = 178
V_LNG = 186
V_LNB = 194
V_CBOUT = 202
V_WDW = 210
NV = 210 + 248

C_TL = 0
C_TU = 128
C_MASK = 256
C_ID = 384


def vnorm(which, l):
    return V_NORM + (which * 2 + l) * 8


class Op:
    __slots__ = ("fn", "waits", "inc", "dma", "tag")

    def __init__(self, fn):
        self.fn = fn
        self.tag = None
        self.waits = []
        self.inc = False
        self.dma = None


class Prog:
    ENGS = ("pe", "act", "dve", "pool", "sp")

    def __init__(self):
        self.ops = {e: [] for e in self.ENGS}
        self.last_write = {}
        self.readers = {}
        self.waited = {e: {} for e in self.ENGS}
        self.dma_val = {}
        self.out_tokens = []
        self.tag = "init"

    def _need(self, eng, tok, deps, raw):
        if tok is None:
            return
        if tok[0] == "e":
            _, src, idx = tok
            if src == eng:
                if not (SAME_ENGINE_SYNC and eng in SYNC_ENGS and (raw or eng in FULL_SYNC_ENGS)):
                    return
            key = src
        else:
            _, src, idx = tok
            key = ("d", src)
        if self.waited[eng].get(key, -1) >= idx:
            return
        if deps.get(key, -1) < idx:
            deps[key] = idx

    def op(self, eng, fn, reads=(), writes=(), dma_sem=None):
        deps = {}
        for r in reads:
            self._need(eng, self.last_write.get(r), deps, True)
        for w in writes:
            self._need(eng, self.last_write.get(w), deps, False)
            for t in self.readers.get(w, ()):
                self._need(eng, t, deps, False)
        o = Op(fn)
        for key, idx in deps.items():
            self.waited[eng][key] = idx
            if isinstance(key, tuple):
                o.waits.append(("d", key[1], idx))
            else:
                self.ops[key][idx].inc = True
                o.waits.append(("e", key, idx))
        my_idx = len(self.ops[eng])
        o.tag = self.tag
        self.ops[eng].append(o)
        if dma_sem is not None:
            v = self.dma_val.get(dma_sem, 0) + 16
            self.dma_val[dma_sem] = v
            o.dma = dma_sem
            tok = ("d", dma_sem, v)
        else:
            tok = ("e", eng, my_idx)
        for r in reads:
            self.readers.setdefault(r, []).append(tok)
        for w in writes:
            self.last_write[w] = tok
            self.readers[w] = []
        return tok

    def barrier(self, engs=("pe", "act", "dve")):
        last = {}
        for s in engs:
            idx = len(self.ops[s]) - 1
            while idx >= 0 and self.ops[s][idx].fn is None:
                idx -= 1
            if idx >= 0:
                last[s] = idx
        for e in engs:
            deps = {}
            for s, idx in last.items():
                if s != e:
                    self._need(e, ("e", s, idx), deps, False)
            if deps:
                o = Op(None)
                for key, idx in deps.items():
                    self.waited[e][key] = idx
                    self.ops[key][idx].inc = True
                    o.waits.append(("e", key, idx))
                self.ops[e].append(o)

    def emit(self, nc, block, sems, dsems):
        cum = {}
        for e in self.ENGS:
            c = 0
            arr = []
            for o in self.ops[e]:
                if o.inc:
                    c += 1
                arr.append(c)
            cum[e] = arr
        prog = self

        def run(eng_name, e):
            for o in prog.ops[eng_name]:
                for w in o.waits:
                    if w[0] == "e":
                        e.wait_ge(sems[w[1]], cum[w[1]][w[2]])
                    else:
                        e.wait_ge(dsems[w[1]], w[2])
                if o.fn is None:
                    continue
                ins = o.fn(e)
                if o.dma is not None:
                    ins.then_inc(dsems[o.dma], 16)
                elif o.inc:
                    ins.then_inc(sems[eng_name], 1)

        @block.tensor
        def _(e):
            run("pe", e)

        @block.scalar
        def _(e):
            run("act", e)

        @block.vector
        def _(e):
            run("dve", e)

        @block.gpsimd
        def _(e):
            run("pool", e)

        @block.sync
        def _(e):
            run("sp", e)


def build_nc():
    nc = bass.Bass("TRN2", target_bir_lowering=False)

    def din(name, shape):
        return nc.dram_tensor(name, list(shape), F32, kind="ExternalInput").ap()

    def dout(name, shape):
        return nc.dram_tensor(name, list(shape), F32, kind="ExternalOutput").ap()

    d_xT = din("xT", [128, KC, NTOK])
    d_xp = din("xpT", [128, KC, NPRE])
    d_cT = din("cT", [128, KC, 2])
    d_sg = din("sgla", [H, DK, DV])
    d_cc = din("cconvT", [128, KC, 30])
    d_flag = din("flag", [128, 1])
    d_vecs = din("vecs", [128, NV])
    d_consts = din("consts", [128, 512])
    d_wmod = din("w_mod", [2, D, 6 * D])
    d_wup = din("w_ffn_up", [2, D, FF])
    d_wdn = din("w_ffn_down", [2, FF, D])
    d_win = din("gla_w_in", [D, 3072])
    d_wga = din("gla_w_ga", [D, 16])
    d_wgb = din("gla_w_gb", [17, 512])
    d_wo = din("gla_w_out", [D, D])
    d_cwin = din("conv_w_in", [D, 2 * D])
    d_cwo = din("conv_w_out", [D, D])

    o_yT = dout("yT", [128, KC, NTOK])
    o_sgm = dout("sg_main", [H, DK, DV])
    o_sgs = dout("sg_smp", [H, DK, DV])
    o_cvm = dout("cv_main", [128, KC, 30])
    o_cvs = dout("cv_smp", [128, KC, 30])

    P = Prog()
    TMAX = 1152

    def kcp(ap2d):
        return ap2d.rearrange("(kc p) n -> p kc n", p=128)

    from contextlib import ExitStack
    es = ExitStack()
    with es:
        arena = {"off": 16512}
        DTSZ = {F32: 4, BF16: 2}

        def sb(name, shape, dt):
            size = DTSZ[dt]
            for d_ in shape[1:]:
                size *= d_
            size = (size + 31) // 32 * 32
            off = arena["off"]
            assert off + size <= 229376, (name, off, size)
            arena["off"] = off + size
            arena["peak"] = max(arena.get("peak", 0), arena["off"])
            return nc.alloc_sbuf_tensor_at(name, list(shape), dt, offset=off)

        xT = sb("xT_sb", [128, KC, TMAX], F32)
        ring = [sb(f"ring{i}", [128, 8192], BF16) for i in range(4)]
        vecs = sb("vecs_sb", [128, NV], F32)
        cstb = sb("consts_bf", [128, 512], BF16)
        ones = sb("ones_bf", [128, 128], BF16)
        kst = sb("kst", [128, 4], F32)
        flag = sb("flag_sb", [128, 1], F32)
        cT = sb("cT_sb", [128, KC, 2], F32)
        ctmp = sb("ctmp", [128, KC, 2], F32)
        csil = sb("csil", [128, KC, 2], BF16)
        modv = sb("modv", [128, 2, 6, KC, 2], F32)
        Aall = sb("Aall", [128, 2, 2, KC, 2], F32)
        Gall = sb("Gall", [128, 2, 2, KC, 2], F32)
        nbg = sb("nbg", [128, 8], F32)
        wga = sb("wga", [128, KC, 16], BF16)
        wgb = sb("wgb", [17, 512], BF16)
        S_m = sb("S_m", [128, H, DV], F32)
        S_s = sb("S_s", [128, H, DV], F32)
        S_bf = [sb(f"S_bf{i}", [128, H, DV], BF16) for i in range(2)]
        S_bfs = sb("S_bfs", [128, H, DV], BF16)
        uctx = sb("uctx", [128, KC, 30], BF16)
        cctx = sb("cctx", [128, KC, 30], F32)
        ulast_m = sb("ulast_m", [128, KC, 30], F32)
        ulast_s = sb("ulast_s", [128, KC, 30], F32)
        rs = sb("rs", [128, SUB], F32)
        sq = sb("sq", [128, KC, SUB], BF16)
        tA = sb("tA", [128, 512], F32)
        tB = sb("tB", [128, 512], F32)

        mark_init = arena["off"]
        cst = sb("consts_sb", [128, 512], F32)
        psb = [es.enter_context(nc.psum_tensor(f"ps{i}", [128, 512], F32)) for i in range(8)]
        sem = {e: es.enter_context(nc.semaphore(f"s_{e}")) for e in Prog.ENGS}
        dsems = {}

        def dsem(name):
            if name not in dsems:
                dsems[name] = es.enter_context(nc.semaphore(f"d_{name}"))
            return name

        bank_ctr = [0]
        bank_pool = [list(range(8))]

        def bank():
            pool = bank_pool[0]
            for _ in range(len(pool)):
                b = pool[bank_ctr[0] % len(pool)]
                bank_ctr[0] += 1
                k = ("ps", b)
                if k in P.last_write and not P.readers.get(k):
                    continue
                return b
            raise RuntimeError("no free PSUM bank")

        TL = cstb[:, C_TL:C_TL + 128]
        TU = cstb[:, C_TU:C_TU + 128]
        IDb = cstb[:, C_ID:C_ID + 128]
        MASK = cstb[:, C_MASK:C_MASK + 128]
        ONE = kst[:, 0:1]
        EPSC = kst[:, 1:2]

        def mm(out, lhsT, rhs, start, stop, reads, writes):
            P.op("pe", lambda e, o=out, l=lhsT, r=rhs, s=start, t=stop: e.matmul(o, lhsT=l, rhs=r, start=s, stop=t),
                 reads=reads, writes=writes)

        def act(out, in_, func, reads, writes, scale=1.0, bias=None):
            if bias is None:
                P.op("act", lambda e: e.activation(out=out, in_=in_, func=func, scale=scale), reads=reads, writes=writes)
            else:
                P.op("act", lambda e: e.activation(out=out, in_=in_, func=func, scale=scale, bias=bias),
                     reads=reads, writes=writes)

        def dve(fn, reads, writes):
            P.op("dve", fn, reads=reads, writes=writes)

        def tt(out, in0, in1, op, reads, writes):
            dve(lambda e: e.tensor_tensor(out=out, in0=in0, in1=in1, op=op), reads, writes)

        def stt(out, in0, scalar, in1, op0, op1, reads, writes):
            dve(lambda e: e.scalar_tensor_tensor(out=out, in0=in0, scalar=scalar, in1=in1, op0=op0, op1=op1),
                reads, writes)

        def dma(eng, out, in_, semname, reads, writes):
            return P.op(eng, lambda e: e.dma_start(out=out, in_=in_), reads=reads, writes=writes, dma_sem=dsem(semname))

        slot_ctr = [0]

        def load_slab(parts):
            s = slot_ctr[0] % 4
            slot_ctr[0] += 1
            for (dstf, src) in parts:
                dma("pool", dstf(ring[s]), src, f"ring{s}", reads=(), writes=[("ring", s)])
            return s

        def slot_kn(s, kc, n, off=0):
            return ring[s][:, off:off + kc * n].rearrange("p (k n) -> p k n", k=kc)

        dma("sp", vecs[:], d_vecs, "vecs", (), ["vecs"])
        dma("sp", cst[:], d_consts, "consts", (), ["cst"])
        dma("sp", flag[:], d_flag, "flag", (), ["flag"])
        dma("sp", cT[:], d_cT, "cT", (), ["cT"])
        dma("sp", S_s[:], d_sg.rearrange("h d e -> d h e"), "S_s", (), [("S", "s", h_) for h_ in range(H)])
        dma("sp", cctx[:], d_cc, "cctx", (), ["cctx"])
        dma("pool", wga[:], kcp(d_wga), "wga", (), ["wga"])
        dma("pool", wgb[:], d_wgb, "wgb", (), ["wgb"])
        P.op("pool", lambda e: e.memset(ones[:], 1.0), (), ["ones"])
        P.op("pool", lambda e: e.memset(kst[:, 0:1], 1.0), (), ["kst0"])
        P.op("pool", lambda e: e.memset(kst[:, 1:2], EPS), (), ["kst"])
        P.op("pool", lambda e: e.memset(S_m[:], 0.0), (), [("S", "m", h_) for h_ in range(H)])
        P.op("pool", lambda e: e.memset(S_bf[0][:], 0.0), (), [("Sbf", 0)])
        dve(lambda e: e.tensor_copy(out=cstb[:], in_=cst[:]), ["cst"], ["cstb"])
        dve(lambda e: e.tensor_copy(out=S_bfs[:], in_=S_s[:]), [("S", "s", h_) for h_ in range(H)], [("Sbf", "s")])
        dve(lambda e: e.tensor_scalar_mul(out=nbg[:], in0=vecs[:, V_CBIN + 8:V_CBIN + 16], scalar1=-1.0), ["vecs"], ["nbg"])
        act(csil[:], cT[:], AF.Silu, ["cT"], ["csil"])
        P.barrier()
        arena["off"] = mark_init

        def mod_slab(l, m):
            s = load_slab([(lambda r: r[:, 0:8192].rearrange("p (k n) -> p k n", k=8),
                            kcp(d_wmod[l])[:, :, m * 1024:(m + 1) * 1024])])
            w = slot_kn(s, 8, 1024)
            b = bank()
            for j in range(8):
                for kc in range(KC):
                    mm(psb[b][:, j * 2:j * 2 + 2], w[:, kc, j * 128:(j + 1) * 128], csil[:, kc, :],
                       kc == 0, kc == KC - 1, [("ring", s), "csil"], [("ps", b)])
            bm = vecs[:, V_BMOD + (l * 6 + m) * 8: V_BMOD + (l * 6 + m) * 8 + 8]
            tt(modv[:, l, m, :, :], psb[b][:, 0:16].rearrange("p (k s) -> p k s", s=2),
               bm.unsqueeze(2).to_broadcast([128, 8, 2]), ALU.add, [("ps", b), "vecs"], [("mod", l, m)])

        def mod_derive(l, w):
            msc = 1 + 3 * w
            mgt = 2 + 3 * w
            gpre = vecs[:, vnorm(0 if w == 0 else 2, l): vnorm(0 if w == 0 else 2, l) + 8]
            gpost = vecs[:, vnorm(1 if w == 0 else 3, l): vnorm(1 if w == 0 else 3, l) + 8]
            for s in range(2):
                stt(Aall[:, l, w, :, s], modv[:, l, msc, :, s], 1.0, gpre, ALU.add, ALU.mult,
                    [("mod", l, msc), "vecs"], [("A", l, w)])
                tt(Gall[:, l, w, :, s], modv[:, l, mgt, :, s], gpost, ALU.mult,
                   [("mod", l, mgt), "vecs"], [("G", l, w)])

        def segs_of(pas, c0, c1):
            out = []
            for (a, b, s) in pas["segs"]:
                lo, hi = max(a, c0), min(b, c1)
                if lo < hi:
                    out.append((lo, hi, s))
            return out

        def stats_rstd(src_key, n, nfeat_scale):
            b = bank()
            for kc in range(KC):
                mm(psb[b][:, 0:n], ones[:], sq[:, kc, 0:n], kc == 0, kc == KC - 1, ["ones", "sq", ("sq", kc // 4)], [("ps", b)])
            act(rs[:, 0:n], psb[b][:, 0:n], AF.Ln, [("ps", b), "kst"], ["rs"], scale=nfeat_scale, bias=EPSC)
            act(rs[:, 0:n], rs[:, 0:n], AF.Exp, ["rs"], ["rs"], scale=-0.5)

        def prenorm_g(pas, c0, c1, l, w, scratch, skey, hdst, hkey, hoff, delay=0):
            n = c1 - c0
            xk = ("xT", pas["stidx"][c0])
            act(sq[:, 0:4, 0:n], xT[:, 0:4, c0:c1], AF.Square, [xk], [("sq", 0)])
            tt(sq[:, 4:8, 0:n], xT[:, 4:8, c0:c1], xT[:, 4:8, c0:c1], ALU.mult, [xk], [("sq", 1)])
            yield
            for _ in range(delay):
                yield
            stats_rstd(xk, n, 1.0 / D)
            yield
            tt(scratch[:, :, 0:n], xT[:, :, c0:c1], rs[:, 0:n].unsqueeze(1).to_broadcast([128, KC, n]), ALU.mult,
               [xk, "rs"], [skey])
            yield
            msh = 0 if w == 0 else 3
            for (a, b_, s) in segs_of(pas, c0, c1):
                for kc in range(KC):
                    o_ = hdst[:, kc, hoff + a - c0: hoff + b_ - c0]
                    i_ = scratch[:, kc, a - c0:b_ - c0]
                    sc_ = Aall[:, l, w, kc, s:s + 1]
                    bi_ = modv[:, l, msh, kc, s:s + 1]
                    if kc % 2 == 0:
                        act(o_, i_, AF.Identity, [skey, ("A", l, w), ("mod", l, msh)], [hkey], scale=sc_, bias=bi_)
                    else:
                        dve(lambda e, o_=o_, i_=i_, sc_=sc_, bi_=bi_: e.tensor_scalar(
                            out=o_, in0=i_, scalar1=sc_, scalar2=bi_, op0=ALU.mult, op1=ALU.add),
                            [skey, ("A", l, w), ("mod", l, msh)], [hkey])
                    if kc % 2 == 1:
                        yield

        def prenorm(*a, **k):
            for _ in prenorm_g(*a, **k):
                pass

        def postnorm_g(pas, c0, c1, l, w, ybuf, ykey, yoff):
            n = c1 - c0
            xk = ("xT", pas["stidx"][c0])
            yv = ybuf[:, :, yoff:yoff + n]
            act(sq[:, :, 0:n], yv, AF.Square, [ykey], ["sq", ("sq", 0), ("sq", 1)])
            yield
            stats_rstd(ykey, n, 1.0 / D)
            yield
            tt(yv, yv, rs[:, 0:n].unsqueeze(1).to_broadcast([128, KC, n]), ALU.mult, [ykey, "rs"], [ykey])
            yield
            for (a, b_, s) in segs_of(pas, c0, c1):
                for kc in range(KC):
                    stt(xT[:, kc, a:b_], ybuf[:, kc, yoff + a - c0: yoff + b_ - c0], Gall[:, l, w, kc, s:s + 1],
                        xT[:, kc, a:b_], ALU.mult, ALU.add, [ykey, ("G", l, w), xk], [xk])
                    if kc % 2 == 1:
                        yield

        def postnorm(*a, **k):
            for _ in postnorm_g(*a, **k):
                pass

        def mk_pass(name, kind, src, col0, T, subs, segs):
            stidx = {}
            sts = []
            c = 0
            for i, n in enumerate(subs):
                sts.append((c, c + n))
                for cc in range(c, c + n):
                    stidx[cc] = i
                c += n
            assert c == T
            return dict(name=name, kind=kind, src=src, col0=col0, T=T, sts=sts, segs=segs, stidx=stidx)

        passes = [
            mk_pass("P1", "prefix", d_xp, 0, 1024, [384, 384, 256], [(0, 1024, 0)]),
            mk_pass("P2", "prefix", d_xp, 1024, 960, [384, 384, 192], [(0, 960, 0)]),
            mk_pass("A", "main", d_xT, 0, 1152, [384, 384, 384], [(0, 1152, 0)]),
            mk_pass("B", "main", d_xT, 1152, 1024, [384, 384, 256], [(0, 960, 0), (960, 1024, 1)]),
        ]

        def load_x(pas):
            for i, (c0, c1) in enumerate(pas["sts"]):
                dma("sp", xT[:, :, c0:c1], pas["src"][:, :, pas["col0"] + c0: pas["col0"] + c1], f"x{i}",
                    (), [("xT", i)])

        chunk_ctr = [0]

        def run(g):
            for _ in g:
                pass

        def chain(*gens):
            for g in gens:
                if g is not None:
                    yield from g

        def merge(*gens):
            gens = [g for g in gens if g is not None]
            while gens:
                for g in list(gens):
                    try:
                        next(g)
                    except StopIteration:
                        gens.remove(g)

        def idle(k):
            for _ in range(k):
                yield

        def merge_g(*gens):
            gens = [g for g in gens if g is not None]
            while gens:
                for g in list(gens):
                    try:
                        next(g)
                    except StopIteration:
                        gens.remove(g)
                yield

        def gla_phase(pas, extras=()):
            extras = list(extras)
            main = pas["kind"] == "main"
            l = 0
            mark0 = arena["off"]
            NP = 1 if main else 2

            def pb(name, shape, dt):
                return sb(f"{name}_{pas['name']}", shape, dt)
            NH = 1 if main else 2
            hTs = [pb(f"g_hT{p_}", [128, KC, SUB], BF16) for p_ in range(NH)]
            yst = pb("g_y", [128, KC, SUB], F32)
            lrT = pb("g_lrT", [17, SUB], BF16)
            l_sbs = [pb(f"g_l{p_}", [128, NTL, 512], BF16) for p_ in range(NP)]
            kends = [pb(f"g_kend{p_}", [128, NTL, 512], BF16) for p_ in range(NP)]
            vtoks = [pb(f"g_vtok{p_}", [128, NTL, 1024], BF16) for p_ in range(NP)]
            ebs = [pb(f"g_eb{p_}", [128, H, SUB], F32) for p_ in range(NP)]
            if main:
                einv = pb("g_einv", [128, H, SUB], F32)
                qdec = pb("g_qdec", [128, H, SUB], BF16)
                kinv = pb("g_kinv", [128, H, SUB], BF16)
                rsil = pb("g_rsil", [128, KC, SUB], BF16)
                sT = pb("g_sT", [128, H, 128], BF16)
                osq = pb("g_osq", [128, KC, 128], BF16)
                rsh = pb("g_rsh", [128, H, 128], F32)
                otmp = pb("g_otmp", [128, KC, 128], F32)

            P.op("dve", lambda e: e.memset(lrT[:], 1.0), (), ["lrT"])
            otile_ctr = [0]
            if main:
                bank_pool[0] = [4, 5, 6, 7]

            full = lambda r: r[:, 0:8192].rearrange("p (k n) -> p k n", k=8)
            if main:
                sQK = load_slab([(full, kcp(d_win)[:, :, 0:1024])])
            else:
                sQK = load_slab([(lambda r: full(r)[:, :, 512:1024], kcp(d_win)[:, :, 512:1024])])
            sV = load_slab([(full, kcp(d_win)[:, :, 1024:2048])])
            if main:
                sR = load_slab([(full, kcp(d_win)[:, :, 2048:3072])])
                sO = load_slab([(full, kcp(d_wo))])
                wR = slot_kn(sR, 8, 1024)
                wO = slot_kn(sO, 8, 1024)
            wQK = slot_kn(sQK, 8, 1024)
            wV = slot_kn(sV, 8, 1024)
            sts = pas["sts"]

            def tiles_of(n):
                out = []
                t0 = 0
                while t0 < n:
                    out.append((t0, min(128, n - t0)))
                    t0 += 128
                return out

            def S1(si):
                c0, c1 = sts[si]
                n = c1 - c0
                yield from prenorm_g(pas, c0, c1, l, 0, yst, "g_y", hTs[si % NH], ("g_hT", si % NH), 0,
                                     delay=1 if main else 0)

            def S2(si):
                c0, c1 = sts[si]
                n = c1 - c0
                hT = hTs[si % NH]
                hk = ("g_hT", si % NH)
                p_ = si % NP
                l_sb, kend, vtok, eb = l_sbs[p_], kends[p_], vtoks[p_], ebs[p_]
                b = bank()
                for kc in range(KC):
                    mm(psb[b][0:16, 0:n], wga[:, kc, :], hT[:, kc, 0:n], kc == 0, kc == KC - 1,
                       ["wga", hk], [("ps", b)])
                act(lrT[0:16, 0:n], psb[b][0:16, 0:n], AF.Copy, [("ps", b)], ["lrT"])
                yield
                for i, (t0, Pn) in enumerate(tiles_of(n)):
                    b = bank()
                    mm(psb[b][0:Pn, :], lrT[0:17, t0:t0 + Pn], wgb[0:17, :], True, True, ["lrT", "wgb"], [("ps", b)])
                    act(tA[0:Pn, :], psb[b][0:Pn, :], AF.Exp, [("ps", b)], ["tA"], scale=-1.0)
                    act(l_sb[0:Pn, i, :], tA[0:Pn, :], AF.Ln, ["tA", "kst0"], [("l", p_, i)], bias=ONE[0:Pn, :])
                    yield
                    for hf in range(2):
                        b = bank()
                        for kc in range(KC):
                            mm(psb[b][0:Pn, :], hT[:, kc, t0:t0 + Pn], wV[:, kc, hf * 512:(hf + 1) * 512],
                               kc == 0, kc == KC - 1, [hk, ("ring", sV)], [("ps", b)])
                        act(vtok[0:Pn, i, hf * 512:(hf + 1) * 512], psb[b][0:Pn, :], AF.Copy, [("ps", b)], [("vtok", p_, i)])
                    yield
                    b = bank()
                    mm(psb[b][0:Pn, :], TU[0:Pn, 0:Pn], l_sb[0:Pn, i, :], True, True, ["cstb", ("l", p_, i)], [("ps", b)])
                    act(tB[0:Pn, :], psb[b][0:Pn, :], AF.Exp, [("ps", b)], ["tB"])
                    b = bank()
                    for kc in range(KC):
                        mm(psb[b][0:Pn, :], hT[:, kc, t0:t0 + Pn], wQK[:, kc, 512:1024], kc == 0, kc == KC - 1,
                           [hk, ("ring", sQK)], [("ps", b)])
                    tt(kend[0:Pn, i, :], psb[b][0:Pn, :], tB[0:Pn, :], ALU.mult, [("ps", b), "tB"], [("kend", p_, i)])
                    yield
                    b = bank()
                    for hd in range(H):
                        mm(psb[b][:, hd * 128: hd * 128 + Pn], l_sb[0:Pn, i, hd * 128:(hd + 1) * 128], TL[0:Pn, 0:Pn],
                           True, True, [("l", p_, i), "cstb"], [("ps", b)])
                    pv = psb[b][:, :].rearrange("p (h t) -> p h t", h=H)[:, :, 0:Pn]
                    act(eb[:, :, t0:t0 + Pn], pv, AF.Exp, [("ps", b)], [("eb", p_)])
                    if main:
                        act(einv[:, :, t0:t0 + Pn], pv, AF.Exp, [("ps", b)], ["einv"], scale=-1.0)
                    yield

            def S3(si):
                c0, c1 = sts[si]
                n = c1 - c0
                hT = hTs[si % NH]
                hk = ("g_hT", si % NH)
                eb = ebs[0]
                for hd in range(H):
                    b = bank()
                    for kc in range(KC):
                        mm(psb[b][:, 0:n], wQK[:, kc, hd * 128:(hd + 1) * 128], hT[:, kc, 0:n], kc == 0, kc == KC - 1,
                           [("ring", sQK), hk], [("ps", b)])
                    stt(qdec[:, hd, 0:n], psb[b][:, 0:n], float(DK) ** -0.5, eb[:, hd, 0:n], ALU.mult, ALU.mult,
                        [("ps", b), ("eb", 0)], ["qdec"])
                    yield
                    b = bank()
                    for kc in range(KC):
                        mm(psb[b][:, 0:n], wQK[:, kc, 512 + hd * 128: 512 + (hd + 1) * 128], hT[:, kc, 0:n],
                           kc == 0, kc == KC - 1, [("ring", sQK), hk], [("ps", b)])
                    tt(kinv[:, hd, 0:n], psb[b][:, 0:n], einv[:, hd, 0:n], ALU.mult, [("ps", b), "einv"], ["kinv"])
                    yield
                for c in range(KC):
                    b = bank()
                    for kc in range(KC):
                        mm(psb[b][:, 0:n], wR[:, kc, c * 128:(c + 1) * 128], hT[:, kc, 0:n], kc == 0, kc == KC - 1,
                           [("ring", sR), hk], [("ps", b)])
                    act(rsil[:, c, 0:n], psb[b][:, 0:n], AF.Silu, [("ps", b)], ["rsil"])
                    yield

            def S4(si):
                c0, c1 = sts[si]
                tl = tiles_of(c1 - c0)
                obs = {}
                gens = []
                for i in range(len(tl)):
                    if i == 0:
                        gens.append(T_a(si, i, obs))
                    else:
                        gens.append(merge_g(T_b(si, i - 1, obs), T_a(si, i, obs)))
                gens.append(T_b(si, len(tl) - 1, obs))
                yield from chain(*gens)

            def T_a(si, i, obs):
                c0, c1 = sts[si]
                n = c1 - c0
                p_ = si % NP
                l_sb, kend, vtok, eb = l_sbs[p_], kends[p_], vtoks[p_], ebs[p_]
                t0, Pn = tiles_of(n)[i]
                if True:
                    nch = Pn // 64
                    chunks = []
                    for ch in range(nch):
                        col = c0 + t0 + ch * 64
                        chunks.append([s for (a, b_, s) in pas["segs"] if a <= col < b_][0])
                    if main:
                        b = bank()
                        for hd in range(H):
                            mm(psb[b][0:Pn, hd * 128: hd * 128 + Pn], kinv[:, hd, t0:t0 + Pn], qdec[:, hd, t0:t0 + Pn],
                               True, True, ["kinv", "qdec"], [("ps", b)])
                        tt(sT[0:Pn, :, 0:Pn], psb[b][0:Pn, :].rearrange("p (h t) -> p h t", h=H)[:, :, 0:Pn],
                           MASK[0:Pn, 0:Pn].unsqueeze(1).to_broadcast([Pn, H, Pn]), ALU.mult,
                           [("ps", b), "cstb"], ["sT"])
                        yield
                    if main:
                        tp_ = otile_ctr[0] % 2
                        otile_ctr[0] += 1
                        ob = (2 * tp_, 2 * tp_ + 1)
                    kvb = []

                    def kv_mm(ch):
                        r0 = ch * 64
                        bb = (bank(), bank())
                        for hd in range(H):
                            bk = bb[hd // 2]
                            mm(psb[bk][:, (hd % 2) * 256:(hd % 2) * 256 + 256],
                               kend[r0:r0 + 64, i, hd * 128:(hd + 1) * 128], vtok[r0:r0 + 64, i, hd * 256:(hd + 1) * 256],
                               True, True, [("kend", p_, i), ("vtok", p_, i)], [("ps", bk)])
                        kvb.append(bb)

                    def state_of(ch):
                        if chunks[ch] == 0:
                            return S_m, "m"
                        return S_s, "s"

                    def update(ch):
                        Sst, Skey = state_of(ch)
                        lastc = t0 + ch * 64 + 63
                        for hd in range(H):
                            bk = kvb[ch][hd // 2]
                            stt(Sst[:, hd, :], Sst[:, hd, :], eb[:, hd, lastc:lastc + 1],
                                psb[bk][:, (hd % 2) * 256:(hd % 2) * 256 + 256], ALU.mult, ALU.add,
                                [("S", Skey, hd), ("eb", p_), ("ps", bk)], [("S", Skey, hd)])
                        if chunks[ch] == 0:
                            chunk_ctr[0] += 1
                            npar = chunk_ctr[0] % 2
                            if main:
                                act(S_bf[npar][:], S_m[:], AF.Copy, [("S", "m", h_) for h_ in range(H)], [("Sbf", npar)])

                    def outputs(ch, sbf, sbk):
                        r0 = ch * 64
                        for idx in range(8):
                            hd, ec = idx // 2, idx % 2
                            bk = ob[idx // 4]
                            oc = (idx % 4) * 128 + ch * 64
                            mm(psb[bk][:, oc:oc + 64],
                               vtok[r0:r0 + 64, i, hd * 256 + ec * 128: hd * 256 + ec * 128 + 128],
                               sT[r0:r0 + 64, hd, r0:r0 + 64], True, False, [("vtok", p_, i), "sT"], [("ps", bk)])
                            mm(psb[bk][:, oc:oc + 64], sbf[:, hd, ec * 128:(ec + 1) * 128],
                               qdec[:, hd, t0 + ch * 64: t0 + ch * 64 + 64], False, True,
                               [sbk, "qdec"], [("ps", bk)])

                    srcs = []
                    cc_ = chunk_ctr[0]
                    for ch in range(nch):
                        if chunks[ch] == 0:
                            srcs.append((S_bf[cc_ % 2], ("Sbf", cc_ % 2)))
                            cc_ += 1
                        else:
                            srcs.append((S_bfs, ("Sbf", "s")))
                    kv_mm(0)
                    yield
                    if nch == 1:
                        if main:
                            outputs(0, *srcs[0])
                            yield
                        update(0)
                        yield
                    else:
                        update(0)
                        kv_mm(1)
                        yield
                        if main:
                            outputs(0, *srcs[0])
                            yield
                        update(1)
                        yield
                        if main:
                            outputs(1, *srcs[1])
                            yield
                    if main:
                        obs[i] = ob

            def T_b(si, i, obs):
                c0, c1 = sts[si]
                n = c1 - c0
                t0, Pn = tiles_of(n)[i]
                if True:
                    if main:
                        ob = obs[i]
                        o3 = [psb[ob[k]][:, :].rearrange("p (a t) -> p a t", a=4)[:, :, 0:Pn] for k in range(2)]
                        for k in range(2):
                            act(osq[:, k * 4:(k + 1) * 4, 0:Pn], o3[k], AF.Square, [("ps", ob[k])], ["osq"])
                        yield
                        yield
                        b = bank()
                        for hd in range(H):
                            for ec in range(2):
                                mm(psb[b][:, hd * 128: hd * 128 + Pn], ones[:], osq[:, hd * 2 + ec, 0:Pn], ec == 0, ec == 1,
                                   ["ones", "osq"], [("ps", b)])
                        yield
                        pv = psb[b][:, :].rearrange("p (h t) -> p h t", h=H)[:, :, 0:Pn]
                        act(rsh[:, :, 0:Pn], pv, AF.Ln, [("ps", b), "kst"], ["rsh"], scale=1.0 / DV, bias=EPSC)
                        act(rsh[:, :, 0:Pn], rsh[:, :, 0:Pn], AF.Exp, ["rsh"], ["rsh"], scale=-0.5)
                        for k in range(2):
                            tt(otmp[:, k * 4:(k + 1) * 4, 0:Pn].rearrange("p (h e) t -> p h e t", e=2),
                               o3[k].rearrange("p (h e) t -> p h e t", e=2),
                               rsh[:, k * 2:(k + 1) * 2, 0:Pn].unsqueeze(2).to_broadcast([128, 2, 2, Pn]), ALU.mult,
                               [("ps", ob[k]), "rsh"], ["otmp"])
                        yield
                        for ec in range(2):
                            ov = otmp[:, :, 0:Pn].rearrange("p (h e) t -> p h e t", e=2)[:, :, ec, :]
                            rv = rsil[:, :, t0:t0 + Pn].rearrange("p (h e) t -> p h e t", e=2)[:, :, ec, :]
                            stt(rv, ov, vecs[:, V_GN + ec:V_GN + ec + 1], rv, ALU.mult, ALU.mult,
                                ["otmp", "vecs", "rsil"], ["rsil"])
                        yield

            def S5a(si):
                c0, c1 = sts[si]
                n = c1 - c0
                for c in range(KC):
                    b = bank()
                    for kc in range(KC):
                        mm(psb[b][:, 0:n], wO[:, kc, c * 128:(c + 1) * 128], rsil[:, kc, 0:n], kc == 0, kc == KC - 1,
                           [("ring", sO), "rsil"], [("ps", b)])
                    act(yst[:, c, 0:n], psb[b][:, 0:n], AF.Copy, [("ps", b)], ["g_y"])
                    yield

            def S5b(si):
                c0, c1 = sts[si]
                yield from postnorm_g(pas, c0, c1, l, 0, yst, "g_y", 0)

            nst = len(sts)
            if main:
                run(S1(0))
                run(S2(0))
                run(S3(0))
                for si in range(nst):
                    nxt = si + 1 < nst
                    merge(S4(si), S1(si + 1) if nxt else None)
                    run(S5a(si))
                    merge(S5b(si), chain(S2(si + 1), S3(si + 1)) if nxt else None)
            else:
                run(S1(0))
                if nst > 1:
                    run(S1(1))
                run(S2(0))
                for si in range(nst):
                    merge(S4(si), chain(S1(si + 2) if si + 2 < nst else None, S2(si + 1) if si + 1 < nst else None))
                    for _ in range(2):
                        if extras:
                            l_, m_ = extras.pop(0)
                            mod_slab(l_, m_)
                while extras:
                    l_, m_ = extras.pop(0)
                    mod_slab(l_, m_)
            P.barrier()
            bank_pool[0] = list(range(8))
            arena["off"] = mark0

        def ffn_phase(pas, l):
            T = pas["T"]
            mark0 = arena["off"]
            if True:
                def pb(name, shape, dt):
                    return sb(f"{name}_{pas['name']}{l}", shape, dt)
                hT = pb("f_hT", [128, KC, TMAX], BF16)
                yf = pb("f_y", [128, KC, TMAX], F32)
                hid = [pb(f"f_hid{i}", [128, 4, TMAX], BF16) for i in range(2)]
                for si, (c0, c1) in enumerate(pas["sts"]):
                    prenorm(pas, c0, c1, l, 1, yf[:, :, c0:c1], ("f_y", si), hT, ("f_hT", si), c0)

                def load_g(g):
                    return load_slab([
                        (lambda r: r[:, 0:4096].rearrange("p (k n) -> p k n", k=8), kcp(d_wup[l])[:, :, g * 512:(g + 1) * 512]),
                        (lambda r: r[:, 4096:8192].rearrange("p (k n) -> p k n", k=4), kcp(d_wdn[l])[:, g * 4:(g + 1) * 4, :]),
                    ])

                def up(g, s):
                    U = ring[s][:, 0:4096].rearrange("p (k n) -> p k n", k=8)
                    hb = hid[g % 2]
                    for si, (c0, c1) in enumerate(pas["sts"]):
                        n = c1 - c0
                        for j in range(4):
                            b = bank()
                            for kc in range(KC):
                                mm(psb[b][:, 0:n], U[:, kc, j * 128:(j + 1) * 128], hT[:, kc, c0:c1], kc == 0, kc == KC - 1,
                                   [("ring", s), ("f_hT", si)], [("ps", b)])
                            tmp = tA if j % 2 == 0 else tB
                            tk = "tA" if j % 2 == 0 else "tB"
                            act(tmp[:, 0:n], psb[b][:, 0:n], AF.Relu, [("ps", b)], [tk])
                            tt(hb[:, j, c0:c1], tmp[:, 0:n], tmp[:, 0:n], ALU.mult, [tk], [("hid", g % 2, si)])

                def down(g, s):
                    Dn = ring[s][:, 4096:8192].rearrange("p (k n) -> p k n", k=4)
                    hb = hid[g % 2]
                    for si, (c0, c1) in enumerate(pas["sts"]):
                        n = c1 - c0
                        for c in range(KC):
                            b = bank()
                            for j in range(4):
                                mm(psb[b][:, 0:n], Dn[:, j, c * 128:(c + 1) * 128], hb[:, j, c0:c1], j == 0, j == 3,
                                   [("ring", s), ("hid", g % 2, si)], [("ps", b)])
                            if g == 0:
                                act(yf[:, c, c0:c1], psb[b][:, 0:n], AF.Copy, [("ps", b)], [("f_y", si)])
                            else:
                                tt(yf[:, c, c0:c1], yf[:, c, c0:c1], psb[b][:, 0:n], ALU.add, [("f_y", si), ("ps", b)],
                                   [("f_y", si)])

                slots = {}
                slots[0] = load_g(0)
                up(0, slots[0])
                for g in range(8):
                    if g + 1 < 8:
                        slots[g + 1] = load_g(g + 1)
                        up(g + 1, slots[g + 1])
                    down(g, slots[g])
                for si, (c0, c1) in enumerate(pas["sts"]):
                    postnorm(pas, c0, c1, l, 1, yf, ("f_y", si), c0)
                P.barrier()
            arena["off"] = mark0

        def conv_phase(pas):
            l = 1
            T = pas["T"]
            isB = pas["name"] == "B"
            mark0 = arena["off"]
            if True:
                def pb(name, shape, dt):
                    return sb(f"{name}_{pas['name']}", shape, dt)
                UW = 30 + TMAX + 30
                uT = pb("c_uT", [128, KC, UW], BF16)
                yst = pb("c_y", [128, KC, SUB], F32)
                diags = [pb(f"c_diag{i_}", [128, 31, 128], BF16) for i_ in range(2)]
                mu = pb("c_mu", [128, SUB], F32)
                accs = [pb(f"c_acc{i_}", [128, SUB], F32) for i_ in range(2)]
                actr = [0]

                def ucol(c):
                    return 30 + c if (not isB or c < 960) else 60 + c

                if not isB:
                    P.op("dve", lambda e: e.memset(uT[:, :, 0:30], 0.0), (), ["uT"])
                else:
                    dve(lambda e: e.tensor_copy(out=uT[:, :, 0:30], in_=uctx[:]), ["uctx"], ["uT"])
                    dve(lambda e: e.tensor_copy(out=uT[:, :, 990:1020], in_=cctx[:]), ["cctx"], ["uT"])

                mark1 = arena["off"]
                sts = pas["sts"]
                nst = len(sts)
                hT = sb(f"c_hT_{pas['name']}", [128, KC, TMAX], BF16)
                slabs = []
                for hf in range(2):
                    slabs.append(load_slab([
                        (lambda r: r[:, 0:8192].rearrange("p (k n) -> p k n", k=8)[:, :, 0:512],
                         kcp(d_cwin)[:, :, hf * 512:(hf + 1) * 512]),
                        (lambda r: r[:, 0:8192].rearrange("p (k n) -> p k n", k=8)[:, :, 512:1024],
                         kcp(d_cwin)[:, :, 1024 + hf * 512: 1024 + (hf + 1) * 512]),
                    ]))

                def C1(si):
                    c0, c1 = sts[si]
                    yield from prenorm_g(pas, c0, c1, l, 0, yst, "c_y", hT, ("c_hT", si), c0)

                def C2(si):
                    c0, c1 = sts[si]
                    n = c1 - c0
                    for hf in range(2):
                        s = slabs[hf]
                        W = slot_kn(s, 8, 1024)
                        for cc in range(4):
                            fc = hf * 4 + cc
                            ba = bank()
                            for kc in range(KC):
                                mm(psb[ba][:, 0:n], W[:, kc, cc * 128:(cc + 1) * 128], hT[:, kc, c0:c1], kc == 0, kc == KC - 1,
                                   [("ring", s), ("c_hT", si)], [("ps", ba)])
                            bg = bank()
                            for kc in range(KC):
                                mm(psb[bg][:, 0:n], W[:, kc, 512 + cc * 128: 512 + (cc + 1) * 128], hT[:, kc, c0:c1],
                                   kc == 0, kc == KC - 1, [("ring", s), ("c_hT", si)], [("ps", bg)])
                            tg = tA if fc % 2 == 0 else tB
                            tgk = "tA" if fc % 2 == 0 else "tB"
                            act(tg[:, 0:n], psb[bg][:, 0:n], AF.Sigmoid, [("ps", bg), "vecs"], [tgk],
                                bias=vecs[:, V_CBIN + 8 + fc: V_CBIN + 9 + fc])
                            bav = vecs[:, V_CBIN + fc: V_CBIN + fc + 1]
                            for (a, b_, sq_) in segs_of(pas, c0, c1):
                                stt(uT[:, fc, ucol(a): ucol(a) + (b_ - a)], psb[ba][:, a - c0:b_ - c0], bav,
                                    tg[:, a - c0:b_ - c0], ALU.add, ALU.mult, [("ps", ba), "vecs", tgk], ["uT"])
                            if isB:
                                if c0 <= 930 and c1 >= 960:
                                    stt(ulast_m[:, fc, :], psb[ba][:, 930 - c0:960 - c0], bav, tg[:, 930 - c0:960 - c0],
                                        ALU.add, ALU.mult, [("ps", ba), "vecs", tgk], ["ulast_m"])
                                if c0 <= 994 and c1 >= 1024:
                                    stt(ulast_s[:, fc, :], psb[ba][:, 994 - c0:1024 - c0], bav, tg[:, 994 - c0:1024 - c0],
                                        ALU.add, ALU.mult, [("ps", ba), "vecs", tgk], ["ulast_s"])
                            yield

                run(C1(0))
                for si in range(nst):
                    merge(C2(si), C1(si + 1) if si + 1 < nst else None)
                P.barrier()
                arena["off"] = mark1
                if not isB:
                    dve(lambda e: e.tensor_copy(out=uctx[:], in_=uT[:, :, 30 + T - 30: 30 + T]), ["uT"], ["uctx"])
                zT = sb(f"c_zT_{pas['name']}", [128, KC, TMAX], BF16)
                csegs_st = [[] for _ in sts]
                for (a, b_, sq_) in pas["segs"]:
                    c = a
                    while c < b_:
                        si = pas["stidx"][c]
                        nn = min(SUB, min(b_, sts[si][1]) - c)
                        csegs_st[si].append((c, nn, ucol(c) - 30))
                        c += nn
                sO = load_slab([(lambda r: r[:, 0:8192].rearrange("p (k n) -> p k n", k=8), kcp(d_cwo))])
                wO = slot_kn(sO, 8, 1024)
                dctr = [0]

                def C3(si):
                    for fc in range(KC):
                        wd = vecs[:, V_WDW + fc * 31: V_WDW + fc * 31 + 31]
                        dp = dctr[0] % 2
                        dctr[0] += 1
                        diag = diags[dp]
                        dgk = ("diag", dp)
                        tt(diag[:], IDb.unsqueeze(1).to_broadcast([128, 31, 128]), wd.unsqueeze(2).to_broadcast([128, 31, 128]),
                           ALU.mult, ["cstb", "vecs"], [dgk])
                        yield
                        for (c, nn, us) in csegs_st[si]:
                            if NPOOL > 0:
                                ap_ = actr[0] % 2
                                actr[0] += 1
                                acc = accs[ap_]
                                ak = ("acc", ap_)
                                P.op("pool", lambda e, acc=acc, fc=fc, us=us, nn=nn, wd=wd: e.tensor_scalar_mul(
                                    out=acc[:, 0:nn], in0=uT[:, fc, us: us + nn], scalar1=wd[:, 0:1]), ["uT", "vecs"], [ak])
                                for j in range(1, NPOOL):
                                    P.op("pool", lambda e, acc=acc, fc=fc, us=us, nn=nn, wd=wd, j=j: e.scalar_tensor_tensor(
                                        out=acc[:, 0:nn], in0=uT[:, fc, us + j: us + j + nn], scalar=wd[:, j:j + 1],
                                        in1=acc[:, 0:nn], op0=ALU.mult, op1=ALU.add), ["uT", "vecs", ak], [ak])
                            b = bank()
                            for j in range(NPOOL, 31):
                                mm(psb[b][:, 0:nn], diag[:, j, :], uT[:, fc, us + j: us + j + nn], j == NPOOL, j == 30,
                                   [dgk, "uT"], [("ps", b)])
                            if NPOOL > 0:
                                stt(zT[:, fc, c:c + nn], psb[b][:, 0:nn], vecs[:, V_CBDW + fc: V_CBDW + fc + 1], acc[:, 0:nn],
                                    ALU.add, ALU.add, [("ps", b), "vecs", ak], [("zT", si)])
                            else:
                                act(zT[:, fc, c:c + nn], psb[b][:, 0:nn], AF.Identity, [("ps", b), "vecs"], [("zT", si)],
                                    bias=vecs[:, V_CBDW + fc: V_CBDW + fc + 1])
                            yield

                def C4(si):
                    c0, c1 = sts[si]
                    n = c1 - c0
                    zk = ("zT", si)
                    act(sq[:, :, 0:n], zT[:, :, c0:c1], AF.Square, [zk], ["sq", ("sq", 0), ("sq", 1)])
                    yield
                    bm = bank()
                    for kc in range(KC):
                        mm(psb[bm][:, 0:n], ones[:], zT[:, kc, c0:c1], kc == 0, kc == KC - 1, ["ones", zk], [("ps", bm)])
                    yield
                    b2 = bank()
                    for kc in range(KC):
                        mm(psb[b2][:, 0:n], ones[:], sq[:, kc, 0:n], kc == 0, kc == KC - 1, ["ones", "sq", ("sq", kc // 4)], [("ps", b2)])
                    dve(lambda e, n=n, bm=bm: e.tensor_scalar_mul(out=mu[:, 0:n], in0=psb[bm][:, 0:n], scalar1=1.0 / D),
                        [("ps", bm)], ["mu"])
                    tt(rs[:, 0:n], mu[:, 0:n], mu[:, 0:n], ALU.mult, ["mu"], ["rs"])
                    yield
                    stt(rs[:, 0:n], psb[b2][:, 0:n], 1.0 / D, rs[:, 0:n], ALU.mult, ALU.subtract, [("ps", b2), "rs"], ["rs"])
                    act(rs[:, 0:n], rs[:, 0:n], AF.Ln, ["rs", "kst"], ["rs"], bias=EPSC)
                    act(rs[:, 0:n], rs[:, 0:n], AF.Exp, ["rs"], ["rs"], scale=-0.5)
                    yield
                    tt(yst[:, :, 0:n], zT[:, :, c0:c1], mu[:, 0:n].unsqueeze(1).to_broadcast([128, KC, n]), ALU.subtract,
                       [zk, "mu"], ["c_y"])
                    yield
                    tt(yst[:, :, 0:n], yst[:, :, 0:n], rs[:, 0:n].unsqueeze(1).to_broadcast([128, KC, n]), ALU.mult,
                       ["c_y", "rs"], ["c_y"])
                    yield
                    for fc in range(KC):
                        act(zT[:, fc, c0:c1], yst[:, fc, 0:n], AF.Silu, ["c_y", "vecs"], [zk],
                            scale=vecs[:, V_LNG + fc: V_LNG + fc + 1], bias=vecs[:, V_LNB + fc: V_LNB + fc + 1])
                        if fc % 2 == 1:
                            yield

                def C5a(si):
                    c0, c1 = sts[si]
                    n = c1 - c0
                    for c in range(KC):
                        b = bank()
                        for kc in range(KC):
                            mm(psb[b][:, 0:n], wO[:, kc, c * 128:(c + 1) * 128], zT[:, kc, c0:c1], kc == 0, kc == KC - 1,
                               [("ring", sO), ("zT", si)], [("ps", b)])
                        act(yst[:, c, 0:n], psb[b][:, 0:n], AF.Identity, [("ps", b), "vecs"], ["c_y"],
                            bias=vecs[:, V_CBOUT + c: V_CBOUT + c + 1])
                        yield

                def C5b(si):
                    c0, c1 = sts[si]
                    yield from postnorm_g(pas, c0, c1, l, 0, yst, "c_y", 0)

                run(C3(0))
                for si in range(nst):
                    merge(C3(si + 1) if si + 1 < nst else None, chain(C4(si), idle(2), C5a(si), C5b(si)))
                P.barrier()
            arena["off"] = mark0

        def chk(tag):
            if _DEBUG_STOP == tag:
                raise _Stop()

        def whole():
            chk("init")
            mod_slab(0, 0)
            mod_slab(0, 1)
            gpre = vecs[:, vnorm(0, 0): vnorm(0, 0) + 8]
            for s in range(2):
                stt(Aall[:, 0, 0, :, s], modv[:, 0, 1, :, s], 1.0, gpre, ALU.add, ALU.mult, [("mod", 0, 1), "vecs"], [("A", 0, 0)])
            chk("mod0")
            later = [(0, 2), (0, 3), (0, 4), (0, 5), (1, 0), (1, 1), (1, 2), (1, 3), (1, 4), (1, 5)]
            for pi, pas in enumerate(passes):
                cur["pas"] = pas
                P.tag = pas["name"] + ".gla"
                load_x(pas)
                chk(pas["name"] + "load")
                if pas["kind"] == "prefix":
                    gla_phase(pas, later[0:6] if pi == 0 else later[6:10])
                else:
                    gla_phase(pas)
                chk(pas["name"] + "gla")
                if pas["kind"] == "prefix":
                    P.tag = pas["name"] + ".mod"
                    if pas["name"] == "P2":
                        dve(lambda e: e.tensor_scalar_mul(out=S_m[:], in0=S_m[:], scalar1=flag[:, 0:1]), [("S", "m", h_) for h_ in range(H)] + ["flag"], [("S", "m", h_) for h_ in range(H)])
                        par = chunk_ctr[0] % 2
                        act(S_bf[par][:], S_m[:], AF.Copy, [("S", "m", h_) for h_ in range(H)], [("Sbf", par)])
                        for s in range(2):
                            tt(Gall[:, 0, 0, :, s], modv[:, 0, 2, :, s], vecs[:, vnorm(1, 0): vnorm(1, 0) + 8], ALU.mult,
                               [("mod", 0, 2), "vecs"], [("G", 0, 0)])
                        mod_derive(0, 1)
                        mod_derive(1, 0)
                        mod_derive(1, 1)
                    chk(pas["name"])
                    continue
                P.tag = pas["name"] + ".ffn0"
                ffn_phase(pas, 0)
                chk(pas["name"] + "ffn0")
                P.tag = pas["name"] + ".conv"
                conv_phase(pas)
                chk(pas["name"] + "conv")
                P.tag = pas["name"] + ".ffn1"
                ffn_phase(pas, 1)
                chk(pas["name"] + "ffn1")
                for i, (c0, c1) in enumerate(pas["sts"]):
                    t = dma("sp", o_yT[:, :, pas["col0"] + c0: pas["col0"] + c1], xT[:, :, c0:c1], f"oy{i}", [("xT", i)], ())
                    P.out_tokens.append(t)
                chk(pas["name"])

        cur = {}
        try:
            whole()
        except _Stop:
            pas = cur.get("pas")
            if pas is not None and pas["kind"] == "main":
                for i, (c0, c1) in enumerate(pas["sts"]):
                    t = dma("sp", o_yT[:, :, pas["col0"] + c0: pas["col0"] + c1], xT[:, :, c0:c1], f"oy{i}", [("xT", i)], ())
                    P.out_tokens.append(t)
        P.out_tokens.append(dma("sp", o_sgm.rearrange("h d e -> d h e"), S_m[:], "o_sgm", [("S", "m", h_) for h_ in range(H)], ()))
        P.out_tokens.append(dma("sp", o_sgs.rearrange("h d e -> d h e"), S_s[:], "o_sgs", [("S", "s", h_) for h_ in range(H)], ()))
        P.out_tokens.append(dma("sp", o_cvm, ulast_m[:], "o_cvm", ["ulast_m"], ()))
        P.out_tokens.append(dma("sp", o_cvs, ulast_s[:], "o_cvs", ["ulast_s"], ()))
        fin = Op(None)
        seen = {}
        for t in P.out_tokens:
            seen[t[1]] = max(seen.get(t[1], 0), t[2])
        for k, v in seen.items():
            fin.waits.append(("d", k, v))
        P.ops["sp"].append(fin)

        global _LAST_P
        _LAST_P = P
        with nc.Block() as block:
            P.emit(nc, block, sem, dsems)
    return nc


_NC_CACHE = {}


def _fm(a2d):
    t = a2d.shape[0]
    return np.ascontiguousarray(a2d.reshape(t, KC, 128).transpose(2, 1, 0))


def _vec_cols(v):
    return v.reshape(-1, 128).T


def _consts():
    s = np.arange(128)[:, None]
    t = np.arange(128)[None, :]
    same = (s // 64) == (t // 64)
    TLm = np.where(same & (s <= t), -1.0 / 16.0, 0.0)
    TUm = np.where(same & (s > t), -1.0 / 16.0, 0.0)
    MK = np.where(same & (s <= t), 1.0, 0.0)
    ID = np.eye(128)
    return np.ascontiguousarray(np.concatenate([TLm, TUm, MK, ID], axis=1).astype(np.float32))


def kernel(x_prompt, x_sample, c_prompt, c_sample, state_gla, cache_conv, w_mod, b_mod,
           norm_mix_pre, norm_mix_post, norm_ffn_pre, norm_ffn_post, w_ffn_up, w_ffn_down,
           gla_w_in, gla_w_gate_a, gla_w_gate_b, gla_b_gate, gla_norm, gla_w_out,
           conv_w_in, conv_b_in, conv_w_dw, conv_b_dw, conv_ln_g, conv_ln_b, conv_w_out, conv_b_out):
    f = lambda a: np.ascontiguousarray(np.asarray(a, dtype=np.float32))
    x_prompt, x_sample, c_prompt, c_sample = f(x_prompt), f(x_sample), f(c_prompt), f(c_sample)
    state_gla, cache_conv = f(state_gla), f(cache_conv)
    if "nc" not in _NC_CACHE:
        _NC_CACHE["nc"] = build_nc()
    nc = _NC_CACHE["nc"]

    cols = []
    for arr in (norm_mix_pre, norm_mix_post, norm_ffn_pre, norm_ffn_post):
        for l in range(2):
            cols.append(_vec_cols(f(arr)[l]))
    for l in range(2):
        cols.append(_vec_cols(f(b_mod)[l]))
    cols.append(_vec_cols(f(gla_norm)[0]))
    cols.append(_vec_cols(f(conv_b_in)[0]))
    cols.append(_vec_cols(f(conv_b_dw)[0]))
    cols.append(_vec_cols(f(conv_ln_g)[0]))
    cols.append(_vec_cols(f(conv_ln_b)[0]))
    cols.append(_vec_cols(f(conv_b_out)[0]))
    wdw = f(conv_w_dw)[0]
    cols.append(np.ascontiguousarray(wdw.reshape(31, KC, 128).transpose(2, 1, 0)).reshape(128, KC * 31))
    vecs = np.ascontiguousarray(np.concatenate(cols, axis=1).astype(np.float32))
    assert vecs.shape == (128, NV), vecs.shape
    consts = _consts()
    wgb_aug = np.ascontiguousarray(np.concatenate([f(gla_w_gate_b)[0], f(gla_b_gate)[0][None, :]], axis=0))

    shared = {
        "vecs": vecs, "consts": consts,
        "w_mod": f(w_mod), "w_ffn_up": f(w_ffn_up), "w_ffn_down": f(w_ffn_down),
        "gla_w_in": f(gla_w_in)[0], "gla_w_ga": f(gla_w_gate_a)[0], "gla_w_gb": wgb_aug,
        "gla_w_out": f(gla_w_out)[0], "conv_w_in": f(conv_w_in)[0], "conv_w_out": f(conv_w_out)[0],
    }
    in_maps = []
    for c in range(8):
        b = c // 2
        odd = c % 2
        t0 = 0 if not odd else 4096 - NMAIN
        xm = x_prompt[b, t0:t0 + NMAIN]
        xall = np.concatenate([xm, x_sample[c]], axis=0)
        xp = x_prompt[b, 0:NPRE]
        c2 = np.stack([c_prompt[b], c_sample[c]], axis=0)
        m = dict(shared)
        m["xT"] = _fm(xall)
        m["xpT"] = _fm(xp)
        m["cT"] = np.ascontiguousarray(c2.reshape(2, KC, 128).transpose(2, 1, 0))
        m["sgla"] = np.ascontiguousarray(state_gla[0, c])
        m["cconvT"] = _fm(cache_conv[0, c])
        m["flag"] = np.full((128, 1), float(odd), dtype=np.float32)
        in_maps.append(m)

    res = run_bass_kernel_spmd(nc, in_maps, core_ids=list(range(8)))
    R = res.results

    def tm(a):
        return np.ascontiguousarray(a.transpose(2, 1, 0).reshape(a.shape[2], D))

    y_prompt = np.empty((4, 4096, D), np.float32)
    y_sample = np.empty((8, 64, D), np.float32)
    gla_p = np.empty((1, 4, H, DK, DV), np.float32)
    conv_p = np.empty((1, 4, 30, D), np.float32)
    gla_s = np.empty((1, 8, H, DK, DV), np.float32)
    conv_s = np.empty((1, 8, 30, D), np.float32)
    for c in range(8):
        b = c // 2
        odd = c % 2
        y = tm(np.asarray(R[c]["yT"]))
        if not odd:
            y_prompt[b, 0:2048] = y[0:2048]
        else:
            y_prompt[b, 2048:4096] = y[NMAIN - 2048:NMAIN]
            gla_p[0, b] = np.asarray(R[c]["sg_main"])
            conv_p[0, b] = tm(np.asarray(R[c]["cv_main"]))
        y_sample[c] = y[NMAIN:NMAIN + 64]
        gla_s[0, c] = np.asarray(R[c]["sg_smp"])
        conv_s[0, c] = tm(np.asarray(R[c]["cv_smp"]))
    return (y_prompt, y_sample, gla_p, conv_p, gla_s, conv_s)
```

```python
import numpy as np
import concourse.bass as bass
import concourse.mybir as mybir
from concourse.bass_utils import run_bass_kernel_spmd

F32 = mybir.dt.float32
BF16 = mybir.dt.bfloat16
AF = mybir.ActivationFunctionType
ALU = mybir.AluOpType

D = 1024
KC = 8
H = 4
DK = 128
DV = 256
FF = 4096
NMAIN = 2112
NPRE = 1984
NTOK = NMAIN + 64
EPS = 1e-6
SAME_ENGINE_SYNC = True
SYNC_ENGS = ("act", "dve", "pool")
FULL_SYNC_ENGS = ("dve", "pool")
SUB = 384
NTL = 3
NPOOL = 0
_DEBUG_STOP = None


class _Stop(Exception):
    pass

V_NORM = 0
V_BMOD = 64
V_GN = 160
V_CBIN = 162
V_CBDW = 178
V_LNG = 186
V_LNB = 194
V_CBOUT = 202
V_WDW = 210
NV = 210 + 248

C_TL = 0
C_TU = 128
C_MASK = 256
C_ID = 384


def vnorm(which, l):
    return V_NORM + (which * 2 + l) * 8


class Op:
    __slots__ = ("fn", "waits", "inc", "dma", "tag")

    def __init__(self, fn):
        self.fn = fn
        self.tag = None
        self.waits = []
        self.inc = False
        self.dma = None


class Prog:
    ENGS = ("pe", "act", "dve", "pool", "sp")

    def __init__(self):
        self.ops = {e: [] for e in self.ENGS}
        self.last_write = {}
        self.readers = {}
        self.waited = {e: {} for e in self.ENGS}
        self.dma_val = {}
        self.out_tokens = []
        self.tag = "init"

    def _need(self, eng, tok, deps, raw):
        if tok is None:
            return
        if tok[0] == "e":
            _, src, idx = tok
            if src == eng:
                if not (SAME_ENGINE_SYNC and eng in SYNC_ENGS and (raw or eng in FULL_SYNC_ENGS)):
                    return
            key = src
        else:
            _, src, idx = tok
            key = ("d", src)
        if self.waited[eng].get(key, -1) >= idx:
            return
        if deps.get(key, -1) < idx:
            deps[key] = idx

    def op(self, eng, fn, reads=(), writes=(), dma_sem=None):
        deps = {}
        for r in reads:
            self._need(eng, self.last_write.get(r), deps, True)
        for w in writes:
            self._need(eng, self.last_write.get(w), deps, False)
            for t in self.readers.get(w, ()):
                self._need(eng, t, deps, False)
        o = Op(fn)
        for key, idx in deps.items():
            self.waited[eng][key] = idx
            if isinstance(key, tuple):
                o.waits.append(("d", key[1], idx))
            else:
                self.ops[key][idx].inc = True
                o.waits.append(("e", key, idx))
        my_idx = len(self.ops[eng])
        o.tag = self.tag
        self.ops[eng].append(o)
        if dma_sem is not None:
            v = self.dma_val.get(dma_sem, 0) + 16
            self.dma_val[dma_sem] = v
            o.dma = dma_sem
            tok = ("d", dma_sem, v)
        else:
            tok = ("e", eng, my_idx)
        for r in reads:
            self.readers.setdefault(r, []).append(tok)
        for w in writes:
            self.last_write[w] = tok
            self.readers[w] = []
        return tok

    def barrier(self, engs=("pe", "act", "dve")):
        last = {}
        for s in engs:
            idx = len(self.ops[s]) - 1
            while idx >= 0 and self.ops[s][idx].fn is None:
                idx -= 1
            if idx >= 0:
                last[s] = idx
        for e in engs:
            deps = {}
            for s, idx in last.items():
                if s != e:
                    self._need(e, ("e", s, idx), deps, False)
            if deps:
                o = Op(None)
                for key, idx in deps.items():
                    self.waited[e][key] = idx
                    self.ops[key][idx].inc = True
                    o.waits.append(("e", key, idx))
                self.ops[e].append(o)

    def emit(self, nc, block, sems, dsems):
        cum = {}
        for e in self.ENGS:
            c = 0
            arr = []
            for o in self.ops[e]:
                if o.inc:
                    c += 1
                arr.append(c)
            cum[e] = arr
        prog = self

        def run(eng_name, e):
            for o in prog.ops[eng_name]:
                for w in o.waits:
                    if w[0] == "e":
                        e.wait_ge(sems[w[1]], cum[w[1]][w[2]])
                    else:
                        e.wait_ge(dsems[w[1]], w[2])
                if o.fn is None:
                    continue
                ins = o.fn(e)
                if o.dma is not None:
                    ins.then_inc(dsems[o.dma], 16)
                elif o.inc:
                    ins.then_inc(sems[eng_name], 1)

        @block.tensor
        def _(e):
            run("pe", e)

        @block.scalar
        def _(e):
            run("act", e)

        @block.vector
        def _(e):
            run("dve", e)

        @block.gpsimd
        def _(e):
            run("pool", e)

        @block.sync
        def _(e):
            run("sp", e)


def build_nc():
    nc = bass.Bass("TRN2", target_bir_lowering=False)

    def din(name, shape):
        return nc.dram_tensor(name, list(shape), F32, kind="ExternalInput").ap()

    def dout(name, shape):
        return nc.dram_tensor(name, list(shape), F32, kind="ExternalOutput").ap()

    d_xT = din("xT", [128, KC, NTOK])
    d_xp = din("xpT", [128, KC, NPRE])
    d_cT = din("cT", [128, KC, 2])
    d_sg = din("sgla", [H, DK, DV])
    d_cc = din("cconvT", [128, KC, 30])
    d_flag = din("flag", [128, 1])
    d_vecs = din("vecs", [128, NV])
    d_consts = din("consts", [128, 512])
    d_wmod = din("w_mod", [2, D, 6 * D])
    d_wup = din("w_ffn_up", [2, D, FF])
    d_wdn = din("w_ffn_down", [2, FF, D])
    d_win = din("gla_w_in", [D, 3072])
    d_wga = din("gla_w_ga", [D, 16])
    d_wgb = din("gla_w_gb", [17, 512])
    d_wo = din("gla_w_out", [D, D])
    d_cwin = din("conv_w_in", [D, 2 * D])
    d_cwo = din("conv_w_out", [D, D])

    o_yT = dout("yT", [128, KC, NTOK])
    o_sgm = dout("sg_main", [H, DK, DV])
    o_sgs = dout("sg_smp", [H, DK, DV])
    o_cvm = dout("cv_main", [128, KC, 30])
    o_cvs = dout("cv_smp", [128, KC, 30])

    P = Prog()
    TMAX = 1152

    def kcp(ap2d):
        return ap2d.rearrange("(kc p) n -> p kc n", p=128)

    from contextlib import ExitStack
    es = ExitStack()
    with es:
        arena = {"off": 16512}
        DTSZ = {F32: 4, BF16: 2}

        def sb(name, shape, dt):
            size = DTSZ[dt]
            for d_ in shape[1:]:
                size *= d_
            size = (size + 31) // 32 * 32
            off = arena["off"]
            assert off + size <= 229376, (name, off, size)
            arena["off"] = off + size
            arena["peak"] = max(arena.get("peak", 0), arena["off"])
            return nc.alloc_sbuf_tensor_at(name, list(shape), dt, offset=off)

        xT = sb("xT_sb", [128, KC, TMAX], F32)
        ring = [sb(f"ring{i}", [128, 8192], BF16) for i in range(4)]
        vecs = sb("vecs_sb", [128, NV], F32)
        cstb = sb("consts_bf", [128, 512], BF16)
        ones = sb("ones_bf", [128, 128], BF16)
        kst = sb("kst", [128, 4], F32)
        flag = sb("flag_sb", [128, 1], F32)
        cT = sb("cT_sb", [128, KC, 2], F32)
        ctmp = sb("ctmp", [128, KC, 2], F32)
        csil = sb("csil", [128, KC, 2], BF16)
        modv = sb("modv", [128, 2, 6, KC, 2], F32)
        Aall = sb("Aall", [128, 2, 2, KC, 2], F32)
        Gall = sb("Gall", [128, 2, 2, KC, 2], F32)
        nbg = sb("nbg", [128, 8], F32)
        wga = sb("wga", [128, KC, 16], BF16)
        wgb = sb("wgb", [17, 512], BF16)
        S_m = sb("S_m", [128, H, DV], F32)
        S_s = sb("S_s", [128, H, DV], F32)
        S_bf = [sb(f"S_bf{i}", [128, H, DV], BF16) for i in range(2)]
        S_bfs = sb("S_bfs", [128, H, DV], BF16)
        uctx = sb("uctx", [128, KC, 30], BF16)
        cctx = sb("cctx", [128, KC, 30], F32)
        ulast_m = sb("ulast_m", [128, KC, 30], F32)
        ulast_s = sb("ulast_s", [128, KC, 30], F32)
        rs = sb("rs", [128, SUB], F32)
        sq = sb("sq", [128, KC, SUB], BF16)
        tA = sb("tA", [128, 512], F32)
        tB = sb("tB", [128, 512], F32)

        mark_init = arena["off"]
        cst = sb("consts_sb", [128, 512], F32)
        psb = [es.enter_context(nc.psum_tensor(f"ps{i}", [128, 512], F32)) for i in range(8)]
        sem = {e: es.enter_context(nc.semaphore(f"s_{e}")) for e in Prog.ENGS}
        dsems = {}

        def dsem(name):
            if name not in dsems:
                dsems[name] = es.enter_context(nc.semaphore(f"d_{name}"))
            return name

        bank_ctr = [0]
        bank_pool = [list(range(8))]

        def bank():
            pool = bank_pool[0]
            for _ in range(len(pool)):
                b = pool[bank_ctr[0] % len(pool)]
                bank_ctr[0] += 1
                k = ("ps", b)
                if k in P.last_write and not P.readers.get(k):
                    continue
                return b
            raise RuntimeError("no free PSUM bank")

        TL = cstb[:, C_TL:C_TL + 128]
        TU = cstb[:, C_TU:C_TU + 128]
        IDb = cstb[:, C_ID:C_ID + 128]
        MASK = cstb[:, C_MASK:C_MASK + 128]
        ONE = kst[:, 0:1]
        EPSC = kst[:, 1:2]

        def mm(out, lhsT, rhs, start, stop, reads, writes):
            P.op("pe", lambda e, o=out, l=lhsT, r=rhs, s=start, t=stop: e.matmul(o, lhsT=l, rhs=r, start=s, stop=t),
                 reads=reads, writes=writes)

        def act(out, in_, func, reads, writes, scale=1.0, bias=None):
            if bias is None:
                P.op("act", lambda e: e.activation(out=out, in_=in_, func=func, scale=scale), reads=reads, writes=writes)
            else:
                P.op("act", lambda e: e.activation(out=out, in_=in_, func=func, scale=scale, bias=bias),
                     reads=reads, writes=writes)

        def dve(fn, reads, writes):
            P.op("dve", fn, reads=reads, writes=writes)

        def tt(out, in0, in1, op, reads, writes):
            dve(lambda e: e.tensor_tensor(out=out, in0=in0, in1=in1, op=op), reads, writes)

        def stt(out, in0, scalar, in1, op0, op1, reads, writes):
            dve(lambda e: e.scalar_tensor_tensor(out=out, in0=in0, scalar=scalar, in1=in1, op0=op0, op1=op1),
                reads, writes)

        def dma(eng, out, in_, semname, reads, writes):
            return P.op(eng, lambda e: e.dma_start(out=out, in_=in_), reads=reads, writes=writes, dma_sem=dsem(semname))

        slot_ctr = [0]

        def load_slab(parts):
            s = slot_ctr[0] % 4
            slot_ctr[0] += 1
            for (dstf, src) in parts:
                dma("pool", dstf(ring[s]), src, f"ring{s}", reads=(), writes=[("ring", s)])
            return s

        def slot_kn(s, kc, n, off=0):
            return ring[s][:, off:off + kc * n].rearrange("p (k n) -> p k n", k=kc)

        dma("sp", vecs[:], d_vecs, "vecs", (), ["vecs"])
        dma("sp", cst[:], d_consts, "consts", (), ["cst"])
        dma("sp", flag[:], d_flag, "flag", (), ["flag"])
        dma("sp", cT[:], d_cT, "cT", (), ["cT"])
        dma("sp", S_s[:], d_sg.rearrange("h d e -> d h e"), "S_s", (), [("S", "s", h_) for h_ in range(H)])
        dma("sp", cctx[:], d_cc, "cctx", (), ["cctx"])
        dma("pool", wga[:], kcp(d_wga), "wga", (), ["wga"])
        dma("pool", wgb[:], d_wgb, "wgb", (), ["wgb"])
        P.op("pool", lambda e: e.memset(ones[:], 1.0), (), ["ones"])
        P.op("pool", lambda e: e.memset(kst[:, 0:1], 1.0), (), ["kst0"])
        P.op("pool", lambda e: e.memset(kst[:, 1:2], EPS), (), ["kst"])
        P.op("pool", lambda e: e.memset(S_m[:], 0.0), (), [("S", "m", h_) for h_ in range(H)])
        P.op("pool", lambda e: e.memset(S_bf[0][:], 0.0), (), [("Sbf", 0)])
        dve(lambda e: e.tensor_copy(out=cstb[:], in_=cst[:]), ["cst"], ["cstb"])
        dve(lambda e: e.tensor_copy(out=S_bfs[:], in_=S_s[:]), [("S", "s", h_) for h_ in range(H)], [("Sbf", "s")])
        dve(lambda e: e.tensor_scalar_mul(out=nbg[:], in0=vecs[:, V_CBIN + 8:V_CBIN + 16], scalar1=-1.0), ["vecs"], ["nbg"])
        act(csil[:], cT[:], AF.Silu, ["cT"], ["csil"])
        P.barrier()
        arena["off"] = mark_init

        def mod_slab(l, m):
            s = load_slab([(lambda r: r[:, 0:8192].rearrange("p (k n) -> p k n", k=8),
                            kcp(d_wmod[l])[:, :, m * 1024:(m + 1) * 1024])])
            w = slot_kn(s, 8, 1024)
            b = bank()
            for j in range(8):
                for kc in range(KC):
                    mm(psb[b][:, j * 2:j * 2 + 2], w[:, kc, j * 128:(j + 1) * 128], csil[:, kc, :],
                       kc == 0, kc == KC - 1, [("ring", s), "csil"], [("ps", b)])
            bm = vecs[:, V_BMOD + (l * 6 + m) * 8: V_BMOD + (l * 6 + m) * 8 + 8]
            tt(modv[:, l, m, :, :], psb[b][:, 0:16].rearrange("p (k s) -> p k s", s=2),
               bm.unsqueeze(2).to_broadcast([128, 8, 2]), ALU.add, [("ps", b), "vecs"], [("mod", l, m)])

        def mod_derive(l, w):
            msc = 1 + 3 * w
            mgt = 2 + 3 * w
            gpre = vecs[:, vnorm(0 if w == 0 else 2, l): vnorm(0 if w == 0 else 2, l) + 8]
            gpost = vecs[:, vnorm(1 if w == 0 else 3, l): vnorm(1 if w == 0 else 3, l) + 8]
            for s in range(2):
                stt(Aall[:, l, w, :, s], modv[:, l, msc, :, s], 1.0, gpre, ALU.add, ALU.mult,
                    [("mod", l, msc), "vecs"], [("A", l, w)])
                tt(Gall[:, l, w, :, s], modv[:, l, mgt, :, s], gpost, ALU.mult,
                   [("mod", l, mgt), "vecs"], [("G", l, w)])

        def segs_of(pas, c0, c1):
            out = []
            for (a, b, s) in pas["segs"]:
                lo, hi = max(a, c0), min(b, c1)
                if lo < hi:
                    out.append((lo, hi, s))
            return out

        def stats_rstd(src_key, n, nfeat_scale):
            b = bank()
            for kc in range(KC):
                mm(psb[b][:, 0:n], ones[:], sq[:, kc, 0:n], kc == 0, kc == KC - 1, ["ones", "sq", ("sq", kc // 4)], [("ps", b)])
            act(rs[:, 0:n], psb[b][:, 0:n], AF.Ln, [("ps", b), "kst"], ["rs"], scale=nfeat_scale, bias=EPSC)
            act(rs[:, 0:n], rs[:, 0:n], AF.Exp, ["rs"], ["rs"], scale=-0.5)

        def prenorm_g(pas, c0, c1, l, w, scratch, skey, hdst, hkey, hoff, delay=0):
            n = c1 - c0
            xk = ("xT", pas["stidx"][c0])
            act(sq[:, 0:4, 0:n], xT[:, 0:4, c0:c1], AF.Square, [xk], [("sq", 0)])
            tt(sq[:, 4:8, 0:n], xT[:, 4:8, c0:c1], xT[:, 4:8, c0:c1], ALU.mult, [xk], [("sq", 1)])
            yield
            for _ in range(delay):
                yield
            stats_rstd(xk, n, 1.0 / D)
            yield
            tt(scratch[:, :, 0:n], xT[:, :, c0:c1], rs[:, 0:n].unsqueeze(1).to_broadcast([128, KC, n]), ALU.mult,
               [xk, "rs"], [skey])
            yield
            msh = 0 if w == 0 else 3
            for (a, b_, s) in segs_of(pas, c0, c1):
                for kc in range(KC):
                    o_ = hdst[:, kc, hoff + a - c0: hoff + b_ - c0]
                    i_ = scratch[:, kc, a - c0:b_ - c0]
                    sc_ = Aall[:, l, w, kc, s:s + 1]
                    bi_ = modv[:, l, msh, kc, s:s + 1]
                    if kc % 2 == 0:
                        act(o_, i_, AF.Identity, [skey, ("A", l, w), ("mod", l, msh)], [hkey], scale=sc_, bias=bi_)
                    else:
                        dve(lambda e, o_=o_, i_=i_, sc_=sc_, bi_=bi_: e.tensor_scalar(
                            out=o_, in0=i_, scalar1=sc_, scalar2=bi_, op0=ALU.mult, op1=ALU.add),
                            [skey, ("A", l, w), ("mod", l, msh)], [hkey])
                    if kc % 2 == 1:
                        yield

        def prenorm(*a, **k):
            for _ in prenorm_g(*a, **k):
                pass

        def postnorm_g(pas, c0, c1, l, w, ybuf, ykey, yoff):
            n = c1 - c0
            xk = ("xT", pas["stidx"][c0])
            yv = ybuf[:, :, yoff:yoff + n]
            act(sq[:, :, 0:n], yv, AF.Square, [ykey], ["sq", ("sq", 0), ("sq", 1)])
            yield
            stats_rstd(ykey, n, 1.0 / D)
            yield
            tt(yv, yv, rs[:, 0:n].unsqueeze(1).to_broadcast([128, KC, n]), ALU.mult, [ykey, "rs"], [ykey])
            yield
            for (a, b_, s) in segs_of(pas, c0, c1):
                for kc in range(KC):
                    stt(xT[:, kc, a:b_], ybuf[:, kc, yoff + a - c0: yoff + b_ - c0], Gall[:, l, w, kc, s:s + 1],
                        xT[:, kc, a:b_], ALU.mult, ALU.add, [ykey, ("G", l, w), xk], [xk])
                    if kc % 2 == 1:
                        yield

        def postnorm(*a, **k):
            for _ in postnorm_g(*a, **k):
                pass

        def mk_pass(name, kind, src, col0, T, subs, segs):
            stidx = {}
            sts = []
            c = 0
            for i, n in enumerate(subs):
                sts.append((c, c + n))
                for cc in range(c, c + n):
                    stidx[cc] = i
                c += n
            assert c == T
            return dict(name=name, kind=kind, src=src, col0=col0, T=T, sts=sts, segs=segs, stidx=stidx)

        passes = [
            mk_pass("P1", "prefix", d_xp, 0, 1024, [384, 384, 256], [(0, 1024, 0)]),
            mk_pass("P2", "prefix", d_xp, 1024, 960, [384, 384, 192], [(0, 960, 0)]),
            mk_pass("A", "main", d_xT, 0, 1152, [384, 384, 384], [(0, 1152, 0)]),
            mk_pass("B", "main", d_xT, 1152, 1024, [384, 384, 256], [(0, 960, 0), (960, 1024, 1)]),
        ]

        def load_x(pas):
            for i, (c0, c1) in enumerate(pas["sts"]):
                dma("sp", xT[:, :, c0:c1], pas["src"][:, :, pas["col0"] + c0: pas["col0"] + c1], f"x{i}",
                    (), [("xT", i)])

        chunk_ctr = [0]

        def run(g):
            for _ in g:
                pass

        def chain(*gens):
            for g in gens:
                if g is not None:
                    yield from g

        def merge(*gens):
            gens = [g for g in gens if g is not None]
            while gens:
                for g in list(gens):
                    try:
                        next(g)
                    except StopIteration:
                        gens.remove(g)

        def idle(k):
            for _ in range(k):
                yield

        def merge_g(*gens):
            gens = [g for g in gens if g is not None]
            while gens:
                for g in list(gens):
                    try:
                        next(g)
                    except StopIteration:
                        gens.remove(g)
                yield

        def gla_phase(pas, extras=()):
            extras = list(extras)
            main = pas["kind"] == "main"
            l = 0
            mark0 = arena["off"]
            NP = 1 if main else 2

            def pb(name, shape, dt):
                return sb(f"{name}_{pas['name']}", shape, dt)
            NH = 1 if main else 2
            hTs = [pb(f"g_hT{p_}", [128, KC, SUB], BF16) for p_ in range(NH)]
            yst = pb("g_y", [128, KC, SUB], F32)
            lrT = pb("g_lrT", [17, SUB], BF16)
            l_sbs = [pb(f"g_l{p_}", [128, NTL, 512], BF16) for p_ in range(NP)]
            kends = [pb(f"g_kend{p_}", [128, NTL, 512], BF16) for p_ in range(NP)]
            vtoks = [pb(f"g_vtok{p_}", [128, NTL, 1024], BF16) for p_ in range(NP)]
            ebs = [pb(f"g_eb{p_}", [128, H, SUB], F32) for p_ in range(NP)]
            if main:
                einv = pb("g_einv", [128, H, SUB], F32)
                qdec = pb("g_qdec", [128, H, SUB], BF16)
                kinv = pb("g_kinv", [128, H, SUB], BF16)
                rsil = pb("g_rsil", [128, KC, SUB], BF16)
                sT = pb("g_sT", [128, H, 128], BF16)
                osq = pb("g_osq", [128, KC, 128], BF16)
                rsh = pb("g_rsh", [128, H, 128], F32)
                otmp = pb("g_otmp", [128, KC, 128], F32)

            P.op("dve", lambda e: e.memset(lrT[:], 1.0), (), ["lrT"])
            otile_ctr = [0]
            if main:
                bank_pool[0] = [4, 5, 6, 7]

            full = lambda r: r[:, 0:8192].rearrange("p (k n) -> p k n", k=8)
            if main:
                sQK = load_slab([(full, kcp(d_win)[:, :, 0:1024])])
            else:
                sQK = load_slab([(lambda r: full(r)[:, :, 512:1024], kcp(d_win)[:, :, 512:1024])])
            sV = load_slab([(full, kcp(d_win)[:, :, 1024:2048])])
            if main:
                sR = load_slab([(full, kcp(d_win)[:, :, 2048:3072])])
                sO = load_slab([(full, kcp(d_wo))])
                wR = slot_kn(sR, 8, 1024)
                wO = slot_kn(sO, 8, 1024)
            wQK = slot_kn(sQK, 8, 1024)
            wV = slot_kn(sV, 8, 1024)
            sts = pas["sts"]

            def tiles_of(n):
                out = []
                t0 = 0
                while t0 < n:
                    out.append((t0, min(128, n - t0)))
                    t0 += 128
                return out

            def S1(si):
                c0, c1 = sts[si]
                n = c1 - c0
                yield from prenorm_g(pas, c0, c1, l, 0, yst, "g_y", hTs[si % NH], ("g_hT", si % NH), 0,
                                     delay=2 if main else 0)

            def S2(si):
                c0, c1 = sts[si]
                n = c1 - c0
                hT = hTs[si % NH]
                hk = ("g_hT", si % NH)
                p_ = si % NP
                l_sb, kend, vtok, eb = l_sbs[p_], kends[p_], vtoks[p_], ebs[p_]
                b = bank()
                for kc in range(KC):
                    mm(psb[b][0:16, 0:n], wga[:, kc, :], hT[:, kc, 0:n], kc == 0, kc == KC - 1,
                       ["wga", hk], [("ps", b)])
                act(lrT[0:16, 0:n], psb[b][0:16, 0:n], AF.Copy, [("ps", b)], ["lrT"])
                yield
                for i, (t0, Pn) in enumerate(tiles_of(n)):
                    b = bank()
                    mm(psb[b][0:Pn, :], lrT[0:17, t0:t0 + Pn], wgb[0:17, :], True, True, ["lrT", "wgb"], [("ps", b)])
                    act(tA[0:Pn, :], psb[b][0:Pn, :], AF.Exp, [("ps", b)], ["tA"], scale=-1.0)
                    act(l_sb[0:Pn, i, :], tA[0:Pn, :], AF.Ln, ["tA", "kst0"], [("l", p_, i)], bias=ONE[0:Pn, :])
                    yield
                    for hf in range(2):
                        b = bank()
                        for kc in range(KC):
                            mm(psb[b][0:Pn, :], hT[:, kc, t0:t0 + Pn], wV[:, kc, hf * 512:(hf + 1) * 512],
                               kc == 0, kc == KC - 1, [hk, ("ring", sV)], [("ps", b)])
                        act(vtok[0:Pn, i, hf * 512:(hf + 1) * 512], psb[b][0:Pn, :], AF.Copy, [("ps", b)], [("vtok", p_, i)])
                    yield
                    b = bank()
                    mm(psb[b][0:Pn, :], TU[0:Pn, 0:Pn], l_sb[0:Pn, i, :], True, True, ["cstb", ("l", p_, i)], [("ps", b)])
                    act(tB[0:Pn, :], psb[b][0:Pn, :], AF.Exp, [("ps", b)], ["tB"])
                    b = bank()
                    for kc in range(KC):
                        mm(psb[b][0:Pn, :], hT[:, kc, t0:t0 + Pn], wQK[:, kc, 512:1024], kc == 0, kc == KC - 1,
                           [hk, ("ring", sQK)], [("ps", b)])
                    tt(kend[0:Pn, i, :], psb[b][0:Pn, :], tB[0:Pn, :], ALU.mult, [("ps", b), "tB"], [("kend", p_, i)])
                    yield
                    b = bank()
                    for hd in range(H):
                        mm(psb[b][:, hd * 128: hd * 128 + Pn], l_sb[0:Pn, i, hd * 128:(hd + 1) * 128], TL[0:Pn, 0:Pn],
                           True, True, [("l", p_, i), "cstb"], [("ps", b)])
                    pv = psb[b][:, :].rearrange("p (h t) -> p h t", h=H)[:, :, 0:Pn]
                    act(eb[:, :, t0:t0 + Pn], pv, AF.Exp, [("ps", b)], [("eb", p_)])
                    if main:
                        act(einv[:, :, t0:t0 + Pn], pv, AF.Exp, [("ps", b)], ["einv"], scale=-1.0)
                    yield

            def S3(si):
                c0, c1 = sts[si]
                n = c1 - c0
                hT = hTs[si % NH]
                hk = ("g_hT", si % NH)
                eb = ebs[0]
                for hd in range(H):
                    b = bank()
                    for kc in range(KC):
                        mm(psb[b][:, 0:n], wQK[:, kc, hd * 128:(hd + 1) * 128], hT[:, kc, 0:n], kc == 0, kc == KC - 1,
                           [("ring", sQK), hk], [("ps", b)])
                    stt(qdec[:, hd, 0:n], psb[b][:, 0:n], float(DK) ** -0.5, eb[:, hd, 0:n], ALU.mult, ALU.mult,
                        [("ps", b), ("eb", 0)], ["qdec"])
                    yield
                    b = bank()
                    for kc in range(KC):
                        mm(psb[b][:, 0:n], wQK[:, kc, 512 + hd * 128: 512 + (hd + 1) * 128], hT[:, kc, 0:n],
                           kc == 0, kc == KC - 1, [("ring", sQK), hk], [("ps", b)])
                    tt(kinv[:, hd, 0:n], psb[b][:, 0:n], einv[:, hd, 0:n], ALU.mult, [("ps", b), "einv"], ["kinv"])
                    yield
                for c in range(KC):
                    b = bank()
                    for kc in range(KC):
                        mm(psb[b][:, 0:n], wR[:, kc, c * 128:(c + 1) * 128], hT[:, kc, 0:n], kc == 0, kc == KC - 1,
                           [("ring", sR), hk], [("ps", b)])
                    act(rsil[:, c, 0:n], psb[b][:, 0:n], AF.Silu, [("ps", b)], ["rsil"])
                    yield

            def S4(si):
                c0, c1 = sts[si]
                tl = tiles_of(c1 - c0)
                obs = {}
                gens = []
                for i in range(len(tl)):
                    if i == 0:
                        gens.append(T_a(si, i, obs))
                    else:
                        gens.append(merge_g(T_b(si, i - 1, obs), T_a(si, i, obs)))
                gens.append(T_b(si, len(tl) - 1, obs))
                yield from chain(*gens)

            def T_a(si, i, obs):
                c0, c1 = sts[si]
                n = c1 - c0
                p_ = si % NP
                l_sb, kend, vtok, eb = l_sbs[p_], kends[p_], vtoks[p_], ebs[p_]
                t0, Pn = tiles_of(n)[i]
                if True:
                    nch = Pn // 64
                    chunks = []
                    for ch in range(nch):
                        col = c0 + t0 + ch * 64
                        chunks.append([s for (a, b_, s) in pas["segs"] if a <= col < b_][0])
                    if main:
                        b = bank()
                        for hd in range(H):
                            mm(psb[b][0:Pn, hd * 128: hd * 128 + Pn], kinv[:, hd, t0:t0 + Pn], qdec[:, hd, t0:t0 + Pn],
                               True, True, ["kinv", "qdec"], [("ps", b)])
                        tt(sT[0:Pn, :, 0:Pn], psb[b][0:Pn, :].rearrange("p (h t) -> p h t", h=H)[:, :, 0:Pn],
                           MASK[0:Pn, 0:Pn].unsqueeze(1).to_broadcast([Pn, H, Pn]), ALU.mult,
                           [("ps", b), "cstb"], ["sT"])
                        yield
                    if main:
                        tp_ = otile_ctr[0] % 2
                        otile_ctr[0] += 1
                        ob = (2 * tp_, 2 * tp_ + 1)
                    kvb = []

                    def kv_mm(ch):
                        r0 = ch * 64
                        bb = (bank(), bank())
                        for hd in range(H):
                            bk = bb[hd // 2]
                            mm(psb[bk][:, (hd % 2) * 256:(hd % 2) * 256 + 256],
                               kend[r0:r0 + 64, i, hd * 128:(hd + 1) * 128], vtok[r0:r0 + 64, i, hd * 256:(hd + 1) * 256],
                               True, True, [("kend", p_, i), ("vtok", p_, i)], [("ps", bk)])
                        kvb.append(bb)

                    def state_of(ch):
                        if chunks[ch] == 0:
                            return S_m, "m"
                        return S_s, "s"

                    def update(ch):
                        Sst, Skey = state_of(ch)
                        lastc = t0 + ch * 64 + 63
                        for hd in range(H):
                            bk = kvb[ch][hd // 2]
                            stt(Sst[:, hd, :], Sst[:, hd, :], eb[:, hd, lastc:lastc + 1],
                                psb[bk][:, (hd % 2) * 256:(hd % 2) * 256 + 256], ALU.mult, ALU.add,
                                [("S", Skey, hd), ("eb", p_), ("ps", bk)], [("S", Skey, hd)])
                        if chunks[ch] == 0:
                            chunk_ctr[0] += 1
                            npar = chunk_ctr[0] % 2
                            if main:
                                act(S_bf[npar][:], S_m[:], AF.Copy, [("S", "m", h_) for h_ in range(H)], [("Sbf", npar)])

                    def outputs(ch, sbf, sbk):
                        r0 = ch * 64
                        for idx in range(8):
                            hd, ec = idx // 2, idx % 2
                            bk = ob[idx // 4]
                            oc = (idx % 4) * 128 + ch * 64
                            mm(psb[bk][:, oc:oc + 64],
                               vtok[r0:r0 + 64, i, hd * 256 + ec * 128: hd * 256 + ec * 128 + 128],
                               sT[r0:r0 + 64, hd, r0:r0 + 64], True, False, [("vtok", p_, i), "sT"], [("ps", bk)])
                            mm(psb[bk][:, oc:oc + 64], sbf[:, hd, ec * 128:(ec + 1) * 128],
                               qdec[:, hd, t0 + ch * 64: t0 + ch * 64 + 64], False, True,
                               [sbk, "qdec"], [("ps", bk)])

                    srcs = []
                    cc_ = chunk_ctr[0]
                    for ch in range(nch):
                        if chunks[ch] == 0:
                            srcs.append((S_bf[cc_ % 2], ("Sbf", cc_ % 2)))
                            cc_ += 1
                        else:
                            srcs.append((S_bfs, ("Sbf", "s")))
                    kv_mm(0)
                    yield
                    if nch == 1:
                        if main:
                            outputs(0, *srcs[0])
                            yield
                        update(0)
                        yield
                    else:
                        update(0)
                        kv_mm(1)
                        yield
                        if main:
                            outputs(0, *srcs[0])
                            yield
                        update(1)
                        yield
                        if main:
                            outputs(1, *srcs[1])
                            yield
                    if main:
                        obs[i] = ob

            def T_b(si, i, obs):
                c0, c1 = sts[si]
                n = c1 - c0
                t0, Pn = tiles_of(n)[i]
                if True:
                    if main:
                        ob = obs[i]
                        o3 = [psb[ob[k]][:, :].rearrange("p (a t) -> p a t", a=4)[:, :, 0:Pn] for k in range(2)]
                        for k in range(2):
                            act(osq[:, k * 4:(k + 1) * 4, 0:Pn], o3[k], AF.Square, [("ps", ob[k])], ["osq"])
                        yield
                        yield
                        b = bank()
                        for hd in range(H):
                            for ec in range(2):
                                mm(psb[b][:, hd * 128: hd * 128 + Pn], ones[:], osq[:, hd * 2 + ec, 0:Pn], ec == 0, ec == 1,
                                   ["ones", "osq"], [("ps", b)])
                        yield
                        pv = psb[b][:, :].rearrange("p (h t) -> p h t", h=H)[:, :, 0:Pn]
                        act(rsh[:, :, 0:Pn], pv, AF.Ln, [("ps", b), "kst"], ["rsh"], scale=1.0 / DV, bias=EPSC)
                        act(rsh[:, :, 0:Pn], rsh[:, :, 0:Pn], AF.Exp, ["rsh"], ["rsh"], scale=-0.5)
                        for k in range(2):
                            tt(otmp[:, k * 4:(k + 1) * 4, 0:Pn].rearrange("p (h e) t -> p h e t", e=2),
                               o3[k].rearrange("p (h e) t -> p h e t", e=2),
                               rsh[:, k * 2:(k + 1) * 2, 0:Pn].unsqueeze(2).to_broadcast([128, 2, 2, Pn]), ALU.mult,
                               [("ps", ob[k]), "rsh"], ["otmp"])
                        yield
                        for ec in range(2):
                            ov = otmp[:, :, 0:Pn].rearrange("p (h e) t -> p h e t", e=2)[:, :, ec, :]
                            rv = rsil[:, :, t0:t0 + Pn].rearrange("p (h e) t -> p h e t", e=2)[:, :, ec, :]
                            stt(rv, ov, vecs[:, V_GN + ec:V_GN + ec + 1], rv, ALU.mult, ALU.mult,
                                ["otmp", "vecs", "rsil"], ["rsil"])
                        yield

            def S5a(si):
                c0, c1 = sts[si]
                n = c1 - c0
                for c in range(KC):
                    b = bank()
                    for kc in range(KC):
                        mm(psb[b][:, 0:n], wO[:, kc, c * 128:(c + 1) * 128], rsil[:, kc, 0:n], kc == 0, kc == KC - 1,
                           [("ring", sO), "rsil"], [("ps", b)])
                    act(yst[:, c, 0:n], psb[b][:, 0:n], AF.Copy, [("ps", b)], ["g_y"])
                    yield

            def S5b(si):
                c0, c1 = sts[si]
                yield from postnorm_g(pas, c0, c1, l, 0, yst, "g_y", 0)

            nst = len(sts)
            if main:
                run(S1(0))
                run(S2(0))
                run(S3(0))
                for si in range(nst):
                    nxt = si + 1 < nst
                    merge(S4(si), S1(si + 1) if nxt else None)
                    run(S5a(si))
                    merge(S5b(si), chain(S2(si + 1), S3(si + 1)) if nxt else None)
            else:
                run(S1(0))
                if nst > 1:
                    run(S1(1))
                run(S2(0))
                for si in range(nst):
                    merge(S4(si), chain(S1(si + 2) if si + 2 < nst else None, S2(si + 1) if si + 1 < nst else None))
                    for _ in range(2):
                        if extras:
                            l_, m_ = extras.pop(0)
                            mod_slab(l_, m_)
                while extras:
                    l_, m_ = extras.pop(0)
                    mod_slab(l_, m_)
            P.barrier()
            bank_pool[0] = list(range(8))
            arena["off"] = mark0

        def ffn_phase(pas, l):
            T = pas["T"]
            mark0 = arena["off"]
            if True:
                def pb(name, shape, dt):
                    return sb(f"{name}_{pas['name']}{l}", shape, dt)
                hT = pb("f_hT", [128, KC, TMAX], BF16)
                yf = pb("f_y", [128, KC, TMAX], F32)
                hid = [pb(f"f_hid{i}", [128, 4, TMAX], BF16) for i in range(2)]
                for si, (c0, c1) in enumerate(pas["sts"]):
                    prenorm(pas, c0, c1, l, 1, yf[:, :, c0:c1], ("f_y", si), hT, ("f_hT", si), c0)

                def load_g(g):
                    return load_slab([
                        (lambda r: r[:, 0:4096].rearrange("p (k n) -> p k n", k=8), kcp(d_wup[l])[:, :, g * 512:(g + 1) * 512]),
                        (lambda r: r[:, 4096:8192].rearrange("p (k n) -> p k n", k=4), kcp(d_wdn[l])[:, g * 4:(g + 1) * 4, :]),
                    ])

                def up(g, s):
                    U = ring[s][:, 0:4096].rearrange("p (k n) -> p k n", k=8)
                    hb = hid[g % 2]
                    for si, (c0, c1) in enumerate(pas["sts"]):
                        n = c1 - c0
                        for j in range(4):
                            b = bank()
                            for kc in range(KC):
                                mm(psb[b][:, 0:n], U[:, kc, j * 128:(j + 1) * 128], hT[:, kc, c0:c1], kc == 0, kc == KC - 1,
                                   [("ring", s), ("f_hT", si)], [("ps", b)])
                            tmp = tA if j % 2 == 0 else tB
                            tk = "tA" if j % 2 == 0 else "tB"
                            act(tmp[:, 0:n], psb[b][:, 0:n], AF.Relu, [("ps", b)], [tk])
                            tt(hb[:, j, c0:c1], tmp[:, 0:n], tmp[:, 0:n], ALU.mult, [tk], [("hid", g % 2, si)])

                def down(g, s):
                    Dn = ring[s][:, 4096:8192].rearrange("p (k n) -> p k n", k=4)
                    hb = hid[g % 2]
                    for si, (c0, c1) in enumerate(pas["sts"]):
                        n = c1 - c0
                        for c in range(KC):
                            b = bank()
                            for j in range(4):
                                mm(psb[b][:, 0:n], Dn[:, j, c * 128:(c + 1) * 128], hb[:, j, c0:c1], j == 0, j == 3,
                                   [("ring", s), ("hid", g % 2, si)], [("ps", b)])
                            if g == 0:
                                act(yf[:, c, c0:c1], psb[b][:, 0:n], AF.Copy, [("ps", b)], [("f_y", si)])
                            else:
                                tt(yf[:, c, c0:c1], yf[:, c, c0:c1], psb[b][:, 0:n], ALU.add, [("f_y", si), ("ps", b)],
                                   [("f_y", si)])

                slots = {}
                slots[0] = load_g(0)
                up(0, slots[0])
                for g in range(8):
                    if g + 1 < 8:
                        slots[g + 1] = load_g(g + 1)
                        up(g + 1, slots[g + 1])
                    down(g, slots[g])
                for si, (c0, c1) in enumerate(pas["sts"]):
                    postnorm(pas, c0, c1, l, 1, yf, ("f_y", si), c0)
                P.barrier()
            arena["off"] = mark0

        def conv_phase(pas):
            l = 1
            T = pas["T"]
            isB = pas["name"] == "B"
            mark0 = arena["off"]
            if True:
                def pb(name, shape, dt):
                    return sb(f"{name}_{pas['name']}", shape, dt)
                UW = 30 + TMAX + 30
                uT = pb("c_uT", [128, KC, UW], BF16)
                yst = pb("c_y", [128, KC, SUB], F32)
                diags = [pb(f"c_diag{i_}", [128, 31, 128], BF16) for i_ in range(2)]
                mu = pb("c_mu", [128, SUB], F32)
                accs = [pb(f"c_acc{i_}", [128, SUB], F32) for i_ in range(2)]
                actr = [0]

                def ucol(c):
                    return 30 + c if (not isB or c < 960) else 60 + c

                if not isB:
                    P.op("dve", lambda e: e.memset(uT[:, :, 0:30], 0.0), (), ["uT"])
                else:
                    dve(lambda e: e.tensor_copy(out=uT[:, :, 0:30], in_=uctx[:]), ["uctx"], ["uT"])
                    dve(lambda e: e.tensor_copy(out=uT[:, :, 990:1020], in_=cctx[:]), ["cctx"], ["uT"])

                mark1 = arena["off"]
                sts = pas["sts"]
                nst = len(sts)
                hT = sb(f"c_hT_{pas['name']}", [128, KC, TMAX], BF16)
                slabs = []
                for hf in range(2):
                    slabs.append(load_slab([
                        (lambda r: r[:, 0:8192].rearrange("p (k n) -> p k n", k=8)[:, :, 0:512],
                         kcp(d_cwin)[:, :, hf * 512:(hf + 1) * 512]),
                        (lambda r: r[:, 0:8192].rearrange("p (k n) -> p k n", k=8)[:, :, 512:1024],
                         kcp(d_cwin)[:, :, 1024 + hf * 512: 1024 + (hf + 1) * 512]),
                    ]))

                def C1(si):
                    c0, c1 = sts[si]
                    yield from prenorm_g(pas, c0, c1, l, 0, yst, "c_y", hT, ("c_hT", si), c0)

                def C2(si):
                    c0, c1 = sts[si]
                    n = c1 - c0
                    for hf in range(2):
                        s = slabs[hf]
                        W = slot_kn(s, 8, 1024)
                        for cc in range(4):
                            fc = hf * 4 + cc
                            ba = bank()
                            for kc in range(KC):
                                mm(psb[ba][:, 0:n], W[:, kc, cc * 128:(cc + 1) * 128], hT[:, kc, c0:c1], kc == 0, kc == KC - 1,
                                   [("ring", s), ("c_hT", si)], [("ps", ba)])
                            bg = bank()
                            for kc in range(KC):
                                mm(psb[bg][:, 0:n], W[:, kc, 512 + cc * 128: 512 + (cc + 1) * 128], hT[:, kc, c0:c1],
                                   kc == 0, kc == KC - 1, [("ring", s), ("c_hT", si)], [("ps", bg)])
                            tg = tA if fc % 2 == 0 else tB
                            tgk = "tA" if fc % 2 == 0 else "tB"
                            act(tg[:, 0:n], psb[bg][:, 0:n], AF.Sigmoid, [("ps", bg), "vecs"], [tgk],
                                bias=vecs[:, V_CBIN + 8 + fc: V_CBIN + 9 + fc])
                            bav = vecs[:, V_CBIN + fc: V_CBIN + fc + 1]
                            for (a, b_, sq_) in segs_of(pas, c0, c1):
                                stt(uT[:, fc, ucol(a): ucol(a) + (b_ - a)], psb[ba][:, a - c0:b_ - c0], bav,
                                    tg[:, a - c0:b_ - c0], ALU.add, ALU.mult, [("ps", ba), "vecs", tgk], ["uT"])
                            if isB:
                                if c0 <= 930 and c1 >= 960:
                                    stt(ulast_m[:, fc, :], psb[ba][:, 930 - c0:960 - c0], bav, tg[:, 930 - c0:960 - c0],
                                        ALU.add, ALU.mult, [("ps", ba), "vecs", tgk], ["ulast_m"])
                                if c0 <= 994 and c1 >= 1024:
                                    stt(ulast_s[:, fc, :], psb[ba][:, 994 - c0:1024 - c0], bav, tg[:, 994 - c0:1024 - c0],
                                        ALU.add, ALU.mult, [("ps", ba), "vecs", tgk], ["ulast_s"])
                            yield

                run(C1(0))
                for si in range(nst):
                    merge(C2(si), C1(si + 1) if si + 1 < nst else None)
                P.barrier()
                arena["off"] = mark1
                if not isB:
                    dve(lambda e: e.tensor_copy(out=uctx[:], in_=uT[:, :, 30 + T - 30: 30 + T]), ["uT"], ["uctx"])
                zT = sb(f"c_zT_{pas['name']}", [128, KC, TMAX], BF16)
                csegs_st = [[] for _ in sts]
                for (a, b_, sq_) in pas["segs"]:
                    c = a
                    while c < b_:
                        si = pas["stidx"][c]
                        nn = min(SUB, min(b_, sts[si][1]) - c)
                        csegs_st[si].append((c, nn, ucol(c) - 30))
                        c += nn
                sO = load_slab([(lambda r: r[:, 0:8192].rearrange("p (k n) -> p k n", k=8), kcp(d_cwo))])
                wO = slot_kn(sO, 8, 1024)
                dctr = [0]

                def C3(si):
                    for fc in range(KC):
                        wd = vecs[:, V_WDW + fc * 31: V_WDW + fc * 31 + 31]
                        dp = dctr[0] % 2
                        dctr[0] += 1
                        diag = diags[dp]
                        dgk = ("diag", dp)
                        tt(diag[:], IDb.unsqueeze(1).to_broadcast([128, 31, 128]), wd.unsqueeze(2).to_broadcast([128, 31, 128]),
                           ALU.mult, ["cstb", "vecs"], [dgk])
                        yield
                        for (c, nn, us) in csegs_st[si]:
                            if NPOOL > 0:
                                ap_ = actr[0] % 2
                                actr[0] += 1
                                acc = accs[ap_]
                                ak = ("acc", ap_)
                                P.op("pool", lambda e, acc=acc, fc=fc, us=us, nn=nn, wd=wd: e.tensor_scalar_mul(
                                    out=acc[:, 0:nn], in0=uT[:, fc, us: us + nn], scalar1=wd[:, 0:1]), ["uT", "vecs"], [ak])
                                for j in range(1, NPOOL):
                                    P.op("pool", lambda e, acc=acc, fc=fc, us=us, nn=nn, wd=wd, j=j: e.scalar_tensor_tensor(
                                        out=acc[:, 0:nn], in0=uT[:, fc, us + j: us + j + nn], scalar=wd[:, j:j + 1],
                                        in1=acc[:, 0:nn], op0=ALU.mult, op1=ALU.add), ["uT", "vecs", ak], [ak])
                            b = bank()
                            for j in range(NPOOL, 31):
                                mm(psb[b][:, 0:nn], diag[:, j, :], uT[:, fc, us + j: us + j + nn], j == NPOOL, j == 30,
                                   [dgk, "uT"], [("ps", b)])
                            if NPOOL > 0:
                                stt(zT[:, fc, c:c + nn], psb[b][:, 0:nn], vecs[:, V_CBDW + fc: V_CBDW + fc + 1], acc[:, 0:nn],
                                    ALU.add, ALU.add, [("ps", b), "vecs", ak], [("zT", si)])
                            else:
                                act(zT[:, fc, c:c + nn], psb[b][:, 0:nn], AF.Identity, [("ps", b), "vecs"], [("zT", si)],
                                    bias=vecs[:, V_CBDW + fc: V_CBDW + fc + 1])
                            yield

                def C4(si):
                    c0, c1 = sts[si]
                    n = c1 - c0
                    zk = ("zT", si)
                    act(sq[:, :, 0:n], zT[:, :, c0:c1], AF.Square, [zk], ["sq", ("sq", 0), ("sq", 1)])
                    yield
                    yield
                    bm = bank()
                    for kc in range(KC):
                        mm(psb[bm][:, 0:n], ones[:], zT[:, kc, c0:c1], kc == 0, kc == KC - 1, ["ones", zk], [("ps", bm)])
                    yield
                    b2 = bank()
                    for kc in range(KC):
                        mm(psb[b2][:, 0:n], ones[:], sq[:, kc, 0:n], kc == 0, kc == KC - 1, ["ones", "sq", ("sq", kc // 4)], [("ps", b2)])
                    dve(lambda e, n=n, bm=bm: e.tensor_scalar_mul(out=mu[:, 0:n], in0=psb[bm][:, 0:n], scalar1=1.0 / D),
                        [("ps", bm)], ["mu"])
                    tt(rs[:, 0:n], mu[:, 0:n], mu[:, 0:n], ALU.mult, ["mu"], ["rs"])
                    yield
                    stt(rs[:, 0:n], psb[b2][:, 0:n], 1.0 / D, rs[:, 0:n], ALU.mult, ALU.subtract, [("ps", b2), "rs"], ["rs"])
                    act(rs[:, 0:n], rs[:, 0:n], AF.Ln, ["rs", "kst"], ["rs"], bias=EPSC)
                    act(rs[:, 0:n], rs[:, 0:n], AF.Exp, ["rs"], ["rs"], scale=-0.5)
                    yield
                    tt(yst[:, :, 0:n], zT[:, :, c0:c1], mu[:, 0:n].unsqueeze(1).to_broadcast([128, KC, n]), ALU.subtract,
                       [zk, "mu"], ["c_y"])
                    yield
                    tt(yst[:, :, 0:n], yst[:, :, 0:n], rs[:, 0:n].unsqueeze(1).to_broadcast([128, KC, n]), ALU.mult,
                       ["c_y", "rs"], ["c_y"])
                    yield
                    for fc in range(KC):
                        act(zT[:, fc, c0:c1], yst[:, fc, 0:n], AF.Silu, ["c_y", "vecs"], [zk],
                            scale=vecs[:, V_LNG + fc: V_LNG + fc + 1], bias=vecs[:, V_LNB + fc: V_LNB + fc + 1])
                        if fc % 2 == 1:
                            yield

                def C5a(si):
                    c0, c1 = sts[si]
                    n = c1 - c0
                    for c in range(KC):
                        b = bank()
                        for kc in range(KC):
                            mm(psb[b][:, 0:n], wO[:, kc, c * 128:(c + 1) * 128], zT[:, kc, c0:c1], kc == 0, kc == KC - 1,
                               [("ring", sO), ("zT", si)], [("ps", b)])
                        act(yst[:, c, 0:n], psb[b][:, 0:n], AF.Identity, [("ps", b), "vecs"], ["c_y"],
                            bias=vecs[:, V_CBOUT + c: V_CBOUT + c + 1])
                        yield

                def C5b(si):
                    c0, c1 = sts[si]
                    yield from postnorm_g(pas, c0, c1, l, 0, yst, "c_y", 0)

                run(C3(0))
                for si in range(nst):
                    merge(C3(si + 1) if si + 1 < nst else None, chain(C4(si), idle(3), C5a(si), C5b(si)))
                P.barrier()
            arena["off"] = mark0

        def chk(tag):
            if _DEBUG_STOP == tag:
                raise _Stop()

        def whole():
            chk("init")
            mod_slab(0, 0)
            mod_slab(0, 1)
            gpre = vecs[:, vnorm(0, 0): vnorm(0, 0) + 8]
            for s in range(2):
                stt(Aall[:, 0, 0, :, s], modv[:, 0, 1, :, s], 1.0, gpre, ALU.add, ALU.mult, [("mod", 0, 1), "vecs"], [("A", 0, 0)])
            chk("mod0")
            later = [(0, 2), (0, 3), (0, 4), (0, 5), (1, 0), (1, 1), (1, 2), (1, 3), (1, 4), (1, 5)]
            for pi, pas in enumerate(passes):
                cur["pas"] = pas
                P.tag = pas["name"] + ".gla"
                load_x(pas)
                chk(pas["name"] + "load")
                if pas["kind"] == "prefix":
                    gla_phase(pas, later[0:6] if pi == 0 else later[6:10])
                else:
                    gla_phase(pas)
                chk(pas["name"] + "gla")
                if pas["kind"] == "prefix":
                    P.tag = pas["name"] + ".mod"
                    if pas["name"] == "P2":
                        dve(lambda e: e.tensor_scalar_mul(out=S_m[:], in0=S_m[:], scalar1=flag[:, 0:1]), [("S", "m", h_) for h_ in range(H)] + ["flag"], [("S", "m", h_) for h_ in range(H)])
                        par = chunk_ctr[0] % 2
                        act(S_bf[par][:], S_m[:], AF.Copy, [("S", "m", h_) for h_ in range(H)], [("Sbf", par)])
                        for s in range(2):
                            tt(Gall[:, 0, 0, :, s], modv[:, 0, 2, :, s], vecs[:, vnorm(1, 0): vnorm(1, 0) + 8], ALU.mult,
                               [("mod", 0, 2), "vecs"], [("G", 0, 0)])
                        mod_derive(0, 1)
                        mod_derive(1, 0)
                        mod_derive(1, 1)
                    chk(pas["name"])
                    continue
                P.tag = pas["name"] + ".ffn0"
                ffn_phase(pas, 0)
                chk(pas["name"] + "ffn0")
                P.tag = pas["name"] + ".conv"
                conv_phase(pas)
                chk(pas["name"] + "conv")
                P.tag = pas["name"] + ".ffn1"
                ffn_phase(pas, 1)
                chk(pas["name"] + "ffn1")
                for i, (c0, c1) in enumerate(pas["sts"]):
                    t = dma("sp", o_yT[:, :, pas["col0"] + c0: pas["col0"] + c1], xT[:, :, c0:c1], f"oy{i}", [("xT", i)], ())
                    P.out_tokens.append(t)
                chk(pas["name"])

        cur = {}
        try:
            whole()
        except _Stop:
            pas = cur.get("pas")
            if pas is not None and pas["kind"] == "main":
                for i, (c0, c1) in enumerate(pas["sts"]):
                    t = dma("sp", o_yT[:, :, pas["col0"] + c0: pas["col0"] + c1], xT[:, :, c0:c1], f"oy{i}", [("xT", i)], ())
                    P.out_tokens.append(t)
        P.out_tokens.append(dma("sp", o_sgm.rearrange("h d e -> d h e"), S_m[:], "o_sgm", [("S", "m", h_) for h_ in range(H)], ()))
        P.out_tokens.append(dma("sp", o_sgs.rearrange("h d e -> d h e"), S_s[:], "o_sgs", [("S", "s", h_) for h_ in range(H)], ()))
        P.out_tokens.append(dma("sp", o_cvm, ulast_m[:], "o_cvm", ["ulast_m"], ()))
        P.out_tokens.append(dma("sp", o_cvs, ulast_s[:], "o_cvs", ["ulast_s"], ()))
        fin = Op(None)
        seen = {}
        for t in P.out_tokens:
            seen[t[1]] = max(seen.get(t[1], 0), t[2])
        for k, v in seen.items():
            fin.waits.append(("d", k, v))
        P.ops["sp"].append(fin)

        global _LAST_P
        _LAST_P = P
        with nc.Block() as block:
            P.emit(nc, block, sem, dsems)
    return nc


_NC_CACHE = {}


def _fm(a2d):
    t = a2d.shape[0]
    return np.ascontiguousarray(a2d.reshape(t, KC, 128).transpose(2, 1, 0))


def _vec_cols(v):
    return v.reshape(-1, 128).T


def _consts():
    s = np.arange(128)[:, None]
    t = np.arange(128)[None, :]
    same = (s // 64) == (t // 64)
    TLm = np.where(same & (s <= t), -1.0 / 16.0, 0.0)
    TUm = np.where(same & (s > t), -1.0 / 16.0, 0.0)
    MK = np.where(same & (s <= t), 1.0, 0.0)
    ID = np.eye(128)
    return np.ascontiguousarray(np.concatenate([TLm, TUm, MK, ID], axis=1).astype(np.float32))


def kernel(x_prompt, x_sample, c_prompt, c_sample, state_gla, cache_conv, w_mod, b_mod,
           norm_mix_pre, norm_mix_post, norm_ffn_pre, norm_ffn_post, w_ffn_up, w_ffn_down,
           gla_w_in, gla_w_gate_a, gla_w_gate_b, gla_b_gate, gla_norm, gla_w_out,
           conv_w_in, conv_b_in, conv_w_dw, conv_b_dw, conv_ln_g, conv_ln_b, conv_w_out, conv_b_out):
    f = lambda a: np.ascontiguousarray(np.asarray(a, dtype=np.float32))
    x_prompt, x_sample, c_prompt, c_sample = f(x_prompt), f(x_sample), f(c_prompt), f(c_sample)
    state_gla, cache_conv = f(state_gla), f(cache_conv)
    if "nc" not in _NC_CACHE:
        _NC_CACHE["nc"] = build_nc()
    nc = _NC_CACHE["nc"]

    cols = []
    for arr in (norm_mix_pre, norm_mix_post, norm_ffn_pre, norm_ffn_post):
        for l in range(2):
            cols.append(_vec_cols(f(arr)[l]))
    for l in range(2):
        cols.append(_vec_cols(f(b_mod)[l]))
    cols.append(_vec_cols(f(gla_norm)[0]))
    cols.append(_vec_cols(f(conv_b_in)[0]))
    cols.append(_vec_cols(f(conv_b_dw)[0]))
    cols.append(_vec_cols(f(conv_ln_g)[0]))
    cols.append(_vec_cols(f(conv_ln_b)[0]))
    cols.append(_vec_cols(f(conv_b_out)[0]))
    wdw = f(conv_w_dw)[0]
    cols.append(np.ascontiguousarray(wdw.reshape(31, KC, 128).transpose(2, 1, 0)).reshape(128, KC * 31))
    vecs = np.ascontiguousarray(np.concatenate(cols, axis=1).astype(np.float32))
    assert vecs.shape == (128, NV), vecs.shape
    consts = _consts()
    wgb_aug = np.ascontiguousarray(np.concatenate([f(gla_w_gate_b)[0], f(gla_b_gate)[0][None, :]], axis=0))

    shared = {
        "vecs": vecs, "consts": consts,
        "w_mod": f(w_mod), "w_ffn_up": f(w_ffn_up), "w_ffn_down": f(w_ffn_down),
        "gla_w_in": f(gla_w_in)[0], "gla_w_ga": f(gla_w_gate_a)[0], "gla_w_gb": wgb_aug,
        "gla_w_out": f(gla_w_out)[0], "conv_w_in": f(conv_w_in)[0], "conv_w_out": f(conv_w_out)[0],
    }
    in_maps = []
    for c in range(8):
        b = c // 2
        odd = c % 2
        t0 = 0 if not odd else 4096 - NMAIN
        xm = x_prompt[b, t0:t0 + NMAIN]
        xall = np.concatenate([xm, x_sample[c]], axis=0)
        xp = x_prompt[b, 0:NPRE]
        c2 = np.stack([c_prompt[b], c_sample[c]], axis=0)
        m = dict(shared)
        m["xT"] = _fm(xall)
        m["xpT"] = _fm(xp)
        m["cT"] = np.ascontiguousarray(c2.reshape(2, KC, 128).transpose(2, 1, 0))
        m["sgla"] = np.ascontiguousarray(state_gla[0, c])
        m["cconvT"] = _fm(cache_conv[0, c])
        m["flag"] = np.full((128, 1), float(odd), dtype=np.float32)
        in_maps.append(m)

    res = run_bass_kernel_spmd(nc, in_maps, core_ids=list(range(8)))
    R = res.results

    def tm(a):
        return np.ascontiguousarray(a.transpose(2, 1, 0).reshape(a.shape[2], D))

    y_prompt = np.empty((4, 4096, D), np.float32)
    y_sample = np.empty((8, 64, D), np.float32)
    gla_p = np.empty((1, 4, H, DK, DV), np.float32)
    conv_p = np.empty((1, 4, 30, D), np.float32)
    gla_s = np.empty((1, 8, H, DK, DV), np.float32)
    conv_s = np.empty((1, 8, 30, D), np.float32)
    for c in range(8):
        b = c // 2
        odd = c % 2
        y = tm(np.asarray(R[c]["yT"]))
        if not odd:
            y_prompt[b, 0:2048] = y[0:2048]
        else:
            y_prompt[b, 2048:4096] = y[NMAIN - 2048:NMAIN]
            gla_p[0, b] = np.asarray(R[c]["sg_main"])
            conv_p[0, b] = tm(np.asarray(R[c]["cv_main"]))
        y_sample[c] = y[NMAIN:NMAIN + 64]
        gla_s[0, c] = np.asarray(R[c]["sg_smp"])
        conv_s[0, c] = tm(np.asarray(R[c]["cv_smp"]))
    return (y_prompt, y_sample, gla_p, conv_p, gla_s, conv_s)
```

```python
import numpy as np
import concourse.bass as bass
import concourse.mybir as mybir
from concourse.bass_utils import run_bass_kernel_spmd

F32 = mybir.dt.float32
BF16 = mybir.dt.bfloat16
AF = mybir.ActivationFunctionType
ALU = mybir.AluOpType

D = 1024
KC = 8
H = 4
DK = 128
DV = 256
FF = 4096
NMAIN = 2112
NPRE = 1984
NTOK = NMAIN + 64
EPS = 1e-6
SAME_ENGINE_SYNC = True
SYNC_ENGS = ("act", "dve", "pool")
FULL_SYNC_ENGS = ("dve", "pool")
SUB = 384
NTL = 3
NPOOL = 0
_DEBUG_STOP = None


class _Stop(Exception):
    pass

V_NORM = 0
V_BMOD = 64
V_GN = 160
V_CBIN = 162
V_CBDW = 178
V_LNG = 186
V_LNB = 194
V_CBOUT = 202
V_WDW = 210
NV = 210 + 248

C_TL = 0
C_TU = 128
C_MASK = 256
C_ID = 384


def vnorm(which, l):
    return V_NORM + (which * 2 + l) * 8


class Op:
    __slots__ = ("fn", "waits", "inc", "dma", "tag")

    def __init__(self, fn):
        self.fn = fn
        self.tag = None
        self.waits = []
        self.inc = False
        self.dma = None


class Prog:
    ENGS = ("pe", "act", "dve", "pool", "sp")

    def __init__(self):
        self.ops = {e: [] for e in self.ENGS}
        self.last_write = {}
        self.readers = {}
        self.waited = {e: {} for e in self.ENGS}
        self.dma_val = {}
        self.out_tokens = []
        self.tag = "init"

    def _need(self, eng, tok, deps, raw):
        if tok is None:
            return
        if tok[0] == "e":
            _, src, idx = tok
            if src == eng:
                if not (SAME_ENGINE_SYNC and eng in SYNC_ENGS and (raw or eng in FULL_SYNC_ENGS)):
                    return
            key = src
        else:
            _, src, idx = tok
            key = ("d", src)
        if self.waited[eng].get(key, -1) >= idx:
            return
        if deps.get(key, -1) < idx:
            deps[key] = idx

    def op(self, eng, fn, reads=(), writes=(), dma_sem=None):
        deps = {}
        for r in reads:
            self._need(eng, self.last_write.get(r), deps, True)
        for w in writes:
            self._need(eng, self.last_write.get(w), deps, False)
            for t in self.readers.get(w, ()):
                self._need(eng, t, deps, False)
        o = Op(fn)
        for key, idx in deps.items():
            self.waited[eng][key] = idx
            if isinstance(key, tuple):
                o.waits.append(("d", key[1], idx))
            else:
                self.ops[key][idx].inc = True
                o.waits.append(("e", key, idx))
        my_idx = len(self.ops[eng])
        o.tag = self.tag
        self.ops[eng].append(o)
        if dma_sem is not None:
            v = self.dma_val.get(dma_sem, 0) + 16
            self.dma_val[dma_sem] = v
            o.dma = dma_sem
            tok = ("d", dma_sem, v)
        else:
            tok = ("e", eng, my_idx)
        for r in reads:
            self.readers.setdefault(r, []).append(tok)
        for w in writes:
            self.last_write[w] = tok
            self.readers[w] = []
        return tok

    def barrier(self, engs=("pe", "act", "dve")):
        last = {}
        for s in engs:
            idx = len(self.ops[s]) - 1
            while idx >= 0 and self.ops[s][idx].fn is None:
                idx -= 1
            if idx >= 0:
                last[s] = idx
        for e in engs:
            deps = {}
            for s, idx in last.items():
                if s != e:
                    self._need(e, ("e", s, idx), deps, False)
            if deps:
                o = Op(None)
                for key, idx in deps.items():
                    self.waited[e][key] = idx
                    self.ops[key][idx].inc = True
                    o.waits.append(("e", key, idx))
                self.ops[e].append(o)

    def emit(self, nc, block, sems, dsems):
        cum = {}
        for e in self.ENGS:
            c = 0
            arr = []
            for o in self.ops[e]:
                if o.inc:
                    c += 1
                arr.append(c)
            cum[e] = arr
        prog = self

        def run(eng_name, e):
            for o in prog.ops[eng_name]:
                for w in o.waits:
                    if w[0] == "e":
                        e.wait_ge(sems[w[1]], cum[w[1]][w[2]])
                    else:
                        e.wait_ge(dsems[w[1]], w[2])
                if o.fn is None:
                    continue
                ins = o.fn(e)
                if o.dma is not None:
                    ins.then_inc(dsems[o.dma], 16)
                elif o.inc:
                    ins.then_inc(sems[eng_name], 1)

        @block.tensor
        def _(e):
            run("pe", e)

        @block.scalar
        def _(e):
            run("act", e)

        @block.vector
        def _(e):
            run("dve", e)

        @block.gpsimd
        def _(e):
            run("pool", e)

        @block.sync
        def _(e):
            run("sp", e)


def build_nc():
    nc = bass.Bass("TRN2", target_bir_lowering=False)

    def din(name, shape):
        return nc.dram_tensor(name, list(shape), F32, kind="ExternalInput").ap()

    def dout(name, shape):
        return nc.dram_tensor(name, list(shape), F32, kind="ExternalOutput").ap()

    d_xT = din("xT", [128, KC, NTOK])
    d_xp = din("xpT", [128, KC, NPRE])
    d_cT = din("cT", [128, KC, 2])
    d_sg = din("sgla", [H, DK, DV])
    d_cc = din("cconvT", [128, KC, 30])
    d_flag = din("flag", [128, 1])
    d_vecs = din("vecs", [128, NV])
    d_consts = din("consts", [128, 512])
    d_wmod = din("w_mod", [2, D, 6 * D])
    d_wup = din("w_ffn_up", [2, D, FF])
    d_wdn = din("w_ffn_down", [2, FF, D])
    d_win = din("gla_w_in", [D, 3072])
    d_wga = din("gla_w_ga", [D, 16])
    d_wgb = din("gla_w_gb", [17, 512])
    d_wo = din("gla_w_out", [D, D])
    d_cwin = din("conv_w_in", [D, 2 * D])
    d_cwo = din("conv_w_out", [D, D])

    o_yT = dout("yT", [128, KC, NTOK])
    o_sgm = dout("sg_main", [H, DK, DV])
    o_sgs = dout("sg_smp", [H, DK, DV])
    o_cvm = dout("cv_main", [128, KC, 30])
    o_cvs = dout("cv_smp", [128, KC, 30])

    P = Prog()
    TMAX = 1152

    def kcp(ap2d):
        return ap2d.rearrange("(kc p) n -> p kc n", p=128)

    from contextlib import ExitStack
    es = ExitStack()
    with es:
        arena = {"off": 16512}
        DTSZ = {F32: 4, BF16: 2}

        def sb(name, shape, dt):
            size = DTSZ[dt]
            for d_ in shape[1:]:
                size *= d_
            size = (size + 31) // 32 * 32
            off = arena["off"]
            assert off + size <= 229376, (name, off, size)
            arena["off"] = off + size
            arena["peak"] = max(arena.get("peak", 0), arena["off"])
            return nc.alloc_sbuf_tensor_at(name, list(shape), dt, offset=off)

        xT = sb("xT_sb", [128, KC, TMAX], F32)
        ring = [sb(f"ring{i}", [128, 8192], BF16) for i in range(4)]
        vecs = sb("vecs_sb", [128, NV], F32)
        cstb = sb("consts_bf", [128, 512], BF16)
        ones = sb("ones_bf", [128, 128], BF16)
        kst = sb("kst", [128, 4], F32)
        flag = sb("flag_sb", [128, 1], F32)
        cT = sb("cT_sb", [128, KC, 2], F32)
        ctmp = sb("ctmp", [128, KC, 2], F32)
        csil = sb("csil", [128, KC, 2], BF16)
        modv = sb("modv", [128, 2, 6, KC, 2], F32)
        Aall = sb("Aall", [128, 2, 2, KC, 2], F32)
        Gall = sb("Gall", [128, 2, 2, KC, 2], F32)
        nbg = sb("nbg", [128, 8], F32)
        wga = sb("wga", [128, KC, 16], BF16)
        wgb = sb("wgb", [17, 512], BF16)
        S_m = sb("S_m", [128, H, DV], F32)
        S_s = sb("S_s", [128, H, DV], F32)
        S_bf = [sb(f"S_bf{i}", [128, H, DV], BF16) for i in range(2)]
        S_bfs = sb("S_bfs", [128, H, DV], BF16)
        uctx = sb("uctx", [128, KC, 30], BF16)
        cctx = sb("cctx", [128, KC, 30], F32)
        ulast_m = sb("ulast_m", [128, KC, 30], F32)
        ulast_s = sb("ulast_s", [128, KC, 30], F32)
        rs = sb("rs", [128, SUB], F32)
        sq = sb("sq", [128, KC, SUB], BF16)
        tA = sb("tA", [128, 512], F32)
        tB = sb("tB", [128, 512], F32)

        mark_init = arena["off"]
        cst = sb("consts_sb", [128, 512], F32)
        psb = [es.enter_context(nc.psum_tensor(f"ps{i}", [128, 512], F32)) for i in range(8)]
        sem = {e: es.enter_context(nc.semaphore(f"s_{e}")) for e in Prog.ENGS}
        dsems = {}

        def dsem(name):
            if name not in dsems:
                dsems[name] = es.enter_context(nc.semaphore(f"d_{name}"))
            return name

        bank_ctr = [0]
        bank_pool = [list(range(8))]

        def bank():
            pool = bank_pool[0]
            for _ in range(len(pool)):
                b = pool[bank_ctr[0] % len(pool)]
                bank_ctr[0] += 1
                k = ("ps", b)
                if k in P.last_write and not P.readers.get(k):
                    continue
                return b
            raise RuntimeError("no free PSUM bank")

        TL = cstb[:, C_TL:C_TL + 128]
        TU = cstb[:, C_TU:C_TU + 128]
        IDb = cstb[:, C_ID:C_ID + 128]
        MASK = cstb[:, C_MASK:C_MASK + 128]
        ONE = kst[:, 0:1]
        EPSC = kst[:, 1:2]

        def mm(out, lhsT, rhs, start, stop, reads, writes):
            P.op("pe", lambda e, o=out, l=lhsT, r=rhs, s=start, t=stop: e.matmul(o, lhsT=l, rhs=r, start=s, stop=t),
                 reads=reads, writes=writes)

        def act(out, in_, func, reads, writes, scale=1.0, bias=None):
            if bias is None:
                P.op("act", lambda e: e.activation(out=out, in_=in_, func=func, scale=scale), reads=reads, writes=writes)
            else:
                P.op("act", lambda e: e.activation(out=out, in_=in_, func=func, scale=scale, bias=bias),
                     reads=reads, writes=writes)

        def dve(fn, reads, writes):
            P.op("dve", fn, reads=reads, writes=writes)

        def tt(out, in0, in1, op, reads, writes):
            dve(lambda e: e.tensor_tensor(out=out, in0=in0, in1=in1, op=op), reads, writes)

        def stt(out, in0, scalar, in1, op0, op1, reads, writes):
            dve(lambda e: e.scalar_tensor_tensor(out=out, in0=in0, scalar=scalar, in1=in1, op0=op0, op1=op1),
                reads, writes)

        def dma(eng, out, in_, semname, reads, writes):
            return P.op(eng, lambda e: e.dma_start(out=out, in_=in_), reads=reads, writes=writes, dma_sem=dsem(semname))

        slot_ctr = [0]

        def load_slab(parts):
            s = slot_ctr[0] % 4
            slot_ctr[0] += 1
            for (dstf, src) in parts:
                dma("pool", dstf(ring[s]), src, f"ring{s}", reads=(), writes=[("ring", s)])
            return s

        def slot_kn(s, kc, n, off=0):
            return ring[s][:, off:off + kc * n].rearrange("p (k n) -> p k n", k=kc)

        dma("sp", vecs[:], d_vecs, "vecs", (), ["vecs"])
        dma("sp", cst[:], d_consts, "consts", (), ["cst"])
        dma("sp", flag[:], d_flag, "flag", (), ["flag"])
        dma("sp", cT[:], d_cT, "cT", (), ["cT"])
        dma("sp", S_s[:], d_sg.rearrange("h d e -> d h e"), "S_s", (), [("S", "s", h_) for h_ in range(H)])
        dma("sp", cctx[:], d_cc, "cctx", (), ["cctx"])
        dma("pool", wga[:], kcp(d_wga), "wga", (), ["wga"])
        dma("pool", wgb[:], d_wgb, "wgb", (), ["wgb"])
        P.op("pool", lambda e: e.memset(ones[:], 1.0), (), ["ones"])
        P.op("pool", lambda e: e.memset(kst[:, 0:1], 1.0), (), ["kst0"])
        P.op("pool", lambda e: e.memset(kst[:, 1:2], EPS), (), ["kst"])
        P.op("pool", lambda e: e.memset(S_m[:], 0.0), (), [("S", "m", h_) for h_ in range(H)])
        P.op("pool", lambda e: e.memset(S_bf[0][:], 0.0), (), [("Sbf", 0)])
        dve(lambda e: e.tensor_copy(out=cstb[:], in_=cst[:]), ["cst"], ["cstb"])
        dve(lambda e: e.tensor_copy(out=S_bfs[:], in_=S_s[:]), [("S", "s", h_) for h_ in range(H)], [("Sbf", "s")])
        dve(lambda e: e.tensor_scalar_mul(out=nbg[:], in0=vecs[:, V_CBIN + 8:V_CBIN + 16], scalar1=-1.0), ["vecs"], ["nbg"])
        act(csil[:], cT[:], AF.Silu, ["cT"], ["csil"])
        P.barrier()
        arena["off"] = mark_init

        def mod_slab(l, m):
            s = load_slab([(lambda r: r[:, 0:8192].rearrange("p (k n) -> p k n", k=8),
                            kcp(d_wmod[l])[:, :, m * 1024:(m + 1) * 1024])])
            w = slot_kn(s, 8, 1024)
            b = bank()
            for j in range(8):
                for kc in range(KC):
                    mm(psb[b][:, j * 2:j * 2 + 2], w[:, kc, j * 128:(j + 1) * 128], csil[:, kc, :],
                       kc == 0, kc == KC - 1, [("ring", s), "csil"], [("ps", b)])
            bm = vecs[:, V_BMOD + (l * 6 + m) * 8: V_BMOD + (l * 6 + m) * 8 + 8]
            tt(modv[:, l, m, :, :], psb[b][:, 0:16].rearrange("p (k s) -> p k s", s=2),
               bm.unsqueeze(2).to_broadcast([128, 8, 2]), ALU.add, [("ps", b), "vecs"], [("mod", l, m)])

        def mod_derive(l, w):
            msc = 1 + 3 * w
            mgt = 2 + 3 * w
            gpre = vecs[:, vnorm(0 if w == 0 else 2, l): vnorm(0 if w == 0 else 2, l) + 8]
            gpost = vecs[:, vnorm(1 if w == 0 else 3, l): vnorm(1 if w == 0 else 3, l) + 8]
            for s in range(2):
                stt(Aall[:, l, w, :, s], modv[:, l, msc, :, s], 1.0, gpre, ALU.add, ALU.mult,
                    [("mod", l, msc), "vecs"], [("A", l, w)])
                tt(Gall[:, l, w, :, s], modv[:, l, mgt, :, s], gpost, ALU.mult,
                   [("mod", l, mgt), "vecs"], [("G", l, w)])

        def segs_of(pas, c0, c1):
            out = []
            for (a, b, s) in pas["segs"]:
                lo, hi = max(a, c0), min(b, c1)
                if lo < hi:
                    out.append((lo, hi, s))
            return out

        def stats_rstd(src_key, n, nfeat_scale):
            b = bank()
            for kc in range(KC):
                mm(psb[b][:, 0:n], ones[:], sq[:, kc, 0:n], kc == 0, kc == KC - 1, ["ones", "sq", ("sq", kc // 4)], [("ps", b)])
            act(rs[:, 0:n], psb[b][:, 0:n], AF.Ln, [("ps", b), "kst"], ["rs"], scale=nfeat_scale, bias=EPSC)
            act(rs[:, 0:n], rs[:, 0:n], AF.Exp, ["rs"], ["rs"], scale=-0.5)

        def prenorm_g(pas, c0, c1, l, w, scratch, skey, hdst, hkey, hoff, delay=0):
            n = c1 - c0
            xk = ("xT", pas["stidx"][c0])
            act(sq[:, 0:4, 0:n], xT[:, 0:4, c0:c1], AF.Square, [xk], [("sq", 0)])
            tt(sq[:, 4:8, 0:n], xT[:, 4:8, c0:c1], xT[:, 4:8, c0:c1], ALU.mult, [xk], [("sq", 1)])
            yield
            for _ in range(delay):
                yield
            stats_rstd(xk, n, 1.0 / D)
            yield
            tt(scratch[:, :, 0:n], xT[:, :, c0:c1], rs[:, 0:n].unsqueeze(1).to_broadcast([128, KC, n]), ALU.mult,
               [xk, "rs"], [skey])
            yield
            msh = 0 if w == 0 else 3
            for (a, b_, s) in segs_of(pas, c0, c1):
                for kc in range(KC):
                    o_ = hdst[:, kc, hoff + a - c0: hoff + b_ - c0]
                    i_ = scratch[:, kc, a - c0:b_ - c0]
                    sc_ = Aall[:, l, w, kc, s:s + 1]
                    bi_ = modv[:, l, msh, kc, s:s + 1]
                    if kc % 2 == 0:
                        act(o_, i_, AF.Identity, [skey, ("A", l, w), ("mod", l, msh)], [hkey], scale=sc_, bias=bi_)
                    else:
                        dve(lambda e, o_=o_, i_=i_, sc_=sc_, bi_=bi_: e.tensor_scalar(
                            out=o_, in0=i_, scalar1=sc_, scalar2=bi_, op0=ALU.mult, op1=ALU.add),
                            [skey, ("A", l, w), ("mod", l, msh)], [hkey])
                    if kc % 2 == 1:
                        yield

        def prenorm(*a, **k):
            for _ in prenorm_g(*a, **k):
                pass

        def postnorm_g(pas, c0, c1, l, w, ybuf, ykey, yoff):
            n = c1 - c0
            xk = ("xT", pas["stidx"][c0])
            yv = ybuf[:, :, yoff:yoff + n]
            act(sq[:, :, 0:n], yv, AF.Square, [ykey], ["sq", ("sq", 0), ("sq", 1)])
            yield
            stats_rstd(ykey, n, 1.0 / D)
            yield
            tt(yv, yv, rs[:, 0:n].unsqueeze(1).to_broadcast([128, KC, n]), ALU.mult, [ykey, "rs"], [ykey])
            yield
            for (a, b_, s) in segs_of(pas, c0, c1):
                for kc in range(KC):
                    stt(xT[:, kc, a:b_], ybuf[:, kc, yoff + a - c0: yoff + b_ - c0], Gall[:, l, w, kc, s:s + 1],
                        xT[:, kc, a:b_], ALU.mult, ALU.add, [ykey, ("G", l, w), xk], [xk])
                    if kc % 2 == 1:
                        yield

        def postnorm(*a, **k):
            for _ in postnorm_g(*a, **k):
                pass

        def mk_pass(name, kind, src, col0, T, subs, segs):
            stidx = {}
            sts = []
            c = 0
            for i, n in enumerate(subs):
                sts.append((c, c + n))
                for cc in range(c, c + n):
                    stidx[cc] = i
                c += n
            assert c == T
            return dict(name=name, kind=kind, src=src, col0=col0, T=T, sts=sts, segs=segs, stidx=stidx)

        passes = [
            mk_pass("P1", "prefix", d_xp, 0, 1024, [384, 384, 256], [(0, 1024, 0)]),
            mk_pass("P2", "prefix", d_xp, 1024, 960, [384, 384, 192], [(0, 960, 0)]),
            mk_pass("A", "main", d_xT, 0, 1152, [384, 384, 384], [(0, 1152, 0)]),
            mk_pass("B", "main", d_xT, 1152, 1024, [384, 384, 256], [(0, 960, 0), (960, 1024, 1)]),
        ]

        def load_x(pas):
            for i, (c0, c1) in enumerate(pas["sts"]):
                dma("sp", xT[:, :, c0:c1], pas["src"][:, :, pas["col0"] + c0: pas["col0"] + c1], f"x{i}",
                    (), [("xT", i)])

        chunk_ctr = [0]

        def run(g):
            for _ in g:
                pass

        def chain(*gens):
            for g in gens:
                if g is not None:
                    yield from g

        def merge(*gens):
            gens = [g for g in gens if g is not None]
            while gens:
                for g in list(gens):
                    try:
                        next(g)
                    except StopIteration:
                        gens.remove(g)

        def idle(k):
            for _ in range(k):
                yield

        def merge_g(*gens):
            gens = [g for g in gens if g is not None]
            while gens:
                for g in list(gens):
                    try:
                        next(g)
                    except StopIteration:
                        gens.remove(g)
                yield

        def gla_phase(pas, extras=()):
            extras = list(extras)
            main = pas["kind"] == "main"
            l = 0
            mark0 = arena["off"]
            NP = 1 if main else 2

            def pb(name, shape, dt):
                return sb(f"{name}_{pas['name']}", shape, dt)
            NH = 1 if main else 2
            hTs = [pb(f"g_hT{p_}", [128, KC, SUB], BF16) for p_ in range(NH)]
            yst = pb("g_y", [128, KC, SUB], F32)
            lrT = pb("g_lrT", [17, SUB], BF16)
            l_sbs = [pb(f"g_l{p_}", [128, NTL, 512], BF16) for p_ in range(NP)]
            kends = [pb(f"g_kend{p_}", [128, NTL, 512], BF16) for p_ in range(NP)]
            vtoks = [pb(f"g_vtok{p_}", [128, NTL, 1024], BF16) for p_ in range(NP)]
            ebs = [pb(f"g_eb{p_}", [128, H, SUB], F32) for p_ in range(NP)]
            if main:
                einv = pb("g_einv", [128, H, SUB], F32)
                qdec = pb("g_qdec", [128, H, SUB], BF16)
                kinv = pb("g_kinv", [128, H, SUB], BF16)
                rsil = pb("g_rsil", [128, KC, SUB], BF16)
                sT = pb("g_sT", [128, H, 128], BF16)
                osq = pb("g_osq", [128, KC, 128], BF16)
                rsh = pb("g_rsh", [128, H, 128], F32)
                otmp = pb("g_otmp", [128, KC, 128], F32)

            P.op("dve", lambda e: e.memset(lrT[:], 1.0), (), ["lrT"])
            otile_ctr = [0]
            if main:
                bank_pool[0] = [4, 5, 6, 7]

            full = lambda r: r[:, 0:8192].rearrange("p (k n) -> p k n", k=8)
            if main:
                sQK = load_slab([(full, kcp(d_win)[:, :, 0:1024])])
            else:
                sQK = load_slab([(lambda r: full(r)[:, :, 512:1024], kcp(d_win)[:, :, 512:1024])])
            sV = load_slab([(full, kcp(d_win)[:, :, 1024:2048])])
            if main:
                sR = load_slab([(full, kcp(d_win)[:, :, 2048:3072])])
                sO = load_slab([(full, kcp(d_wo))])
                wR = slot_kn(sR, 8, 1024)
                wO = slot_kn(sO, 8, 1024)
            wQK = slot_kn(sQK, 8, 1024)
            wV = slot_kn(sV, 8, 1024)
            sts = pas["sts"]

            def tiles_of(n):
                out = []
                t0 = 0
                while t0 < n:
                    out.append((t0, min(128, n - t0)))
                    t0 += 128
                return out

            def S1(si):
                c0, c1 = sts[si]
                n = c1 - c0
                yield from prenorm_g(pas, c0, c1, l, 0, yst, "g_y", hTs[si % NH], ("g_hT", si % NH), 0,
                                     delay=2 if main else 0)

            def S2(si):
                c0, c1 = sts[si]
                n = c1 - c0
                hT = hTs[si % NH]
                hk = ("g_hT", si % NH)
                p_ = si % NP
                l_sb, kend, vtok, eb = l_sbs[p_], kends[p_], vtoks[p_], ebs[p_]
                b = bank()
                for kc in range(KC):
                    mm(psb[b][0:16, 0:n], wga[:, kc, :], hT[:, kc, 0:n], kc == 0, kc == KC - 1,
                       ["wga", hk], [("ps", b)])
                act(lrT[0:16, 0:n], psb[b][0:16, 0:n], AF.Copy, [("ps", b)], ["lrT"])
                yield
                for i, (t0, Pn) in enumerate(tiles_of(n)):
                    b = bank()
                    mm(psb[b][0:Pn, :], lrT[0:17, t0:t0 + Pn], wgb[0:17, :], True, True, ["lrT", "wgb"], [("ps", b)])
                    act(tA[0:Pn, :], psb[b][0:Pn, :], AF.Exp, [("ps", b)], ["tA"], scale=-1.0)
                    act(l_sb[0:Pn, i, :], tA[0:Pn, :], AF.Ln, ["tA", "kst0"], [("l", p_, i)], bias=ONE[0:Pn, :])
                    yield
                    for hf in range(2):
                        b = bank()
                        for kc in range(KC):
                            mm(psb[b][0:Pn, :], hT[:, kc, t0:t0 + Pn], wV[:, kc, hf * 512:(hf + 1) * 512],
                               kc == 0, kc == KC - 1, [hk, ("ring", sV)], [("ps", b)])
                        act(vtok[0:Pn, i, hf * 512:(hf + 1) * 512], psb[b][0:Pn, :], AF.Copy, [("ps", b)], [("vtok", p_, i)])
                    yield
                    b = bank()
                    mm(psb[b][0:Pn, :], TU[0:Pn, 0:Pn], l_sb[0:Pn, i, :], True, True, ["cstb", ("l", p_, i)], [("ps", b)])
                    act(tB[0:Pn, :], psb[b][0:Pn, :], AF.Exp, [("ps", b)], ["tB"])
                    b = bank()
                    for kc in range(KC):
                        mm(psb[b][0:Pn, :], hT[:, kc, t0:t0 + Pn], wQK[:, kc, 512:1024], kc == 0, kc == KC - 1,
                           [hk, ("ring", sQK)], [("ps", b)])
                    tt(kend[0:Pn, i, :], psb[b][0:Pn, :], tB[0:Pn, :], ALU.mult, [("ps", b), "tB"], [("kend", p_, i)])
                    yield
                    b = bank()
                    for hd in range(H):
                        mm(psb[b][:, hd * 128: hd * 128 + Pn], l_sb[0:Pn, i, hd * 128:(hd + 1) * 128], TL[0:Pn, 0:Pn],
                           True, True, [("l", p_, i), "cstb"], [("ps", b)])
                    pv = psb[b][:, :].rearrange("p (h t) -> p h t", h=H)[:, :, 0:Pn]
                    act(eb[:, :, t0:t0 + Pn], pv, AF.Exp, [("ps", b)], [("eb", p_)])
                    if main:
                        act(einv[:, :, t0:t0 + Pn], pv, AF.Exp, [("ps", b)], ["einv"], scale=-1.0)
                    yield

            def S3(si):
                c0, c1 = sts[si]
                n = c1 - c0
                hT = hTs[si % NH]
                hk = ("g_hT", si % NH)
                eb = ebs[0]
                for hd in range(H):
                    b = bank()
                    for kc in range(KC):
                        mm(psb[b][:, 0:n], wQK[:, kc, hd * 128:(hd + 1) * 128], hT[:, kc, 0:n], kc == 0, kc == KC - 1,
                           [("ring", sQK), hk], [("ps", b)])
                    stt(qdec[:, hd, 0:n], psb[b][:, 0:n], float(DK) ** -0.5, eb[:, hd, 0:n], ALU.mult, ALU.mult,
                        [("ps", b), ("eb", 0)], ["qdec"])
                    yield
                    b = bank()
                    for kc in range(KC):
                        mm(psb[b][:, 0:n], wQK[:, kc, 512 + hd * 128: 512 + (hd + 1) * 128], hT[:, kc, 0:n],
                           kc == 0, kc == KC - 1, [("ring", sQK), hk], [("ps", b)])
                    tt(kinv[:, hd, 0:n], psb[b][:, 0:n], einv[:, hd, 0:n], ALU.mult, [("ps", b), "einv"], ["kinv"])
                    yield
                for c in range(KC):
                    b = bank()
                    for kc in range(KC):
                        mm(psb[b][:, 0:n], wR[:, kc, c * 128:(c + 1) * 128], hT[:, kc, 0:n], kc == 0, kc == KC - 1,
                           [("ring", sR), hk], [("ps", b)])
                    act(rsil[:, c, 0:n], psb[b][:, 0:n], AF.Silu, [("ps", b)], ["rsil"])
                    yield

            def S4(si):
                c0, c1 = sts[si]
                tl = tiles_of(c1 - c0)
                obs = {}
                gens = []
                for i in range(len(tl)):
                    if i == 0:
                        gens.append(T_a(si, i, obs))
                    else:
                        gens.append(merge_g(T_b(si, i - 1, obs), T_a(si, i, obs)))
                gens.append(T_b(si, len(tl) - 1, obs))
                yield from chain(*gens)

            def T_a(si, i, obs):
                c0, c1 = sts[si]
                n = c1 - c0
                p_ = si % NP
                l_sb, kend, vtok, eb = l_sbs[p_], kends[p_], vtoks[p_], ebs[p_]
                t0, Pn = tiles_of(n)[i]
                if True:
                    nch = Pn // 64
                    chunks = []
                    for ch in range(nch):
                        col = c0 + t0 + ch * 64
                        chunks.append([s for (a, b_, s) in pas["segs"] if a <= col < b_][0])
                    if main:
                        b = bank()
                        for hd in range(H):
                            mm(psb[b][0:Pn, hd * 128: hd * 128 + Pn], kinv[:, hd, t0:t0 + Pn], qdec[:, hd, t0:t0 + Pn],
                               True, True, ["kinv", "qdec"], [("ps", b)])
                        tt(sT[0:Pn, :, 0:Pn], psb[b][0:Pn, :].rearrange("p (h t) -> p h t", h=H)[:, :, 0:Pn],
                           MASK[0:Pn, 0:Pn].unsqueeze(1).to_broadcast([Pn, H, Pn]), ALU.mult,
                           [("ps", b), "cstb"], ["sT"])
                        yield
                    if main:
                        tp_ = otile_ctr[0] % 2
                        otile_ctr[0] += 1
                        ob = (2 * tp_, 2 * tp_ + 1)
                    kvb = []

                    def kv_mm(ch):
                        r0 = ch * 64
                        bb = (bank(), bank())
                        for hd in range(H):
                            bk = bb[hd // 2]
                            mm(psb[bk][:, (hd % 2) * 256:(hd % 2) * 256 + 256],
                               kend[r0:r0 + 64, i, hd * 128:(hd + 1) * 128], vtok[r0:r0 + 64, i, hd * 256:(hd + 1) * 256],
                               True, True, [("kend", p_, i), ("vtok", p_, i)], [("ps", bk)])
                        kvb.append(bb)

                    def state_of(ch):
                        if chunks[ch] == 0:
                            return S_m, "m"
                        return S_s, "s"

                    def update(ch):
                        Sst, Skey = state_of(ch)
                        lastc = t0 + ch * 64 + 63
                        for hd in range(H):
                            bk = kvb[ch][hd // 2]
                            stt(Sst[:, hd, :], Sst[:, hd, :], eb[:, hd, lastc:lastc + 1],
                                psb[bk][:, (hd % 2) * 256:(hd % 2) * 256 + 256], ALU.mult, ALU.add,
                                [("S", Skey, hd), ("eb", p_), ("ps", bk)], [("S", Skey, hd)])
                        if chunks[ch] == 0:
                            chunk_ctr[0] += 1
                            npar = chunk_ctr[0] % 2
                            if main:
                                act(S_bf[npar][:], S_m[:], AF.Copy, [("S", "m", h_) for h_ in range(H)], [("Sbf", npar)])

                    def outputs(ch, sbf, sbk):
                        r0 = ch * 64
                        for idx in range(8):
                            hd, ec = idx // 2, idx % 2
                            bk = ob[idx // 4]
                            oc = (idx % 4) * 128 + ch * 64
                            mm(psb[bk][:, oc:oc + 64],
                               vtok[r0:r0 + 64, i, hd * 256 + ec * 128: hd * 256 + ec * 128 + 128],
                               sT[r0:r0 + 64, hd, r0:r0 + 64], True, False, [("vtok", p_, i), "sT"], [("ps", bk)])
                            mm(psb[bk][:, oc:oc + 64], sbf[:, hd, ec * 128:(ec + 1) * 128],
                               qdec[:, hd, t0 + ch * 64: t0 + ch * 64 + 64], False, True,
                               [sbk, "qdec"], [("ps", bk)])

                    srcs = []
                    cc_ = chunk_ctr[0]
                    for ch in range(nch):
                        if chunks[ch] == 0:
                            srcs.append((S_bf[cc_ % 2], ("Sbf", cc_ % 2)))
                            cc_ += 1
                        else:
                            srcs.append((S_bfs, ("Sbf", "s")))
                    kv_mm(0)
                    yield
                    if nch == 1:
                        if main:
                            outputs(0, *srcs[0])
                            yield
                        update(0)
                        yield
                    else:
                        update(0)
                        kv_mm(1)
                        yield
                        if main:
                            outputs(0, *srcs[0])
                            yield
                        update(1)
                        yield
                        if main:
                            outputs(1, *srcs[1])
                            yield
                    if main:
                        obs[i] = ob

            def T_b(si, i, obs):
                c0, c1 = sts[si]
                n = c1 - c0
                t0, Pn = tiles_of(n)[i]
                if True:
                    if main:
                        ob = obs[i]
                        o3 = [psb[ob[k]][:, :].rearrange("p (a t) -> p a t", a=4)[:, :, 0:Pn] for k in range(2)]
                        for k in range(2):
                            act(osq[:, k * 4:(k + 1) * 4, 0:Pn], o3[k], AF.Square, [("ps", ob[k])], ["osq"])
                        yield
                        yield
                        b = bank()
                        for hd in range(H):
                            for ec in range(2):
                                mm(psb[b][:, hd * 128: hd * 128 + Pn], ones[:], osq[:, hd * 2 + ec, 0:Pn], ec == 0, ec == 1,
                                   ["ones", "osq"], [("ps", b)])
                        yield
                        pv = psb[b][:, :].rearrange("p (h t) -> p h t", h=H)[:, :, 0:Pn]
                        act(rsh[:, :, 0:Pn], pv, AF.Ln, [("ps", b), "kst"], ["rsh"], scale=1.0 / DV, bias=EPSC)
                        act(rsh[:, :, 0:Pn], rsh[:, :, 0:Pn], AF.Exp, ["rsh"], ["rsh"], scale=-0.5)
                        for k in range(2):
                            tt(otmp[:, k * 4:(k + 1) * 4, 0:Pn].rearrange("p (h e) t -> p h e t", e=2),
                               o3[k].rearrange("p (h e) t -> p h e t", e=2),
                               rsh[:, k * 2:(k + 1) * 2, 0:Pn].unsqueeze(2).to_broadcast([128, 2, 2, Pn]), ALU.mult,
                               [("ps", ob[k]), "rsh"], ["otmp"])
                        yield
                        for ec in range(2):
                            ov = otmp[:, :, 0:Pn].rearrange("p (h e) t -> p h e t", e=2)[:, :, ec, :]
                            rv = rsil[:, :, t0:t0 + Pn].rearrange("p (h e) t -> p h e t", e=2)[:, :, ec, :]
                            stt(rv, ov, vecs[:, V_GN + ec:V_GN + ec + 1], rv, ALU.mult, ALU.mult,
                                ["otmp", "vecs", "rsil"], ["rsil"])
                        yield

            def S5a(si):
                c0, c1 = sts[si]
                n = c1 - c0
                for c in range(KC):
                    b = bank()
                    for kc in range(KC):
                        mm(psb[b][:, 0:n], wO[:, kc, c * 128:(c + 1) * 128], rsil[:, kc, 0:n], kc == 0, kc == KC - 1,
                           [("ring", sO), "rsil"], [("ps", b)])
                    act(yst[:, c, 0:n], psb[b][:, 0:n], AF.Copy, [("ps", b)], ["g_y"])
                    yield

            def S5b(si):
                c0, c1 = sts[si]
                yield from postnorm_g(pas, c0, c1, l, 0, yst, "g_y", 0)

            nst = len(sts)
            if main:
                run(S1(0))
                run(S2(0))
                run(S3(0))
                for si in range(nst):
                    nxt = si + 1 < nst
                    merge(S4(si), S1(si + 1) if nxt else None)
                    run(S5a(si))
                    merge(S5b(si), chain(S2(si + 1), S3(si + 1)) if nxt else None)
            else:
                run(S1(0))
                if nst > 1:
                    run(S1(1))
                run(S2(0))
                for si in range(nst):
                    merge(S4(si), chain(S1(si + 2) if si + 2 < nst else None, S2(si + 1) if si + 1 < nst else None))
                    for _ in range(2):
                        if extras:
                            l_, m_ = extras.pop(0)
                            mod_slab(l_, m_)
                while extras:
                    l_, m_ = extras.pop(0)
                    mod_slab(l_, m_)
            P.barrier()
            bank_pool[0] = list(range(8))
            arena["off"] = mark0

        def ffn_phase(pas, l):
            T = pas["T"]
            mark0 = arena["off"]
            if True:
                def pb(name, shape, dt):
                    return sb(f"{name}_{pas['name']}{l}", shape, dt)
                hT = pb("f_hT", [128, KC, TMAX], BF16)
                yf = pb("f_y", [128, KC, TMAX], F32)
                hid0 = pb("f_hid0", [128, 4, TMAX], BF16)
                off1 = arena["off"]
                hid1 = pb("f_hid1", [128, 4, TMAX], BF16)
                hid = [hid0, hid1]
                sqX = nc.alloc_sbuf_tensor_at(f"f_sqX_{pas['name']}{l}", [128, KC, SUB], BF16, offset=off1)
                rsX1 = nc.alloc_sbuf_tensor_at(f"f_rsX1_{pas['name']}{l}", [128, SUB], F32, offset=off1 + KC * SUB * 2)
                rsX2 = nc.alloc_sbuf_tensor_at(f"f_rsX2_{pas['name']}{l}", [128, SUB], F32, offset=off1 + KC * SUB * 2 + SUB * 4)
                ALIAS = ["f_sqX", "f_rsX1", "f_rsX2"]

                def pre_alt(si, sqt, sqk, rst, rsk):
                    c0, c1 = pas["sts"][si]
                    n = c1 - c0
                    xk = ("xT", pas["stidx"][c0])
                    scratch = yf[:, :, c0:c1]
                    skey = ("f_y", si)
                    act(sqt[:, 0:4, 0:n], xT[:, 0:4, c0:c1], AF.Square, [xk], sqk)
                    tt(sqt[:, 4:8, 0:n], xT[:, 4:8, c0:c1], xT[:, 4:8, c0:c1], ALU.mult, [xk], sqk)
                    yield
                    b = bank()
                    for kc in range(KC):
                        mm(psb[b][:, 0:n], ones[:], sqt[:, kc, 0:n], kc == 0, kc == KC - 1, ["ones"] + sqk, [("ps", b)])
                    act(rst[:, 0:n], psb[b][:, 0:n], AF.Ln, [("ps", b), "kst"], [rsk], scale=1.0 / D, bias=EPSC)
                    act(rst[:, 0:n], rst[:, 0:n], AF.Exp, [rsk], [rsk], scale=-0.5)
                    yield
                    tt(scratch[:, :, 0:n], xT[:, :, c0:c1], rst[:, 0:n].unsqueeze(1).to_broadcast([128, KC, n]), ALU.mult,
                       [xk, rsk], [skey])
                    yield
                    for (a, b_, s_) in segs_of(pas, c0, c1):
                        for kc in range(KC):
                            o_ = hT[:, kc, a: b_]
                            i_ = scratch[:, kc, a - c0:b_ - c0]
                            sc_ = Aall[:, l, 1, kc, s_:s_ + 1]
                            bi_ = modv[:, l, 3, kc, s_:s_ + 1]
                            if kc % 2 == 0:
                                act(o_, i_, AF.Identity, [skey, ("A", l, 1), ("mod", l, 3)], [("f_hT", si)], scale=sc_, bias=bi_)
                            else:
                                dve(lambda e, o_=o_, i_=i_, sc_=sc_, bi_=bi_: e.tensor_scalar(
                                    out=o_, in0=i_, scalar1=sc_, scalar2=bi_, op0=ALU.mult, op1=ALU.add),
                                    [skey, ("A", l, 1), ("mod", l, 3)], [("f_hT", si)])
                            if kc % 2 == 1:
                                yield

                SQW = ["sq", ("sq", 0), ("sq", 1)]
                bufsets = [(sq, SQW, rs, "rs"), (sqX, ["f_sqX"], rsX1, "f_rsX1"), (sq, SQW, rsX2, "f_rsX2")]
                merge(*[pre_alt(si, *bufsets[si % 3]) for si in range(len(pas["sts"]))])

                def load_g(g):
                    return load_slab([
                        (lambda r: r[:, 0:4096].rearrange("p (k n) -> p k n", k=8), kcp(d_wup[l])[:, :, g * 512:(g + 1) * 512]),
                        (lambda r: r[:, 4096:8192].rearrange("p (k n) -> p k n", k=4), kcp(d_wdn[l])[:, g * 4:(g + 1) * 4, :]),
                    ])

                def up(g, s):
                    U = ring[s][:, 0:4096].rearrange("p (k n) -> p k n", k=8)
                    hb = hid[g % 2]
                    for si, (c0, c1) in enumerate(pas["sts"]):
                        n = c1 - c0
                        for j in range(4):
                            b = bank()
                            for kc in range(KC):
                                mm(psb[b][:, 0:n], U[:, kc, j * 128:(j + 1) * 128], hT[:, kc, c0:c1], kc == 0, kc == KC - 1,
                                   [("ring", s), ("f_hT", si)], [("ps", b)])
                            tmp = tA if j % 2 == 0 else tB
                            tk = "tA" if j % 2 == 0 else "tB"
                            act(tmp[:, 0:n], psb[b][:, 0:n], AF.Relu, [("ps", b)], [tk])
                            tt(hb[:, j, c0:c1], tmp[:, 0:n], tmp[:, 0:n], ALU.mult, [tk],
                               [("hid", g % 2, si)] + (ALIAS if g == 1 else []))

                def down(g, s):
                    Dn = ring[s][:, 4096:8192].rearrange("p (k n) -> p k n", k=4)
                    hb = hid[g % 2]
                    for si, (c0, c1) in enumerate(pas["sts"]):
                        n = c1 - c0
                        for c in range(KC):
                            b = bank()
                            for j in range(4):
                                mm(psb[b][:, 0:n], Dn[:, j, c * 128:(c + 1) * 128], hb[:, j, c0:c1], j == 0, j == 3,
                                   [("ring", s), ("hid", g % 2, si)], [("ps", b)])
                            if g == 0:
                                act(yf[:, c, c0:c1], psb[b][:, 0:n], AF.Copy, [("ps", b)], [("f_y", si)])
                            else:
                                tt(yf[:, c, c0:c1], yf[:, c, c0:c1], psb[b][:, 0:n], ALU.add, [("f_y", si), ("ps", b)],
                                   [("f_y", si)])

                slots = {}
                slots[0] = load_g(0)
                up(0, slots[0])
                for g in range(8):
                    if g + 1 < 8:
                        slots[g + 1] = load_g(g + 1)
                        up(g + 1, slots[g + 1])
                    down(g, slots[g])
                for si, (c0, c1) in enumerate(pas["sts"]):
                    postnorm(pas, c0, c1, l, 1, yf, ("f_y", si), c0)
                P.barrier()
            arena["off"] = mark0

        def conv_phase(pas):
            l = 1
            T = pas["T"]
            isB = pas["name"] == "B"
            mark0 = arena["off"]
            if True:
                def pb(name, shape, dt):
                    return sb(f"{name}_{pas['name']}", shape, dt)
                UW = 30 + TMAX + 30
                uT = pb("c_uT", [128, KC, UW], BF16)
                yst = pb("c_y", [128, KC, SUB], F32)
                diags = [pb(f"c_diag{i_}", [128, 31, 128], BF16) for i_ in range(2)]
                mu = pb("c_mu", [128, SUB], F32)
                accs = [pb(f"c_acc{i_}", [128, SUB], F32) for i_ in range(2)]
                actr = [0]

                def ucol(c):
                    return 30 + c if (not isB or c < 960) else 60 + c

                if not isB:
                    P.op("dve", lambda e: e.memset(uT[:, :, 0:30], 0.0), (), ["uT"])
                else:
                    dve(lambda e: e.tensor_copy(out=uT[:, :, 0:30], in_=uctx[:]), ["uctx"], ["uT"])
                    dve(lambda e: e.tensor_copy(out=uT[:, :, 990:1020], in_=cctx[:]), ["cctx"], ["uT"])

                mark1 = arena["off"]
                sts = pas["sts"]
                nst = len(sts)
                hT = sb(f"c_hT_{pas['name']}", [128, KC, TMAX], BF16)
                slabs = []
                for hf in range(2):
                    slabs.append(load_slab([
                        (lambda r: r[:, 0:8192].rearrange("p (k n) -> p k n", k=8)[:, :, 0:512],
                         kcp(d_cwin)[:, :, hf * 512:(hf + 1) * 512]),
                        (lambda r: r[:, 0:8192].rearrange("p (k n) -> p k n", k=8)[:, :, 512:1024],
                         kcp(d_cwin)[:, :, 1024 + hf * 512: 1024 + (hf + 1) * 512]),
                    ]))

                def C1(si):
                    c0, c1 = sts[si]
                    yield from prenorm_g(pas, c0, c1, l, 0, yst, "c_y", hT, ("c_hT", si), c0)

                def C2(si):
                    c0, c1 = sts[si]
                    n = c1 - c0
                    for hf in range(2):
                        s = slabs[hf]
                        W = slot_kn(s, 8, 1024)
                        for cc in range(4):
                            fc = hf * 4 + cc
                            ba = bank()
                            for kc in range(KC):
                                mm(psb[ba][:, 0:n], W[:, kc, cc * 128:(cc + 1) * 128], hT[:, kc, c0:c1], kc == 0, kc == KC - 1,
                                   [("ring", s), ("c_hT", si)], [("ps", ba)])
                            bg = bank()
                            for kc in range(KC):
                                mm(psb[bg][:, 0:n], W[:, kc, 512 + cc * 128: 512 + (cc + 1) * 128], hT[:, kc, c0:c1],
                                   kc == 0, kc == KC - 1, [("ring", s), ("c_hT", si)], [("ps", bg)])
                            tg = tA if fc % 2 == 0 else tB
                            tgk = "tA" if fc % 2 == 0 else "tB"
                            act(tg[:, 0:n], psb[bg][:, 0:n], AF.Sigmoid, [("ps", bg), "vecs"], [tgk],
                                bias=vecs[:, V_CBIN + 8 + fc: V_CBIN + 9 + fc])
                            bav = vecs[:, V_CBIN + fc: V_CBIN + fc + 1]
                            for (a, b_, sq_) in segs_of(pas, c0, c1):
                                stt(uT[:, fc, ucol(a): ucol(a) + (b_ - a)], psb[ba][:, a - c0:b_ - c0], bav,
                                    tg[:, a - c0:b_ - c0], ALU.add, ALU.mult, [("ps", ba), "vecs", tgk], ["uT"])
                            if isB:
                                if c0 <= 930 and c1 >= 960:
                                    stt(ulast_m[:, fc, :], psb[ba][:, 930 - c0:960 - c0], bav, tg[:, 930 - c0:960 - c0],
                                        ALU.add, ALU.mult, [("ps", ba), "vecs", tgk], ["ulast_m"])
                                if c0 <= 994 and c1 >= 1024:
                                    stt(ulast_s[:, fc, :], psb[ba][:, 994 - c0:1024 - c0], bav, tg[:, 994 - c0:1024 - c0],
                                        ALU.add, ALU.mult, [("ps", ba), "vecs", tgk], ["ulast_s"])
                            yield

                run(C1(0))
                for si in range(nst):
                    merge(C2(si), C1(si + 1) if si + 1 < nst else None)
                P.barrier()
                arena["off"] = mark1
                if not isB:
                    dve(lambda e: e.tensor_copy(out=uctx[:], in_=uT[:, :, 30 + T - 30: 30 + T]), ["uT"], ["uctx"])
                zT = sb(f"c_zT_{pas['name']}", [128, KC, TMAX], BF16)
                csegs_st = [[] for _ in sts]
                for (a, b_, sq_) in pas["segs"]:
                    c = a
                    while c < b_:
                        si = pas["stidx"][c]
                        nn = min(SUB, min(b_, sts[si][1]) - c)
                        csegs_st[si].append((c, nn, ucol(c) - 30))
                        c += nn
                sO = load_slab([(lambda r: r[:, 0:8192].rearrange("p (k n) -> p k n", k=8), kcp(d_cwo))])
                wO = slot_kn(sO, 8, 1024)
                dctr = [0]

                def C3(si):
                    for fc in range(KC):
                        wd = vecs[:, V_WDW + fc * 31: V_WDW + fc * 31 + 31]
                        dp = dctr[0] % 2
                        dctr[0] += 1
                        diag = diags[dp]
                        dgk = ("diag", dp)
                        tt(diag[:], IDb.unsqueeze(1).to_broadcast([128, 31, 128]), wd.unsqueeze(2).to_broadcast([128, 31, 128]),
                           ALU.mult, ["cstb", "vecs"], [dgk])
                        yield
                        for (c, nn, us) in csegs_st[si]:
                            if NPOOL > 0:
                                ap_ = actr[0] % 2
                                actr[0] += 1
                                acc = accs[ap_]
                                ak = ("acc", ap_)
                                P.op("pool", lambda e, acc=acc, fc=fc, us=us, nn=nn, wd=wd: e.tensor_scalar_mul(
                                    out=acc[:, 0:nn], in0=uT[:, fc, us: us + nn], scalar1=wd[:, 0:1]), ["uT", "vecs"], [ak])
                                for j in range(1, NPOOL):
                                    P.op("pool", lambda e, acc=acc, fc=fc, us=us, nn=nn, wd=wd, j=j: e.scalar_tensor_tensor(
                                        out=acc[:, 0:nn], in0=uT[:, fc, us + j: us + j + nn], scalar=wd[:, j:j + 1],
                                        in1=acc[:, 0:nn], op0=ALU.mult, op1=ALU.add), ["uT", "vecs", ak], [ak])
                            b = bank()
                            for j in range(NPOOL, 31):
                                mm(psb[b][:, 0:nn], diag[:, j, :], uT[:, fc, us + j: us + j + nn], j == NPOOL, j == 30,
                                   [dgk, "uT"], [("ps", b)])
                            if NPOOL > 0:
                                stt(zT[:, fc, c:c + nn], psb[b][:, 0:nn], vecs[:, V_CBDW + fc: V_CBDW + fc + 1], acc[:, 0:nn],
                                    ALU.add, ALU.add, [("ps", b), "vecs", ak], [("zT", si)])
                            else:
                                act(zT[:, fc, c:c + nn], psb[b][:, 0:nn], AF.Identity, [("ps", b), "vecs"], [("zT", si)],
                                    bias=vecs[:, V_CBDW + fc: V_CBDW + fc + 1])
                            yield

                def C4(si):
                    c0, c1 = sts[si]
                    n = c1 - c0
                    zk = ("zT", si)
                    act(sq[:, :, 0:n], zT[:, :, c0:c1], AF.Square, [zk], ["sq", ("sq", 0), ("sq", 1)])
                    yield
                    yield
                    bm = bank()
                    for kc in range(KC):
                        mm(psb[bm][:, 0:n], ones[:], zT[:, kc, c0:c1], kc == 0, kc == KC - 1, ["ones", zk], [("ps", bm)])
                    yield
                    b2 = bank()
                    for kc in range(KC):
                        mm(psb[b2][:, 0:n], ones[:], sq[:, kc, 0:n], kc == 0, kc == KC - 1, ["ones", "sq", ("sq", kc // 4)], [("ps", b2)])
                    dve(lambda e, n=n, bm=bm: e.tensor_scalar_mul(out=mu[:, 0:n], in0=psb[bm][:, 0:n], scalar1=1.0 / D),
                        [("ps", bm)], ["mu"])
                    tt(rs[:, 0:n], mu[:, 0:n], mu[:, 0:n], ALU.mult, ["mu"], ["rs"])
                    yield
                    stt(rs[:, 0:n], psb[b2][:, 0:n], 1.0 / D, rs[:, 0:n], ALU.mult, ALU.subtract, [("ps", b2), "rs"], ["rs"])
                    act(rs[:, 0:n], rs[:, 0:n], AF.Ln, ["rs", "kst"], ["rs"], bias=EPSC)
                    act(rs[:, 0:n], rs[:, 0:n], AF.Exp, ["rs"], ["rs"], scale=-0.5)
                    yield
                    tt(yst[:, :, 0:n], zT[:, :, c0:c1], mu[:, 0:n].unsqueeze(1).to_broadcast([128, KC, n]), ALU.subtract,
                       [zk, "mu"], ["c_y"])
                    yield
                    tt(yst[:, :, 0:n], yst[:, :, 0:n], rs[:, 0:n].unsqueeze(1).to_broadcast([128, KC, n]), ALU.mult,
                       ["c_y", "rs"], ["c_y"])
                    yield
                    for fc in range(KC):
                        act(zT[:, fc, c0:c1], yst[:, fc, 0:n], AF.Silu, ["c_y", "vecs"], [zk],
                            scale=vecs[:, V_LNG + fc: V_LNG + fc + 1], bias=vecs[:, V_LNB + fc: V_LNB + fc + 1])
                        if fc % 2 == 1:
                            yield

                def C5a(si):
                    c0, c1 = sts[si]
                    n = c1 - c0
                    for c in range(KC):
                        b = bank()
                        for kc in range(KC):
                            mm(psb[b][:, 0:n], wO[:, kc, c * 128:(c + 1) * 128], zT[:, kc, c0:c1], kc == 0, kc == KC - 1,
                               [("ring", sO), ("zT", si)], [("ps", b)])
                        act(yst[:, c, 0:n], psb[b][:, 0:n], AF.Identity, [("ps", b), "vecs"], ["c_y"],
                            bias=vecs[:, V_CBOUT + c: V_CBOUT + c + 1])
                        yield

                def C5b(si):
                    c0, c1 = sts[si]
                    yield from postnorm_g(pas, c0, c1, l, 0, yst, "c_y", 0)

                run(C3(0))
                for si in range(nst):
                    merge(C3(si + 1) if si + 1 < nst else None, chain(C4(si), idle(3), C5a(si), C5b(si)))
                P.barrier()
            arena["off"] = mark0

        def chk(tag):
            if _DEBUG_STOP == tag:
                raise _Stop()

        def whole():
            chk("init")
            mod_slab(0, 0)
            mod_slab(0, 1)
            gpre = vecs[:, vnorm(0, 0): vnorm(0, 0) + 8]
            for s in range(2):
                stt(Aall[:, 0, 0, :, s], modv[:, 0, 1, :, s], 1.0, gpre, ALU.add, ALU.mult, [("mod", 0, 1), "vecs"], [("A", 0, 0)])
            chk("mod0")
            later = [(0, 2), (0, 3), (0, 4), (0, 5), (1, 0), (1, 1), (1, 2), (1, 3), (1, 4), (1, 5)]
            for pi, pas in enumerate(passes):
                cur["pas"] = pas
                P.tag = pas["name"] + ".gla"
                load_x(pas)
                chk(pas["name"] + "load")
                if pas["kind"] == "prefix":
                    gla_phase(pas, later[0:6] if pi == 0 else later[6:10])
                else:
                    gla_phase(pas)
                chk(pas["name"] + "gla")
                if pas["kind"] == "prefix":
                    P.tag = pas["name"] + ".mod"
                    if pas["name"] == "P2":
                        dve(lambda e: e.tensor_scalar_mul(out=S_m[:], in0=S_m[:], scalar1=flag[:, 0:1]), [("S", "m", h_) for h_ in range(H)] + ["flag"], [("S", "m", h_) for h_ in range(H)])
                        par = chunk_ctr[0] % 2
                        act(S_bf[par][:], S_m[:], AF.Copy, [("S", "m", h_) for h_ in range(H)], [("Sbf", par)])
                        for s in range(2):
                            tt(Gall[:, 0, 0, :, s], modv[:, 0, 2, :, s], vecs[:, vnorm(1, 0): vnorm(1, 0) + 8], ALU.mult,
                               [("mod", 0, 2), "vecs"], [("G", 0, 0)])
                        mod_derive(0, 1)
                        mod_derive(1, 0)
                        mod_derive(1, 1)
                    chk(pas["name"])
                    continue
                P.tag = pas["name"] + ".ffn0"
                ffn_phase(pas, 0)
                chk(pas["name"] + "ffn0")
                P.tag = pas["name"] + ".conv"
                conv_phase(pas)
                chk(pas["name"] + "conv")
                P.tag = pas["name"] + ".ffn1"
                ffn_phase(pas, 1)
                chk(pas["name"] + "ffn1")
                for i, (c0, c1) in enumerate(pas["sts"]):
                    t = dma("sp", o_yT[:, :, pas["col0"] + c0: pas["col0"] + c1], xT[:, :, c0:c1], f"oy{i}", [("xT", i)], ())
                    P.out_tokens.append(t)
                chk(pas["name"])

        cur = {}
        try:
            whole()
        except _Stop:
            pas = cur.get("pas")
            if pas is not None and pas["kind"] == "main":
                for i, (c0, c1) in enumerate(pas["sts"]):
                    t = dma("sp", o_yT[:, :, pas["col0"] + c0: pas["col0"] + c1], xT[:, :, c0:c1], f"oy{i}", [("xT", i)], ())
                    P.out_tokens.append(t)
        P.out_tokens.append(dma("sp", o_sgm.rearrange("h d e -> d h e"), S_m[:], "o_sgm", [("S", "m", h_) for h_ in range(H)], ()))
        P.out_tokens.append(dma("sp", o_sgs.rearrange("h d e -> d h e"), S_s[:], "o_sgs", [("S", "s", h_) for h_ in range(H)], ()))
        P.out_tokens.append(dma("sp", o_cvm, ulast_m[:], "o_cvm", ["ulast_m"], ()))
        P.out_tokens.append(dma("sp", o_cvs, ulast_s[:], "o_cvs", ["ulast_s"], ()))
        fin = Op(None)
        seen = {}
        for t in P.out_tokens:
            seen[t[1]] = max(seen.get(t[1], 0), t[2])
        for k, v in seen.items():
            fin.waits.append(("d", k, v))
        P.ops["sp"].append(fin)

        global _LAST_P
        _LAST_P = P
        with nc.Block() as block:
            P.emit(nc, block, sem, dsems)
    return nc


_NC_CACHE = {}


def _fm(a2d):
    t = a2d.shape[0]
    return np.ascontiguousarray(a2d.reshape(t, KC, 128).transpose(2, 1, 0))


def _vec_cols(v):
    return v.reshape(-1, 128).T


def _consts():
    s = np.arange(128)[:, None]
    t = np.arange(128)[None, :]
    same = (s // 64) == (t // 64)
    TLm = np.where(same & (s <= t), -1.0 / 16.0, 0.0)
    TUm = np.where(same & (s > t), -1.0 / 16.0, 0.0)
    MK = np.where(same & (s <= t), 1.0, 0.0)
    ID = np.eye(128)
    return np.ascontiguousarray(np.concatenate([TLm, TUm, MK, ID], axis=1).astype(np.float32))


def kernel(x_prompt, x_sample, c_prompt, c_sample, state_gla, cache_conv, w_mod, b_mod,
           norm_mix_pre, norm_mix_post, norm_ffn_pre, norm_ffn_post, w_ffn_up, w_ffn_down,
           gla_w_in, gla_w_gate_a, gla_w_gate_b, gla_b_gate, gla_norm, gla_w_out,
           conv_w_in, conv_b_in, conv_w_dw, conv_b_dw, conv_ln_g, conv_ln_b, conv_w_out, conv_b_out):
    f = lambda a: np.ascontiguousarray(np.asarray(a, dtype=np.float32))
    x_prompt, x_sample, c_prompt, c_sample = f(x_prompt), f(x_sample), f(c_prompt), f(c_sample)
    state_gla, cache_conv = f(state_gla), f(cache_conv)
    if "nc" not in _NC_CACHE:
        _NC_CACHE["nc"] = build_nc()
    nc = _NC_CACHE["nc"]

    cols = []
    for arr in (norm_mix_pre, norm_mix_post, norm_ffn_pre, norm_ffn_post):
        for l in range(2):
            cols.append(_vec_cols(f(arr)[l]))
    for l in range(2):
        cols.append(_vec_cols(f(b_mod)[l]))
    cols.append(_vec_cols(f(gla_norm)[0]))
    cols.append(_vec_cols(f(conv_b_in)[0]))
    cols.append(_vec_cols(f(conv_b_dw)[0]))
    cols.append(_vec_cols(f(conv_ln_g)[0]))
    cols.append(_vec_cols(f(conv_ln_b)[0]))
    cols.append(_vec_cols(f(conv_b_out)[0]))
    wdw = f(conv_w_dw)[0]
    cols.append(np.ascontiguousarray(wdw.reshape(31, KC, 128).transpose(2, 1, 0)).reshape(128, KC * 31))
    vecs = np.ascontiguousarray(np.concatenate(cols, axis=1).astype(np.float32))
    assert vecs.shape == (128, NV), vecs.shape
    consts = _consts()
    wgb_aug = np.ascontiguousarray(np.concatenate([f(gla_w_gate_b)[0], f(gla_b_gate)[0][None, :]], axis=0))

    shared = {
        "vecs": vecs, "consts": consts,
        "w_mod": f(w_mod), "w_ffn_up": f(w_ffn_up), "w_ffn_down": f(w_ffn_down),
        "gla_w_in": f(gla_w_in)[0], "gla_w_ga": f(gla_w_gate_a)[0], "gla_w_gb": wgb_aug,
        "gla_w_out": f(gla_w_out)[0], "conv_w_in": f(conv_w_in)[0], "conv_w_out": f(conv_w_out)[0],
    }
    in_maps = []
    for c in range(8):
        b = c // 2
        odd = c % 2
        t0 = 0 if not odd else 4096 - NMAIN
        xm = x_prompt[b, t0:t0 + NMAIN]
        xall = np.concatenate([xm, x_sample[c]], axis=0)
        xp = x_prompt[b, 0:NPRE]
        c2 = np.stack([c_prompt[b], c_sample[c]], axis=0)
        m = dict(shared)
        m["xT"] = _fm(xall)
        m["xpT"] = _fm(xp)
        m["cT"] = np.ascontiguousarray(c2.reshape(2, KC, 128).transpose(2, 1, 0))
        m["sgla"] = np.ascontiguousarray(state_gla[0, c])
        m["cconvT"] = _fm(cache_conv[0, c])
        m["flag"] = np.full((128, 1), float(odd), dtype=np.float32)
        in_maps.append(m)

    res = run_bass_kernel_spmd(nc, in_maps, core_ids=list(range(8)))
    R = res.results

    def tm(a):
        return np.ascontiguousarray(a.transpose(2, 1, 0).reshape(a.shape[2], D))

    y_prompt = np.empty((4, 4096, D), np.float32)
    y_sample = np.empty((8, 64, D), np.float32)
    gla_p = np.empty((1, 4, H, DK, DV), np.float32)
    conv_p = np.empty((1, 4, 30, D), np.float32)
    gla_s = np.empty((1, 8, H, DK, DV), np.float32)
    conv_s = np.empty((1, 8, 30, D), np.float32)
    for c in range(8):
        b = c // 2
        odd = c % 2
        y = tm(np.asarray(R[c]["yT"]))
        if not odd:
            y_prompt[b, 0:2048] = y[0:2048]
        else:
            y_prompt[b, 2048:4096] = y[NMAIN - 2048:NMAIN]
            gla_p[0, b] = np.asarray(R[c]["sg_main"])
            conv_p[0, b] = tm(np.asarray(R[c]["cv_main"]))
        y_sample[c] = y[NMAIN:NMAIN + 64]
        gla_s[0, c] = np.asarray(R[c]["sg_smp"])
        conv_s[0, c] = tm(np.asarray(R[c]["cv_smp"]))
    return (y_prompt, y_sample, gla_p, conv_p, gla_s, conv_s)
```
